# Optimizing a Trainium2 kernel written in Bass

```python
import jax, jax.numpy as jnp
from jax import lax
import numpy as np

D_MODEL = 1024
BATCH = 16
SEQ = 4096
DEPTH = 4
DEC_BATCH = 16
DEC_SEQ = 16
PAST_LEN = 1024

CHUNK = 64
N_MIXERS = 3
N_GLA_LAYERS = (DEPTH + 2) // 3
N_RET_LAYERS = (DEPTH + 1) // 3
N_SWA_LAYERS = DEPTH // 3

GLA_HEADS = 4
GLA_KEY_DIM = D_MODEL // 2
GLA_DK = GLA_KEY_DIM // GLA_HEADS
GLA_DV = D_MODEL // GLA_HEADS
GLA_GATE_RANK = 16
GLA_TAU = 16.0

RET_HEADS = D_MODEL // 256
RET_DK = D_MODEL // RET_HEADS
RET_VALUE_DIM = 2 * D_MODEL
RET_DV = RET_VALUE_DIM // RET_HEADS

SWA_Q_HEADS = 16
SWA_KV_HEADS = 4
SWA_GROUP = SWA_Q_HEADS // SWA_KV_HEADS
SWA_HEAD_DIM = D_MODEL // SWA_Q_HEADS
WINDOW = 128
SWA_CHUNKS = WINDOW // CHUNK
SWA_CACHE_LEN = min(WINDOW, PAST_LEN)

MLP_HIDDEN = 4 * D_MODEL
ADA_CHUNKS = 6
NORM_EPS = 1e-6
GN_EPS = 1e-5
NEG_INF = -1e30

kernel_name = 'chunk_streaming_hybrid_gla_retnet_swa'


def rms_norm(x, g):
    xf = x.astype(jnp.float32)
    y = xf * lax.rsqrt(jnp.mean(jnp.square(xf), axis=-1, keepdims=True) + NORM_EPS)
    return (y * g.astype(jnp.float32)).astype(x.dtype)


def head_group_norm(x):
    xf = x.astype(jnp.float32)
    mu = jnp.mean(xf, axis=-1, keepdims=True)
    var = jnp.mean(jnp.square(xf - mu), axis=-1, keepdims=True)
    return ((xf - mu) * lax.rsqrt(var + GN_EPS)).astype(x.dtype)


def ada_modulation(c, w, b):
    mod = jax.nn.silu(c) @ w + b
    return jnp.split(mod[:, None, :], ADA_CHUNKS, axis=-1)


def modulate(x, g, shift, scale):
    return rms_norm(x, g) * (1.0 + scale) + shift


def sq_relu_mlp(h, w1, w2):
    return jnp.square(jax.nn.relu(h @ w1)) @ w2


def to_chunks(x, L):
    B, T, H, d = x.shape
    return x.reshape(B, T // L, L, H, d).transpose(1, 0, 3, 2, 4)


def from_chunks(x):
    NC, B, H, L, d = x.shape
    return x.transpose(1, 0, 3, 2, 4).reshape(B, NC * L, H, d)


def gla_recurrence(q, k, v, logg, S0):
    T = q.shape[1]
    L = min(CHUNK, T)
    xs = tuple(to_chunks(a.astype(jnp.float32), L) for a in (q, k, v, logg))
    causal = jnp.tril(jnp.ones((L, L), dtype=bool))

    def step(S, inp):
        qc, kc, vc, gc = inp
        b = jnp.cumsum(gc, axis=-2)
        bL = b[..., -1:, :]
        qd = qc * jnp.exp(b)
        kd = kc * jnp.exp(-b)
        att = jnp.where(causal, jnp.einsum('bhld,bhsd->bhls', qd, kd), 0.0)
        o = jnp.einsum('bhls,bhsv->bhlv', att, vc) + jnp.einsum('bhld,bhdv->bhlv', qd, S)
        S = S * jnp.exp(jnp.swapaxes(bL, -1, -2)) + jnp.einsum('bhsd,bhsv->bhdv', kc * jnp.exp(bL - b), vc)
        return S, o

    S, o = lax.scan(step, S0.astype(jnp.float32), xs)
    return from_chunks(o).astype(q.dtype), S


def gla_mixer(h, S0, wq, wk, wv, wg1, wg2, bg, wr, norm_g, wo):
    B, T, _ = h.shape
    q = (h @ wq).reshape(B, T, GLA_HEADS, GLA_DK) * GLA_DK ** -0.5
    k = (h @ wk).reshape(B, T, GLA_HEADS, GLA_DK)
    v = (h @ wv).reshape(B, T, GLA_HEADS, GLA_DV)
    logg = jax.nn.log_sigmoid(((h @ wg1) @ wg2 + bg).astype(jnp.float32)).reshape(B, T, GLA_HEADS, GLA_DK) / GLA_TAU
    o, S = gla_recurrence(q, k, v, logg, S0)
    o = rms_norm(o, norm_g).reshape(B, T, D_MODEL) * jax.nn.silu(h @ wr)
    return o @ wo, S


def retention_log_decay():
    return jnp.log(1.0 - 2.0 ** (-5.0 - jnp.arange(RET_HEADS, dtype=jnp.float32)))


def retention_recurrence(q, k, v, S0):
    T = q.shape[1]
    L = min(CHUNK, T)
    lg = retention_log_decay()
    pos = jnp.arange(L, dtype=jnp.float32)
    rel = pos[:, None] - pos[None, :]
    decay_intra = jnp.where(rel >= 0, jnp.exp(lg[:, None, None] * jnp.maximum(rel, 0.0)), 0.0)
    decay_query = jnp.exp(lg[:, None] * (pos + 1.0))[None, :, :, None]
    decay_key = jnp.exp(lg[:, None] * (L - 1.0 - pos))[:, :, None]
    decay_chunk = jnp.exp(lg * L)[:, None, None]
    xs = tuple(to_chunks(a.astype(jnp.float32), L) for a in (q, k, v))

    def step(S, inp):
        qc, kc, vc = inp
        att = jnp.einsum('bhld,bhsd->bhls', qc, kc) * decay_intra
        o = jnp.einsum('bhls,bhsv->bhlv', att, vc) + jnp.einsum('bhld,bhdv->bhlv', qc, S) * decay_query
        S = S * decay_chunk + jnp.einsum('bhsd,bhsv->bhdv', kc * decay_key, vc)
        return S, o

    S, o = lax.scan(step, S0.astype(jnp.float32), xs)
    return from_chunks(o).astype(q.dtype), S


def retention_mixer(h, S0, wq, wk, wv, wg, wo):
    B, T, _ = h.shape
    q = (h @ wq).reshape(B, T, RET_HEADS, RET_DK)
    k = (h @ wk).reshape(B, T, RET_HEADS, RET_DK) * RET_DK ** -0.5
    v = (h @ wv).reshape(B, T, RET_HEADS, RET_DV)
    o, S = retention_recurrence(q, k, v, S0)
    o = head_group_norm(o).reshape(B, T, RET_VALUE_DIM) * jax.nn.silu(h @ wg)
    return o @ wo, S


def alibi_slopes():
    hh = jnp.arange(1, SWA_Q_HEADS + 1, dtype=jnp.float32)
    return (2.0 ** (-8.0 * hh / SWA_Q_HEADS)).reshape(SWA_KV_HEADS, SWA_GROUP)


def swa_project(h, wqkv, qn, kn):
    B, T, _ = h.shape
    nq = SWA_Q_HEADS * SWA_HEAD_DIM
    nk = SWA_KV_HEADS * SWA_HEAD_DIM
    qkv = h @ wqkv
    q = rms_norm(qkv[..., :nq].reshape(B, T, SWA_KV_HEADS, SWA_GROUP, SWA_HEAD_DIM), qn)
    k = rms_norm(qkv[..., nq:nq + nk].reshape(B, T, SWA_KV_HEADS, SWA_HEAD_DIM), kn)
    v = qkv[..., nq + nk:].reshape(B, T, SWA_KV_HEADS, SWA_HEAD_DIM)
    return q, k, v


def sink_attention(q, k, v, q_pos, k_pos, k_valid, sinks):
    logits = jnp.einsum('bnqkgd,bnskd->bnkgqs', q, k, preferred_element_type=jnp.float32) * SWA_HEAD_DIM ** -0.5
    dist = jnp.abs(q_pos[:, :, None] - k_pos[:, None, :]).astype(jnp.float32)
    logits = logits - alibi_slopes()[None, :, :, None, None] * dist[:, None, None]
    logits = jnp.where(k_valid[:, None, None, None, :], logits, NEG_INF)
    sink = jnp.broadcast_to(sinks.astype(jnp.float32).reshape(SWA_KV_HEADS, SWA_GROUP)[:, :, None, None],
                            logits.shape[:-1] + (1,))
    probs = jax.nn.softmax(jnp.concatenate([logits, sink], axis=-1), axis=-1)[..., :-1]
    return jnp.einsum('bnkgqs,bnskd->bnqkgd', probs.astype(v.dtype), v)


def swa_prompt(h, wqkv, qn, kn, sinks, wo):
    B, T, _ = h.shape
    q, k, v = swa_project(h, wqkv, qn, kn)
    L = CHUNK
    NC = T // L
    P = SWA_CHUNKS

    def band(a):
        ap = jnp.pad(a, ((0, 0), (P * L, 0), (0, 0), (0, 0))).reshape(B, NC + P, L, SWA_KV_HEADS, SWA_HEAD_DIM)
        return jnp.concatenate([ap[:, j:j + NC] for j in range(P + 1)], axis=2)

    c = jnp.arange(NC, dtype=jnp.int32)[:, None]
    q_pos = c * L + jnp.arange(L, dtype=jnp.int32)[None, :]
    k_pos = (c - P) * L + jnp.arange((P + 1) * L, dtype=jnp.int32)[None, :]
    k_valid = k_pos >= 0
    qb = q.reshape(B, NC, L, SWA_KV_HEADS, SWA_GROUP, SWA_HEAD_DIM)
    o = sink_attention(qb, band(k), band(v), q_pos, k_pos, k_valid, sinks)
    return o.reshape(B, T, D_MODEL) @ wo, k[:, -WINDOW:], v[:, -WINDOW:]


def swa_sample(h, cache_k, cache_v, wqkv, qn, kn, sinks, wo):
    B, T, _ = h.shape
    q, k, v = swa_project(h, wqkv, qn, kn)
    W = cache_k.shape[1]
    kk = jnp.concatenate([cache_k.astype(k.dtype), k], axis=1)[:, None]
    vv = jnp.concatenate([cache_v.astype(v.dtype), v], axis=1)[:, None]
    q_pos = (PAST_LEN + jnp.arange(T, dtype=jnp.int32))[None, :]
    k_pos = jnp.concatenate([PAST_LEN - W + jnp.arange(W, dtype=jnp.int32),
                             PAST_LEN + jnp.arange(T, dtype=jnp.int32)])[None, :]
    k_valid = jnp.ones((1, W + T), dtype=bool)
    o = sink_attention(q[:, None], kk, vv, q_pos, k_pos, k_valid, sinks)
    return o.reshape(B, T, D_MODEL) @ wo, k, v


def setup_inputs(seed: int = 0) -> dict:
    key = jax.random.key(seed)
    ks = iter(jax.random.split(key, 48))
    D = D_MODEL

    def nrm(shape, scale=1.0):
        return jax.random.normal(next(ks), shape, jnp.float32) * scale

    qkv_width = (SWA_Q_HEADS + 2 * SWA_KV_HEADS) * SWA_HEAD_DIM
    return {
        'x_prompt': nrm((BATCH, SEQ, D)),
        'x_sample': nrm((DEC_BATCH, DEC_SEQ, D)),
        'state_gla_l0': nrm((DEC_BATCH, GLA_HEADS, GLA_DK, GLA_DV)),
        'state_ret_l1': nrm((DEC_BATCH, RET_HEADS, RET_DK, RET_DV)),
        'cache_swa_k_l2': nrm((DEC_BATCH, SWA_CACHE_LEN, SWA_KV_HEADS, SWA_HEAD_DIM)),
        'cache_swa_v_l2': nrm((DEC_BATCH, SWA_CACHE_LEN, SWA_KV_HEADS, SWA_HEAD_DIM)),
        'state_gla_l3': nrm((DEC_BATCH, GLA_HEADS, GLA_DK, GLA_DV)),
        'c_prompt': nrm((BATCH, D)),
        'c_sample': nrm((DEC_BATCH, D)),
        'norm_mix': 1.0 + nrm((DEPTH, D), 0.05),
        'norm_mlp': 1.0 + nrm((DEPTH, D), 0.05),
        'ada_w': nrm((DEPTH, D, ADA_CHUNKS * D), 0.5 * D ** -0.5),
        'ada_b': nrm((DEPTH, ADA_CHUNKS * D), 0.02),
        'mlp_w1': nrm((DEPTH, D, MLP_HIDDEN), D ** -0.5),
        'mlp_w2': nrm((DEPTH, MLP_HIDDEN, D), MLP_HIDDEN ** -0.5),
        'gla_wq': nrm((N_GLA_LAYERS, D, GLA_KEY_DIM), D ** -0.5),
        'gla_wk': nrm((N_GLA_LAYERS, D, GLA_KEY_DIM), D ** -0.5),
        'gla_wv': nrm((N_GLA_LAYERS, D, D), D ** -0.5),
        'gla_wg1': nrm((N_GLA_LAYERS, D, GLA_GATE_RANK), D ** -0.5),
        'gla_wg2': nrm((N_GLA_LAYERS, GLA_GATE_RANK, GLA_KEY_DIM), GLA_GATE_RANK ** -0.5),
        'gla_bg': nrm((N_GLA_LAYERS, GLA_KEY_DIM), 0.1),
        'gla_wr': nrm((N_GLA_LAYERS, D, D), D ** -0.5),
        'gla_norm': 1.0 + nrm((N_GLA_LAYERS, GLA_DV), 0.05),
        'gla_wo': nrm((N_GLA_LAYERS, D, D), D ** -0.5),
        'ret_wq': nrm((N_RET_LAYERS, D, D), D ** -0.5),
        'ret_wk': nrm((N_RET_LAYERS, D, D), D ** -0.5),
        'ret_wv': nrm((N_RET_LAYERS, D, RET_VALUE_DIM), D ** -0.5),
        'ret_wg': nrm((N_RET_LAYERS, D, RET_VALUE_DIM), D ** -0.5),
        'ret_wo': nrm((N_RET_LAYERS, RET_VALUE_DIM, D), RET_VALUE_DIM ** -0.5),
        'swa_wqkv': nrm((N_SWA_LAYERS, D, qkv_width), D ** -0.5),
        'swa_qnorm': 1.0 + nrm((N_SWA_LAYERS, SWA_HEAD_DIM), 0.05),
        'swa_knorm': 1.0 + nrm((N_SWA_LAYERS, SWA_HEAD_DIM), 0.05),
        'swa_sinks': nrm((N_SWA_LAYERS, SWA_Q_HEADS), 1.0),
        'swa_wo': nrm((N_SWA_LAYERS, D, D), D ** -0.5),
    }


def reference(x_prompt, x_sample, state_gla_l0, state_ret_l1, cache_swa_k_l2, cache_swa_v_l2, state_gla_l3,
              c_prompt, c_sample, norm_mix, norm_mlp, ada_w, ada_b, mlp_w1, mlp_w2,
              gla_wq, gla_wk, gla_wv, gla_wg1, gla_wg2, gla_bg, gla_wr, gla_norm, gla_wo,
              ret_wq, ret_wk, ret_wv, ret_wg, ret_wo,
              swa_wqkv, swa_qnorm, swa_knorm, swa_sinks, swa_wo):
    gla_states = [state_gla_l0, state_gla_l3]
    ret_states = [state_ret_l1]
    swa_caches = [(cache_swa_k_l2, cache_swa_v_l2)]
    yp, ys = x_prompt, x_sample
    bp = x_prompt.shape[0]
    new_state = []
    for layer in range(DEPTH):
        kind, j = layer % N_MIXERS, layer // N_MIXERS
        sh_mp, sc_mp, g_mp, sh_fp, sc_fp, g_fp = ada_modulation(c_prompt, ada_w[layer], ada_b[layer])
        sh_ms, sc_ms, g_ms, sh_fs, sc_fs, g_fs = ada_modulation(c_sample, ada_w[layer], ada_b[layer])
        hp = modulate(yp, norm_mix[layer], sh_mp, sc_mp)
        hs = modulate(ys, norm_mix[layer], sh_ms, sc_ms)
        if kind == 0:
            w = (gla_wq[j], gla_wk[j], gla_wv[j], gla_wg1[j], gla_wg2[j], gla_bg[j], gla_wr[j], gla_norm[j], gla_wo[j])
            zero_state = jnp.zeros((bp, GLA_HEADS, GLA_DK, GLA_DV), jnp.float32)
            o_p, st_p = gla_mixer(hp, zero_state, *w)
            o_s, st_s = gla_mixer(hs, gla_states[j], *w)
            new_state += [st_p, st_s]
        elif kind == 1:
            w = (ret_wq[j], ret_wk[j], ret_wv[j], ret_wg[j], ret_wo[j])
            zero_state = jnp.zeros((bp, RET_HEADS, RET_DK, RET_DV), jnp.float32)
            o_p, st_p = retention_mixer(hp, zero_state, *w)
            o_s, st_s = retention_mixer(hs, ret_states[j], *w)
            new_state += [st_p, st_s]
        else:
            w = (swa_wqkv[j], swa_qnorm[j], swa_knorm[j], swa_sinks[j], swa_wo[j])
            ck, cv = swa_caches[j]
            o_p, k_p, v_p = swa_prompt(hp, *w)
            o_s, k_s, v_s = swa_sample(hs, ck, cv, *w)
            new_state += [k_p, v_p, k_s, v_s]
        yp = yp + g_mp * o_p
        ys = ys + g_ms * o_s
        yp = yp + g_fp * sq_relu_mlp(modulate(yp, norm_mlp[layer], sh_fp, sc_fp), mlp_w1[layer], mlp_w2[layer])
        ys = ys + g_fs * sq_relu_mlp(modulate(ys, norm_mlp[layer], sh_fs, sc_fs), mlp_w1[layer], mlp_w2[layer])
    return (yp, ys, *new_state)
```

```python
import math
from contextlib import ExitStack
import numpy as np
import concourse.bass as bass
import concourse.mybir as mybir
from concourse.bass_utils import run_bass_kernel_spmd

F32 = mybir.dt.float32
BF16 = mybir.dt.bfloat16
AF = mybir.ActivationFunctionType
ALU = mybir.AluOpType

D = 1024
KC = 8
L_LAYERS = 4
SB_BASE = 16896
SB_END = 229376
PG = 256
W = 512
NSLOT = 6
SLAB = 4096
NORM_EPS = 1e-6
GN_EPS = 1e-5
RET_LG = [math.log(1.0 - 2.0 ** (-5.0 - h)) for h in range(4)]
SLOPES = [2.0 ** (-8.0 * (h + 1) / 16) for h in range(16)]


class T:
    def __init__(self, handle, pstep):
        self.h = handle
        self.pstep = pstep

    def v(self, p0, npart, off, dims):
        return bass.AP(self.h, p0 * self.pstep + off, [[self.pstep, npart]] + [list(d) for d in dims])

    def c(self, k, lo, n, p0=0, npart=128, w=W):
        return self.v(p0, npart, k * w + lo, [(1, n)])


class Op:
    __slots__ = ("eng", "fn", "waits", "signal", "sig", "dma", "cseq", "ph")


class Prog:
    def __init__(self, nc):
        self.nc = nc
        self.tinfo = {}
        self.sb_ptr = SB_BASE
        self.ops = {e: [] for e in ("pe", "act", "dve", "pool", "sp")}
        self.cops = {e: [] for e in self.ops}
        self.last_w = {}
        self.readers = {}
        self.seen = {e: {} for e in self.ops}
        self.dma_cnt = {}
        self.pcache = {}
        self.phase = "pro"
        self.banks = []
        for i in range(8):
            h = nc.alloc_psum_tensor(f"psb{i}", [128, 512], F32)
            self.tinfo[h.name] = ("ps", i * 2048)
            self.banks.append(T(h, 512))
        self.bank_live = [False] * 8
        self.bank_rr = 0

    def sb(self, name, free, dtype, at=None, parts=128):
        esz = 2 if dtype == BF16 else 4
        nbytes = (free * esz + 63) // 64 * 64
        if at is None:
            at = (self.sb_ptr + PG - 1) // PG * PG
            self.sb_ptr = at + nbytes
        assert at + nbytes <= SB_END, (name, at, nbytes)
        h = self.nc.alloc_sbuf_tensor_at(name, [parts, free], dtype, offset=at)
        self.tinfo[h.name] = ("sb", at)
        return T(h, free)

    def bank(self):
        for i in range(8):
            b = (self.bank_rr + i) % 8
            if not self.bank_live[b]:
                self.bank_live[b] = True
                self.bank_rr = (b + 1) % 8
                return b
        raise RuntimeError("no free psum bank")

    def free(self, b):
        self.bank_live[b] = False

    def pb(self, b, p0, npart, off, dims):
        return self.banks[b].v(p0, npart, off, dims)

    def pb16(self, b, p0, npart, off, dims):
        a = self.banks[b].h.ap().bitcast(BF16)
        return bass.AP(a.tensor, p0 * 1024 + off, [[1024, npart]] + [list(d) for d in dims])

    def pages(self, ap):
        key = (ap.tensor.name, ap.offset, ap.ap, ap.dtype == BF16)
        r = self.pcache.get(key)
        if r is not None:
            return r
        info = self.tinfo.get(ap.tensor.name)
        if info is None:
            self.pcache[key] = ()
            return ()
        space, base = info
        if space == "ps":
            r = ((1 << 20) + base // 2048,)
            self.pcache[key] = r
            return r
        esz = 2 if ap.dtype == BF16 else 4
        dims = ap.ap
        pstep = dims[0][0]
        inner = ap.offset % pstep if pstep else ap.offset
        free = list(dims[1:])
        starts = [inner]
        for st, cnt in free[:-1]:
            if len(starts) * cnt > 512:
                starts = [min(starts), max(starts) + st * (cnt - 1)]
                continue
            starts = [s + st * i for s in starts for i in range(cnt)]
        lst, lcnt = free[-1]
        length = (lcnt - 1) * abs(lst) + 1
        pg = set()
        sp = 0 if space == "sb" else (1 << 20)
        for s in set(starts):
            lo = base + s * esz
            hi = base + (s + length) * esz
            pg.update(range(sp + lo // PG, sp + (hi - 1) // PG + 1))
        r = tuple(pg)
        self.pcache[key] = r
        return r

    def add(self, eng, fn, reads, writes, stream=None, extra=()):
        op = Op()
        op.eng = eng
        op.ph = self.phase
        op.fn = fn
        op.signal = False
        op.sig = None
        op.dma = stream
        deps = {}

        def need(tok, raw):
            k, s = tok
            if stream is None and k == eng and eng == "pe":
                return
            if deps.get(k, 0) < s:
                deps[k] = s

        rp = set()
        for a in reads:
            rp.update(self.pages(a))
        wp = set()
        for a in writes:
            wp.update(self.pages(a))
        for p in rp:
            t = self.last_w.get(p)
            if t:
                need(t, True)
        for p in wp:
            t = self.last_w.get(p)
            if t:
                need(t, False)
            rd = self.readers.get(p)
            if rd:
                for k, s in rd.items():
                    need((k, s), False)
        for tok_ in extra:
            need(tok_, True)
        if stream is not None:
            n = self.dma_cnt.get(stream, 0) + 1
            self.dma_cnt[stream] = n
            tok = (("dma", stream), n)
            if n > 1:
                need((("dma", stream), n - 1), True)
        else:
            self.cops[eng].append(op)
            op.cseq = len(self.cops[eng])
            tok = (eng, op.cseq)
        seen = self.seen[eng]
        waits = []
        for k, s in deps.items():
            if seen.get(k, 0) >= s:
                continue
            seen[k] = s
            waits.append((k, s))
            if not isinstance(k, tuple):
                self.cops[k][s - 1].signal = True
        op.waits = waits
        self.ops[eng].append(op)
        for p in rp:
            rd = self.readers.get(p)
            if rd is None:
                rd = self.readers[p] = {}
            if rd.get(tok[0], 0) < tok[1]:
                rd[tok[0]] = tok[1]
        for p in wp:
            self.last_w[p] = tok
            self.readers[p] = {}
        return op

    def mm(self, out, lhsT, rhs, start, stop):
        self.add("pe", lambda e: e.matmul(out, lhsT, rhs, start=start, stop=stop), [lhsT, rhs], [out])

    def tr(self, out, in_, ident):
        self.add("pe", lambda e: e.transpose(out, in_, ident), [in_, ident], [out])

    def act(self, out, in_, func, scale=None, bias=None):
        kw = {}
        rd = [in_]
        if scale is not None:
            kw["scale"] = scale
            if not isinstance(scale, float):
                rd.append(scale)
        if bias is not None:
            kw["bias"] = bias
            if not isinstance(bias, float):
                rd.append(bias)
        self.add("act", lambda e: e.activation(out, in_, func, **kw), rd, [out])

    def tt(self, out, a, b, op, eng="dve"):
        self.add(eng, lambda e: e.tensor_tensor(out, a, b, op), [a, b], [out])

    def stt(self, out, in0, scalar, in1, op0, op1, eng="dve"):
        rd = [in0, in1]
        if not isinstance(scalar, float):
            rd.append(scalar)
        self.add(eng, lambda e: e.scalar_tensor_tensor(out, in0, scalar, in1, op0, op1), rd, [out])

    def ts(self, out, in0, s1, s2, op0, op1=None, eng="dve"):
        rd = [in0]
        for s in (s1, s2):
            if s is not None and not isinstance(s, float):
                rd.append(s)
        if op1 is None:
            self.add(eng, lambda e: e.tensor_scalar(out, in0, s1, None, op0), rd, [out])
        else:
            self.add(eng, lambda e: e.tensor_scalar(out, in0, s1, s2, op0, op1), rd, [out])

    def cp(self, out, in_, eng="dve"):
        if eng == "act":
            self.add("act", lambda e: e.activation(out, in_, AF.Copy), [in_], [out])
        else:
            self.add(eng, lambda e: e.tensor_copy(out, in_), [in_], [out])

    def recip(self, out, in_):
        self.add("dve", lambda e: e.reciprocal(out, in_), [in_], [out])

    def memset(self, out, val, eng="dve"):
        self.add(eng, lambda e: e.memset(out, val), [], [out])

    def dma(self, eng, out, in_, stream, extra=()):
        self.add(eng, lambda e: e.dma_start(out=out, in_=in_), [in_], [out], stream=stream, extra=extra)

    def finish(self):
        nc = self.nc
        GEN = 60000
        with ExitStack() as es:
            sems = {}
            for e in self.ops:
                n = 0
                for op in self.cops[e]:
                    if op.signal:
                        n += 1
                        op.sig = n
                ng = max(1, (n + GEN - 1) // GEN)
                sems[e] = [es.enter_context(nc.semaphore(f"s_{e}{g}")) for g in range(ng)]
            dsem = {s: es.enter_context(nc.semaphore(f"d_{i}")) for i, s in enumerate(self.dma_cnt)}
            block = es.enter_context(nc.Block())

            def run(ename, e):
                for op in self.ops[ename]:
                    for k, s in op.waits:
                        if isinstance(k, tuple):
                            e.wait_ge(dsem[k[1]], 16 * s)
                        else:
                            c = self.cops[k][s - 1].sig
                            e.wait_ge(sems[k][(c - 1) // GEN], (c - 1) % GEN + 1)
                    ins = op.fn(e)
                    if op.dma is not None:
                        ins.then_inc(dsem[op.dma], 16)
                    elif op.signal:
                        ins.then_inc(sems[ename][(op.sig - 1) // GEN], 1)
                if ename == "sp":
                    for s, n in self.dma_cnt.items():
                        e.wait_ge(dsem[s], 16 * n)

            block.tensor(lambda e: run("pe", e))
            block.scalar(lambda e: run("act", e))
            block.vector(lambda e: run("dve", e))
            block.gpsimd(lambda e: run("pool", e))
            block.sync(lambda e: run("sp", e))


def stream_blocks():
    out = []
    for l in range(L_LAYERS):
        kind, j = l % 3, l // 3
        if kind == 0:
            out += [("gla_wq", j, 0, 0), ("gla_wk", j, 0, 0)]
            out += [("gla_wv", j, 0, c * 512) for c in range(2)]
            out += [("gla_wr", j, 0, c * 512) for c in range(2)]
            out += [("gla_wo", j, 0, c * 512) for c in range(2)]
        elif kind == 1:
            out += [("ret_wq", j, 0, c * 512) for c in range(2)]
            out += [("ret_wk", j, 0, c * 512) for c in range(2)]
            out += [("ret_wv", j, 0, c * 512) for c in range(4)]
            out += [("ret_wg", j, 0, c * 512) for c in range(4)]
            out += [("ret_wo", j, kb * 1024, cb * 512) for cb in range(2) for kb in range(2)]
        else:
            out += [("swa_wqkv", j, 0, c * 512) for c in range(3)]
            out += [("swa_wo", j, 0, c * 512) for c in range(2)]
        out += [("mlp_w1", l, 0, n * 512) for n in range(8)]
        out += [("mlp_w2", l, kb * 1024, cb * 512) for cb in range(2) for kb in range(4)]
    return out


def slab_np(Wm, r0, c0):
    return np.ascontiguousarray(Wm[r0:r0 + 1024, c0:c0 + 512].reshape(8, 128, 512).transpose(1, 0, 2)).reshape(128, 4096)


def const_table():
    s = np.arange(128)[:, None].astype(np.float64)
    t = np.arange(128)[None, :].astype(np.float64)
    ident = np.eye(128)
    tri16 = -(1.0 / 16.0) * (s <= t)
    triu = 1.0 * (s <= t)
    blk = 1.0 * ((s // 64) == (t // 64))
    BIG = -1.0e6
    ndpp = -(t - s + 128.0)
    ndpp = np.where((t >= 64) & (s < 64), BIG, ndpp)
    ndop = -np.abs(t - s)
    ndop = np.where((t < 64) & (s >= 64), BIG, ndop)
    ndps = -(t - s + 128.0)
    ndos = -np.abs(t - s)
    reb = np.concatenate([np.broadcast_to(np.exp(RET_LG[h] * (t + 1.0)), (128, 128)) for h in range(4)], 1)
    renb = np.concatenate([np.broadcast_to(np.exp(-RET_LG[h] * (t + 1.0)), (128, 128)) for h in range(4)], 1)
    return np.concatenate([ident, tri16, triu, blk, ndpp, ndop, ndps, ndos, reb, renb], 1).astype(np.float32)


C_ID, C_TRI16, C_TRIU, C_BLK, C_NDPP, C_NDOP, C_NDPS, C_NDOS, C_REB, C_RENB = [i * 128 for i in range(8)] + [1024, 1536]
NCST = 2048


class StopBuild(Exception):
    pass


def build(NP, SEQ, NS, TS=16):
    import os
    nc = bass.Bass("TRN2", target_bir_lowering=False)
    P = Prog(nc)
    kstop = int(os.environ.get("KSTOP", "0"))
    stg = {"n": 0}

    def mark(name):
        stg["n"] += 1
        if kstop and stg["n"] == kstop:
            print("STOP at stage", stg["n"], name, flush=True)
            raise StopBuild()
    try:
        _build_body(nc, P, NP, SEQ, NS, TS, mark)
    except StopBuild:
        pass
    P.finish()
    return nc


def _build_body(nc, P, NP, SEQ, NS, TS, mark):
    blocks = stream_blocks()
    NBLK = len(blocks)
    NSEQ = NP + NS

    def din(name, shape):
        return nc.dram_tensor(name, list(shape), F32, kind="ExternalInput").ap()

    def dout(name, shape):
        return nc.dram_tensor(name, list(shape), F32, kind="ExternalOutput").ap()

    xp = din("xp", [NP * SEQ, D]); xs = din("xs", [NS * TS, D])
    wstream = din("wstream", [NBLK, 128, SLAB])
    wbf = nc.dram_tensor("wbf", [NBLK, 128, SLAB], BF16, kind="Internal").ap()
    adaw = din("adaw", [L_LAYERS * 12, 128, SLAB])
    cT_d = din("cT", [128, KC * 4]); adab_d = din("adab", [128, L_LAYERS * 48]); nrm_d = din("nrm", [128, 64])
    glan_d = din("glan", [128, 4]); qkn_d = din("qkn", [128, 2]); sinks_d = din("sinks", [1, 16])
    wg1_d = din("wg1", [128, 256]); wg2_d = din("wg2a", [32, 1024]); cst_d = din("cst", [128, NCST])
    sgla0_d = din("sgla0", [NS, 4, 128, 256]); sgla3_d = din("sgla3", [NS, 4, 128, 256])
    sret_d = din("sret", [NS, 4, 256, 512]); ck_d = din("ck", [NS, 128, 256]); cv_d = din("cv", [NS, 128, 256])
    yp = dout("yp", [NP * SEQ, D]); ys = dout("ys", [NS * TS, D])
    o_g0p = dout("o_g0p", [NP, 4, 128, 256]); o_g0s = dout("o_g0s", [NS, 4, 128, 256])
    o_rp = dout("o_rp", [NP, 4, 256, 512]); o_rs = dout("o_rs", [NS, 4, 256, 512])
    o_kp = dout("o_kp", [NP, 128, 256]); o_vp = dout("o_vp", [NP, 128, 256])
    o_ks = dout("o_ks", [NS, TS, 256]); o_vs = dout("o_vs", [NS, TS, 256])
    o_g3p = dout("o_g3p", [NP, 4, 128, 256]); o_g3s = dout("o_g3s", [NS, 4, 128, 256])

    cst = P.sb("cst", NCST, F32)
    idb = P.sb("idb", 128, BF16); onesb = P.sb("onesb", 128, BF16); blkb = P.sb("blkb", 128, BF16)
    modT = P.sb("modT", L_LAYERS * 6 * 8 * 4, F32); gsT = P.sb("gsT", L_LAYERS * 2 * 8 * 4, F32)
    nrm = P.sb("nrm", 64, F32); glan = P.sb("glan", 4, F32); qkn = P.sb("qkn", 2, F32)
    esink = P.sb("esink", 16, F32); csil = P.sb("csil", 32, F32)
    wg1b = P.sb("wg1b", 256, BF16); wg2b = P.sb("wg2b", 1024, BF16); g1a = P.sb("g1a", 512, BF16)
    xT = P.sb("xT", KC * W, F32); hT = P.sb("hT", KC * W, BF16)
    ring = P.sb("ring", NSLOT * SLAB, BF16)
    Sg = [P.sb("Sg0", 1024, F32), P.sb("Sg3", 1024, F32)]
    Sr = P.sb("Sr", 4096, F32)
    kTd = P.sb("kTd", 4 * 640, BF16); vdup = P.sb("vdup", 5 * 512, BF16)
    ckT = P.sb("ckT", NS * 4 * 128, BF16); cvd = P.sb("cvd", NS * 512, BF16)
    sq = P.sb("sq", 2 * 512, BF16); rstd = P.sb("rstd", 512, F32); tmpf = P.sb("tmpf", 2 * 512, F32)
    stage = P.sb("stage", 2 * 1024, F32)
    AR = (P.sb_ptr + PG - 1) // PG * PG
    P.sb_ptr = AR
    ARENA = 63 * 1024
    P.sb_ptr += ARENA
    assert P.sb_ptr <= SB_END, P.sb_ptr

    def ar(name, off, free, dt):
        return P.sb(name, free, dt, at=AR + off)

    g_eb = ar("g_eb", 0, 4 * 512, F32); g_enb = ar("g_enb", 8192, 4 * 512, F32)
    g_qd = ar("g_qd", 16384, 4 * 512, BF16); g_kd = ar("g_kd", 20480, 4 * 512, BF16)
    g_vt = ar("g_vt", 24576, 4 * 1024, BF16); g_rT = ar("g_rT", 32768, 8 * 512, BF16)
    g_gt = ar("g_gt", 40960, 2 * 512, F32); g_kdt = ar("g_kdt", 45056, 2 * 512, BF16)
    g_att = ar("g_att", 47104, 2 * 512, BF16); g_Sb = ar("g_Sb", 49152, 1024, BF16)
    g_ot = ar("g_ot", 51200, 512, F32)
    g_qlo = ar("g_qlo", 53248, 4 * 128, BF16); g_klo = ar("g_klo", 54272, 4 * 128, BF16)
    g_Sb2 = ar("g_Sb2", 55296, 1024, BF16)
    r_qd = ar("r_qd", 0, 8 * 512, BF16); r_kd = ar("r_kd", 8192, 8 * 512, BF16)
    r_vt = ar("r_vt", 16384, 4 * 2048, BF16); r_rT = ar("r_rT", 32768, 16 * 512, BF16)
    r_kdt = ar("r_kdt", 49152, 1024, BF16); r_att = ar("r_att", 51200, 512, BF16)
    r_Sb = ar("r_Sb", 52224, 1024, BF16); r_ob = ar("r_ob", 54272, 512, BF16); r_sq = ar("r_sq", 55296, 512, BF16)
    r_ot = ar("r_ot", 56320, 512, F32)
    r_on = ar("r_on", 56320, 512, BF16); r_on2 = ar("r_on2", 57344, 512, BF16)
    r_qlo = ar("r_qlo", 58368, 8 * 128, BF16); r_klo = ar("r_klo", 60416, 8 * 128, BF16)
    r_Sb2 = ar("r_Sb2", 62464, 1024, BF16)
    s_qn = ar("s_qn", 0, 8 * 512, BF16); s_og = ar("s_og", 8192, 8 * 512, BF16)
    s_lg = ar("s_lg", 16384, 2 * 512, F32); s_pT = ar("s_pT", 20480, 2 * 512, BF16)
    s_kfin = ar("s_kfin", 22528, 8 * 128, F32); s_vfin = ar("s_vfin", 26624, 256, F32)
    s_kout = ar("s_kout", 27648, 256, F32); s_den = ar("s_den", 28672, 512, F32); s_rden = ar("s_rden", 30720, 512, F32)
    s_lg2 = ar("s_lg2", 36864, 2 * 512, F32); s_pT2 = ar("s_pT2", 40960, 2 * 512, BF16)
    s_den2 = ar("s_den2", 43008, 512, F32); s_rden2 = ar("s_rden2", 45056, 512, F32)
    s_kc = ar("s_kc", 32768, 256, F32); s_vc = ar("s_vc", 33792, 256, F32); s_kdp = ar("s_kdp", 34816, 128, F32)
    m_hid = ar("m_hid", 0, 32 * 512, BF16); m_rl = ar("m_rl", 32768, 2 * 512, F32)
    xstage = ar("xstage", 40960, 4 * 1024, F32)
    p_aw = [ar("p_aw0", 0, SLAB, F32), ar("p_aw1", 16384, SLAB, F32)]
    p_ab = ar("p_ab", 32768, L_LAYERS * 48, F32); p_cT = ar("p_cT", 34816, 32, F32)
    p_wg1 = ar("p_wg1", 36864, 256, F32)

    def full(t, n, p0=0, npart=128, off=0):
        return t.v(p0, npart, off, [(1, n)])

    wstate = {"emitted": 0, "next": 0}
    TOTAL_PASSES = NP * (SEQ // 512) + 1
    TOTAL_SLABS = TOTAL_PASSES * NBLK

    def next_slab(expect):
        i = wstate["next"]
        assert blocks[i % NBLK][0] == expect, (blocks[i % NBLK], expect)
        while wstate["emitted"] < min(TOTAL_SLABS, i + NSLOT):
            jj = wstate["emitted"]
            slot = jj % NSLOT
            if jj < NBLK:
                P.dma("pool", full(ring, SLAB, off=slot * SLAB), wstream[jj], ("w", slot))
                P.dma("sp", wbf[jj], full(ring, SLAB, off=slot * SLAB), "wst")
            else:
                ex = [(("dma", "wst"), jj % NBLK + 1)] if jj < 2 * NBLK else []
                P.dma("pool", full(ring, SLAB, off=slot * SLAB), wbf[jj % NBLK], ("w", slot), extra=ex)
            wstate["emitted"] += 1
        wstate["next"] += 1
        return (i % NSLOT) * SLAB

    def wl(base, k, c0, n):
        return ring.v(0, 128, base + k * 512 + c0, [(1, n)])

    P.dma("sp", full(cst, NCST), cst_d, "misc")
    P.dma("sp", full(nrm, 64), nrm_d, "misc")
    P.dma("sp", full(glan, 4), glan_d, "misc")
    P.dma("sp", full(qkn, 2), qkn_d, "misc")
    P.dma("sp", full(p_ab, L_LAYERS * 48), adab_d, "misc")
    P.dma("sp", full(p_cT, 32), cT_d, "misc")
    P.dma("sp", full(esink, 16), bass.AP(sinks_d.tensor, 0, [[0, 128], [1, 16]]), "misc")
    P.dma("pool", full(wg1b, 256), wg1_d, "misc2")
    P.dma("pool", full(wg2b, 1024, npart=32), wg2_d, "misc2")
    P.cp(full(idb, 128), cst.v(0, 128, C_ID, [(1, 128)]))
    P.cp(full(blkb, 128), cst.v(0, 128, C_BLK, [(1, 128)]))
    P.memset(full(onesb, 128), 1.0)
    P.memset(full(g1a, 512, npart=32), 1.0)
    P.act(full(esink, 16), full(esink, 16), AF.Exp)
    P.act(full(csil, 32), full(p_cT, 32), AF.Silu)
    p_mt = ar("p_mt", 38912, 512, F32)
    for l in range(L_LAYERS):
        bt = P.bank()
        for n12 in range(12):
            sidx = l * 12 + n12
            P.dma("sp", full(p_aw[sidx % 2], SLAB), adaw[sidx], ("aw", sidx % 2))
            aw = p_aw[sidx % 2]
            b = P.bank()
            for k in range(KC):
                P.mm(P.pb(b, 0, 4, 0, [(1, 512)]), csil.v(0, 128, k * 4, [(1, 4)]), aw.v(0, 128, k * 512, [(1, 512)]), k == 0, k == KC - 1)
            P.cp(p_mt.v(0, 4, 0, [(1, 512)]), P.pb(b, 0, 4, 0, [(1, 512)]), eng="act" if n12 % 2 else "dve")
            P.free(b)
            for jj4 in range(4):
                j = n12 * 4 + jj4
                P.tr(P.pb(bt, 0, 128, j * 4, [(1, 4)]), p_mt.v(0, 4, jj4 * 128, [(1, 128)]), cst.v(0, 4, C_ID, [(1, 4)]))
        P.tt(modT.v(0, 128, l * 192, [(4, 48), (1, 4)]), P.pb(bt, 0, 128, 0, [(4, 48), (1, 4)]),
             p_ab.v(0, 128, l * 48, [(1, 48), (0, 4)]), ALU.add)
        P.free(bt)

    def modap(l, c, k, s):
        return modT.v(0, 128, ((l * 6 + c) * 8 + k) * 4 + s, [(1, 1)])

    def gsap(l, which, k, s):
        return gsT.v(0, 128, ((l * 2 + which) * 8 + k) * 4 + s, [(1, 1)])

    for l in range(L_LAYERS):
        for which in range(2):
            for k in range(KC):
                P.ts(gsT.v(0, 128, ((l * 2 + which) * 8 + k) * 4, [(1, 4)]),
                     modT.v(0, 128, ((l * 6 + 1 + 3 * which) * 8 + k) * 4, [(1, 4)]),
                     1.0, nrm.v(0, 128, (l * 2 + which) * 8 + k, [(1, 1)]), ALU.add, ALU.mult)

    mark("prologue")

    def emit_xload(kind_, subs_):
        SL_ = 128 if kind_ == "p" else TS
        xin_ = xp if kind_ == "p" else xs
        for i_, q_ in enumerate(subs_):
            P.dma("sp", xstage.v(0, SL_, i_ * 1024, [(1, 1024)]), xin_[q_["xrow"]:q_["xrow"] + SL_, :], ("xl", i_))

    def process_tile(kind, subs, prefetched=False, nxt=None):
        SL = 128 if kind == "p" else TS
        NSUB = len(subs)
        TT = SL * NSUB
        if kind == "p":
            groups = [(0, TT, subs[0]["slot"])]
        else:
            groups = [(i * SL, SL, sb_["slot"]) for i, sb_ in enumerate(subs)]
        xin = xp if kind == "p" else xs
        yout = yp if kind == "p" else ys
        idf = cst.v(0, SL, C_ID, [(1, SL)])

        P.phase = "xload"
        if not prefetched:
            emit_xload(kind, subs)
        for i, sb_ in enumerate(subs):
            for half in range(2):
                b = P.bank()
                for kk in range(4):
                    k = half * 4 + kk
                    P.tr(P.pb(b, 0, 128, kk * SL, [(1, SL)]), xstage.v(0, SL, i * 1024 + k * 128, [(1, 128)]), idf)
                P.cp(xT.v(0, 128, half * 4 * W + i * SL, [(W, 4), (1, SL)]), P.pb(b, 0, 128, 0, [(SL, 4), (1, SL)]),
                     eng="act" if half else "dve")
                P.free(b)
        mark("xload")

        def norm_mod(l, which):
            P.phase = "norm"
            b = P.bank()
            for k in range(KC):
                sqk = sq.v(0, 128, (k % 2) * 512, [(1, TT)])
                P.act(sqk, xT.c(k, 0, TT), AF.Square)
                P.mm(P.pb(b, 0, 128, 0, [(1, TT)]), full(onesb, 128), sqk, k == 0, k == KC - 1)
            P.act(tmpf.v(0, 128, 0, [(1, TT)]), P.pb(b, 0, 128, 0, [(1, TT)]), AF.Ln, scale=1.0 / D, bias=NORM_EPS)
            P.free(b)
            P.act(full(rstd, TT), tmpf.v(0, 128, 0, [(1, TT)]), AF.Exp, scale=-0.5)
            for k in range(KC):
                t = tmpf.v(0, 128, (k % 2) * 512, [(1, TT)])
                P.tt(t, xT.c(k, 0, TT), full(rstd, TT), ALU.mult)
                for (c0, n, s) in groups:
                    P.act(hT.c(k, c0, n), tmpf.v(0, 128, (k % 2) * 512 + c0, [(1, n)]), AF.Identity,
                          scale=gsap(l, which, k, s), bias=modap(l, 3 * which, k, s))

        def resid(l, which, m, b):
            for (c0, n, s) in groups:
                P.stt(xT.c(m, c0, n), P.pb(b, 0, 128, c0, [(1, n)]), modap(l, 2 + 3 * which, m, s), xT.c(m, c0, n),
                      ALU.mult, ALU.add)

        def lin_fm(wname, nslab, kchunks_per_slab, src, evac, kouter=False):
            for n in range(nslab):
                base = next_slab(wname)
                if kouter and n == 0:
                    bs = [P.bank() for _ in range(4)]
                    for k in range(KC):
                        for m4 in range(4):
                            P.mm(P.pb(bs[m4], 0, 128, 0, [(1, TT)]), wl(base, k, m4 * 128, 128), src.c(k, 0, TT), k == 0, k == KC - 1)
                    for m4 in range(4):
                        evac(m4, bs[m4])
                        P.free(bs[m4])
                    continue
                for m4 in range(4):
                    b = P.bank()
                    for k in range(KC):
                        P.mm(P.pb(b, 0, 128, 0, [(1, TT)]), wl(base, k, m4 * 128, 128), src.c(k, 0, TT), k == 0, k == KC - 1)
                    evac(n * 4 + m4, b)
                    P.free(b)

        def lin_fm_bigk(wname, nkb, src, evac):
            for cb in range(2):
                bs = [P.bank() for _ in range(4)]
                for kb in range(nkb):
                    base = next_slab(wname)
                    for m4 in range(4):
                        for k in range(KC):
                            P.mm(P.pb(bs[m4], 0, 128, 0, [(1, TT)]), wl(base, k, m4 * 128, 128), src.c(kb * 8 + k, 0, TT),
                                 kb == 0 and k == 0, kb == nkb - 1 and k == KC - 1)
                for m4 in range(4):
                    evac(cb * 4 + m4, bs[m4])
                    P.free(bs[m4])

        def lin_tm(wname, nslab, dst, dst_w, evac_eng_alt=True):
            for n in range(nslab):
                base = next_slab(wname)
                for i in range(NSUB):
                    b = P.bank()
                    for k in range(KC):
                        P.mm(P.pb(b, 0, SL, 0, [(1, 512)]), hT.c(k, i * SL, SL), wl(base, k, 0, 512), k == 0, k == KC - 1)
                    if (i + n) % 2 == 0:
                        P.act(dst.v(0, SL, i * dst_w + n * 512, [(1, 512)]), P.pb(b, 0, SL, 0, [(1, 512)]), AF.Copy)
                    else:
                        P.cp(dst.v(0, SL, i * dst_w + n * 512, [(1, 512)]), P.pb(b, 0, SL, 0, [(1, 512)]))
                    P.free(b)

        def rstd_from(b, ncols, scale, eps, out):
            P.act(tmpf.v(0, 128, 0, [(1, ncols)]), P.pb(b, 0, 128, 0, [(1, ncols)]), AF.Ln, scale=scale, bias=eps)
            P.act(out, tmpf.v(0, 128, 0, [(1, ncols)]), AF.Exp, scale=-0.5)

        def gla(l):
            j = l // 3
            S = Sg[j]
            sdram = (sgla0_d, sgla3_d)[j]
            odram_p, odram_s = ((o_g0p, o_g0s), (o_g3p, o_g3s))[j]
            P.phase = "gla_gate"
            b = P.bank()
            for k in range(KC):
                P.mm(P.pb(b, 0, 16, 0, [(1, TT)]), wg1b.v(0, 128, (j * 8 + k) * 16, [(1, 16)]), hT.c(k, 0, TT), k == 0, k == KC - 1)
            P.act(g1a.v(0, 16, 0, [(1, TT)]), P.pb(b, 0, 16, 0, [(1, TT)]), AF.Copy)
            P.free(b)
            def gate_l(i):
                b = P.bank()
                P.mm(P.pb(b, 0, SL, 0, [(1, 512)]), g1a.v(0, 32, i * SL, [(1, SL)]), wg2b.v(0, 32, j * 512, [(1, 512)]), True, True)
                gt = g_gt.v(0, SL, (i % 2) * 512, [(1, 512)])
                P.act(gt, P.pb(b, 0, SL, 0, [(1, 512)]), AF.Exp, scale=-1.0)
                P.free(b)
                P.act(gt, gt, AF.Ln, bias=1.0)

            def gate_c(i):
                b = P.bank()
                for h in range(4):
                    P.mm(P.pb(b, 0, 128, h * SL, [(1, SL)]), g_gt.v(0, SL, (i % 2) * 512 + h * 128, [(1, 128)]),
                         cst.v(0, SL, C_TRI16, [(1, SL)]), True, True)
                P.act(g_eb.v(0, 128, i * SL, [(512, 4), (1, SL)]), P.pb(b, 0, 128, 0, [(SL, 4), (1, SL)]), AF.Exp)
                P.act(g_enb.v(0, 128, i * SL, [(512, 4), (1, SL)]), P.pb(b, 0, 128, 0, [(SL, 4), (1, SL)]), AF.Exp, scale=-1.0)
                P.free(b)
            gate_l(0)
            for i in range(NSUB):
                if i + 1 < NSUB:
                    gate_l(i + 1)
                gate_c(i)
            hilo = subs[0]["first"] and kind == "p"
            P.phase = "gla_proj"

            def ev_gq(m, b):
                P.stt(g_qd.c(m, 0, TT), P.pb(b, 0, 128, 0, [(1, TT)]), 128.0 ** -0.5, g_eb.c(m, 0, TT), ALU.mult, ALU.mult)
                if hilo:
                    xf = tmpf.v(0, 128, (m % 2) * 512, [(1, 128)])
                    P.stt(xf, P.pb(b, 0, 128, 0, [(1, 128)]), 128.0 ** -0.5, g_eb.c(m, 0, 128), ALU.mult, ALU.mult)
                    P.tt(g_qlo.c(m, 0, 128, w=128), xf, g_qd.c(m, 0, 128), ALU.subtract)

            def ev_gk(m, b):
                P.tt(g_kd.c(m, 0, TT), P.pb(b, 0, 128, 0, [(1, TT)]), g_enb.c(m, 0, TT), ALU.mult)
                if hilo:
                    xf = tmpf.v(0, 128, (m % 2) * 512, [(1, 128)])
                    P.tt(xf, P.pb(b, 0, 128, 0, [(1, 128)]), g_enb.c(m, 0, 128), ALU.mult)
                    P.tt(g_klo.c(m, 0, 128, w=128), xf, g_kd.c(m, 0, 128), ALU.subtract)
            lin_fm("gla_wq", 1, 8, hT, ev_gq)
            lin_fm("gla_wk", 1, 8, hT, ev_gk)
            lin_tm("gla_wv", 2, g_vt, 1024)

            def ev_r(m, b):
                P.act(tmpf.v(0, 128, (m % 2) * 512, [(1, TT)]), P.pb(b, 0, 128, 0, [(1, TT)]), AF.Silu)
                P.ts(g_rT.c(m, 0, TT), tmpf.v(0, 128, (m % 2) * 512, [(1, TT)]), glan.v(0, 128, j * 2 + m % 2, [(1, 1)]), None, ALU.mult)
            lin_fm("gla_wr", 2, 8, hT, ev_r)
            P.phase = "gla_rec"
            hs_ = [(not q["first"]) or kind == "s" for q in subs]
            bo_of = {}

            def g_T(i):
                sb_ = subs[i]
                cols = i * SL
                if kind == "s":
                    P.dma("sp", S.v(0, 128, 0, [(256, 4), (1, 256)]), sdram[sb_["bidx"]].rearrange("h d v -> d h v"), "misc")
                b = P.bank()
                for h in range(4):
                    P.tr(P.pb16(b, 0, SL, h * 128, [(1, 128)]), g_kd.c(h, cols, SL), full(idb, 128))
                P.act(g_kdt.v(0, SL, (i % 2) * 512, [(1, 512)]), P.pb16(b, 0, SL, 0, [(1, 512)]), AF.Copy)
                P.free(b)
                if hs_[i]:
                    P.act(full((g_Sb, g_Sb2)[i % 2], 1024), full(S, 1024), AF.Copy)

            def g_A(i):
                cols = i * SL
                b = P.bank()
                for h in range(4):
                    ex = hilo and i == 0
                    P.mm(P.pb(b, 0, SL, h * SL, [(1, SL)]), g_kd.c(h, cols, SL), g_qd.c(h, cols, SL), True, not ex)
                    if ex:
                        P.mm(P.pb(b, 0, SL, h * SL, [(1, SL)]), g_kd.c(h, 0, 128), g_qlo.c(h, 0, 128, w=128), False, False)
                        P.mm(P.pb(b, 0, SL, h * SL, [(1, SL)]), g_klo.c(h, 0, 128, w=128), g_qd.c(h, 0, 128), False, True)
                att = g_att.v(0, SL, (i % 2) * 512, [(SL, 4), (1, SL)])
                P.tt(att, P.pb(b, 0, SL, 0, [(SL, 4), (1, SL)]), cst.v(0, SL, C_TRIU, [(0, 4), (1, SL)]), ALU.mult)
                P.free(b)

            def g_U(i):
                sb_ = subs[i]
                cols = i * SL
                for hp in range(2):
                    b = P.bank()
                    for hh in range(2):
                        h = hp * 2 + hh
                        P.mm(P.pb(b, 0, 128, hh * 256, [(1, 256)]), g_kdt.v(0, SL, (i % 2) * 512 + h * 128, [(1, 128)]),
                             g_vt.v(0, SL, i * 1024 + h * 256, [(1, 256)]), True, True)
                    for hh in range(2):
                        h = hp * 2 + hh
                        ebl = g_eb.v(0, 128, h * 512 + cols + SL - 1, [(1, 1)])
                        Sh = S.v(0, 128, h * 256, [(1, 256)])
                        if hs_[i]:
                            P.ts(Sh, Sh, ebl, None, ALU.mult)
                            P.stt(Sh, P.pb(b, 0, 128, hh * 256, [(1, 256)]), ebl, Sh, ALU.mult, ALU.add)
                        else:
                            P.ts(Sh, P.pb(b, 0, 128, hh * 256, [(1, 256)]), ebl, None, ALU.mult)
                    P.free(b)
                if sb_["last"]:
                    od = odram_p if kind == "p" else odram_s
                    P.dma("sp", od[sb_["bidx"]].rearrange("h d v -> d h v"), S.v(0, 128, 0, [(256, 4), (1, 256)]), "out")

            def g_O(i):
                cols = i * SL
                Sb_ = (g_Sb, g_Sb2)[i % 2]
                bo = [P.bank(), P.bank()]
                for h in range(4):
                    for jv in range(2):
                        o_ap = P.pb(bo[h // 2], 0, 128, ((h % 2) * 2 + jv) * SL, [(1, SL)])
                        P.mm(o_ap, g_vt.v(0, SL, i * 1024 + h * 256 + jv * 128, [(1, 128)]),
                             g_att.v(0, SL, (i % 2) * 512 + h * SL, [(1, SL)]), True, not hs_[i])
                        if hs_[i]:
                            P.mm(o_ap, Sb_.v(0, 128, h * 256 + jv * 128, [(1, 128)]), g_qd.c(h, cols, SL), False, True)
                for hp in range(2):
                    P.act(sq.v(0, 128, hp * 512, [(1, 4 * SL)]), P.pb(bo[hp], 0, 128, 0, [(1, 4 * SL)]), AF.Square)
                bo_of[i] = bo

            def g_N(i):
                cols = i * SL
                bo = bo_of[i]
                bn = P.bank()
                for hp in range(2):
                    for hh in range(2):
                        for jv in range(2):
                            P.mm(P.pb(bn, 0, 128, (hp * 2 + hh) * SL, [(1, SL)]), full(onesb, 128),
                                 sq.v(0, 128, hp * 512 + (hh * 2 + jv) * SL, [(1, SL)]), jv == 0, jv == 1)
                rstd_from(bn, 4 * SL, 1.0 / 256, NORM_EPS, full(rstd, 4 * SL))
                P.free(bn)
                for hp in range(2):
                    ot = g_ot.v(0, 128, 0, [(2 * SL, 2), (SL, 2), (1, SL)])
                    P.tt(ot, P.pb(bo[hp], 0, 128, 0, [(2 * SL, 2), (SL, 2), (1, SL)]),
                         rstd.v(0, 128, hp * 2 * SL, [(SL, 2), (0, 2), (1, SL)]), ALU.mult)
                    rt = g_rT.v(0, 128, hp * 4 * W + cols, [(W, 4), (1, SL)])
                    P.tt(rt, g_ot.v(0, 128, 0, [(SL, 4), (1, SL)]), rt, ALU.mult)
                    P.free(bo[hp])

            g_T(0); g_A(0); g_U(0)
            for i in range(NSUB):
                if i + 1 < NSUB:
                    g_T(i + 1); g_A(i + 1)
                g_O(i)
                if i + 1 < NSUB:
                    g_U(i + 1)
                g_N(i)
            P.phase = "gla_wo"
            lin_fm("gla_wo", 2, 8, g_rT, lambda m, b: resid(l, 0, m, b))

        def ret(l):
            S = Sr
            sc_k = 256.0 ** -0.5

            hilo = subs[0]["first"] and kind == "p"

            def ev_q(m, b):
                h = m // 2
                P.tt(r_qd.v(0, 128, m * W, [(SL, NSUB), (1, SL)]), P.pb(b, 0, 128, 0, [(SL, NSUB), (1, SL)]),
                     cst.v(0, 128, C_REB + h * 128, [(0, NSUB), (1, SL)]), ALU.mult)
                if hilo:
                    xf = tmpf.v(0, 128, (m % 2) * 512, [(1, 128)])
                    P.tt(xf, P.pb(b, 0, 128, 0, [(1, 128)]), cst.v(0, 128, C_REB + h * 128, [(1, 128)]), ALU.mult)
                    P.tt(r_qlo.c(m, 0, 128, w=128), xf, r_qd.c(m, 0, 128), ALU.subtract)

            def ev_k(m, b):
                h = m // 2
                P.stt(r_kd.v(0, 128, m * W, [(SL, NSUB), (1, SL)]), P.pb(b, 0, 128, 0, [(SL, NSUB), (1, SL)]), sc_k,
                      cst.v(0, 128, C_RENB + h * 128, [(0, NSUB), (1, SL)]), ALU.mult, ALU.mult)
                if hilo:
                    xf = tmpf.v(0, 128, (m % 2) * 512, [(1, 128)])
                    P.stt(xf, P.pb(b, 0, 128, 0, [(1, 128)]), sc_k, cst.v(0, 128, C_RENB + h * 128, [(1, 128)]), ALU.mult, ALU.mult)
                    P.tt(r_klo.c(m, 0, 128, w=128), xf, r_kd.c(m, 0, 128), ALU.subtract)
            P.phase = "ret_proj"
            lin_fm("ret_wq", 2, 8, hT, ev_q, kouter=True)
            lin_fm("ret_wk", 2, 8, hT, ev_k)
            lin_tm("ret_wv", 4, r_vt, 2048)
            lin_fm("ret_wg", 4, 8, hT, lambda m, b: P.act(r_rT.c(m, 0, TT), P.pb(b, 0, 128, 0, [(1, TT)]), AF.Silu))
            P.phase = "ret_rec"
            for i, sb_ in enumerate(subs):
                cols = i * SL
                has_state = not sb_["first"]
                if kind == "s":
                    P.dma("sp", S.v(0, 128, 0, [(1024, 4), (512, 2), (1, 512)]),
                          sret_d[sb_["bidx"]].rearrange("h (c d) v -> d h c v", d=128), "misc")
                    has_state = True
                b = P.bank()
                for c in range(8):
                    P.tr(P.pb16(b, 0, SL, c * 128, [(1, 128)]), r_kd.c(c, cols, SL), full(idb, 128))
                for h in range(4):
                    P.act(r_kdt.v(0, SL, h * 256, [(1, 256)]), P.pb16(b, 0, SL, h * 256, [(1, 256)]), AF.Copy,
                          scale=math.exp(RET_LG[h] * SL))
                P.free(b)
                b = P.bank()
                for h in range(4):
                    ex = hilo and i == 0
                    for c in range(2):
                        m = h * 2 + c
                        P.mm(P.pb(b, 0, SL, h * SL, [(1, SL)]), r_kd.c(m, cols, SL), r_qd.c(m, cols, SL), c == 0, c == 1 and not ex)
                        if ex:
                            P.mm(P.pb(b, 0, SL, h * SL, [(1, SL)]), r_kd.c(m, 0, 128), r_qlo.c(m, 0, 128, w=128), False, False)
                            P.mm(P.pb(b, 0, SL, h * SL, [(1, SL)]), r_klo.c(m, 0, 128, w=128), r_qd.c(m, 0, 128), False, c == 1)
                P.tt(r_att.v(0, SL, 0, [(SL, 4), (1, SL)]), P.pb(b, 0, SL, 0, [(SL, 4), (1, SL)]),
                     cst.v(0, SL, C_TRIU, [(0, 4), (1, SL)]), ALU.mult)
                P.free(b)
                bo_of = {}

                def ret_o(h):
                    rSb = (r_Sb, r_Sb2)[h % 2]
                    if has_state:
                        P.cp(full(rSb, 1024), S.v(0, 128, h * 1024, [(1, 1024)]), eng="act")
                    bo = P.bank()
                    o_ap = P.pb(bo, 0, SL, 0, [(1, 512)])
                    P.mm(o_ap, r_att.v(0, SL, h * SL, [(1, SL)]), r_vt.v(0, SL, i * 2048 + h * 512, [(1, 512)]), True, not has_state)
                    if has_state:
                        for c in range(2):
                            P.mm(o_ap, r_qd.c(h * 2 + c, cols, SL), rSb.v(0, 128, c * 512, [(1, 512)]), False, c == 1)
                    bo_of[h] = bo

                def ret_norm(h):
                    bo = bo_of[h]
                    o_ap = P.pb(bo, 0, SL, 0, [(1, 512)])

                    def sc(idx):
                        return tmpf.v(0, SL, 512 + 8 * (h % 2) + idx, [(1, 1)])
                    s1, s2, mean, msq, var, rs = [sc(q) for q in range(6)]
                    ob_ = r_ob.v(0, SL, 0, [(1, 512)])
                    sq_ = r_sq.v(0, SL, 0, [(1, 512)])
                    P.add("act", lambda e: e.activation(ob_, o_ap, AF.Copy, accum_out=s1), [o_ap], [ob_, s1])
                    P.add("act", lambda e: e.activation(sq_, o_ap, AF.Square, accum_out=s2), [o_ap], [sq_, s2])
                    P.ts(mean, s1, 1.0 / 512, None, ALU.mult)
                    P.tt(msq, mean, mean, ALU.mult)
                    P.stt(var, s2, 1.0 / 512, msq, ALU.mult, ALU.subtract)
                    P.act(var, var, AF.Ln, bias=GN_EPS)
                    P.act(rs, var, AF.Exp, scale=-0.5)
                    on = (r_on, r_on2)[h % 2].v(0, SL, 0, [(1, 512)])
                    P.ts(on, o_ap, mean, rs, ALU.subtract, ALU.mult)
                    P.free(bo)

                def ret_rest(h):
                    ron = (r_on, r_on2)[h % 2]
                    bt = P.bank()
                    for jv in range(4):
                        P.tr(P.pb16(bt, 0, 128, jv * SL, [(1, SL)]), ron.v(0, SL, jv * 128, [(1, 128)]), idb.v(0, SL, 0, [(1, SL)]))
                    rt = r_rT.v(0, 128, h * 4 * W + cols, [(W, 4), (1, SL)])
                    P.tt(rt, P.pb16(bt, 0, 128, 0, [(SL, 4), (1, SL)]), rt, ALU.mult)
                    P.free(bt)
                    gL = math.exp(RET_LG[h] * SL)
                    for c in range(2):
                        b = P.bank()
                        P.mm(P.pb(b, 0, 128, 0, [(1, 512)]), r_kdt.v(0, SL, (h * 2 + c) * 128, [(1, 128)]),
                             r_vt.v(0, SL, i * 2048 + h * 512, [(1, 512)]), True, True)
                        Sh = S.v(0, 128, h * 1024 + c * 512, [(1, 512)])
                        if has_state:
                            P.stt(Sh, Sh, gL, P.pb(b, 0, 128, 0, [(1, 512)]), ALU.mult, ALU.add)
                        else:
                            P.cp(Sh, P.pb(b, 0, 128, 0, [(1, 512)]))
                        P.free(b)


                ret_o(0)
                ret_o(1)
                ret_norm(0)
                for h in range(4):
                    if h + 2 < 4:
                        ret_o(h + 2)
                    if h + 1 < 4:
                        ret_norm(h + 1)
                    ret_rest(h)
                if sb_["last"]:
                    od = o_rp if kind == "p" else o_rs
                    P.dma("sp", od[sb_["bidx"]].rearrange("h (c d) v -> d h c v", d=128),
                          S.v(0, 128, 0, [(1024, 4), (512, 2), (1, 512)]), "out")
            P.phase = "ret_wo"
            lin_fm_bigk("ret_wo", 2, r_rT, lambda m, b: resid(l, 0, m, b))

        def swa(l):
            KOFF = 128 if kind == "p" else 0
            last_subs = [i for i, q in enumerate(subs) if q["last"]]
            P.phase = "swa_proj"
            for n in range(2):
                base = next_slab("swa_wqkv")
                for m4 in range(4):
                    m = n * 4 + m4
                    b = P.bank()
                    for k in range(KC):
                        P.mm(P.pb(b, 0, 128, 0, [(1, TT)]), wl(base, k, m4 * 128, 128), hT.c(k, 0, TT), k == 0, k == KC - 1)
                    sqv = sq.v(0, 128, (m % 2) * 512, [(1, TT)])
                    P.act(sqv, P.pb(b, 0, 128, 0, [(1, TT)]), AF.Square)
                    bn = P.bank()
                    P.mm(P.pb(bn, 0, 128, 0, [(1, TT)]), full(blkb, 128), sqv, True, True)
                    rstd_from(bn, TT, 1.0 / 64, NORM_EPS, full(rstd, TT))
                    P.free(bn)
                    P.stt(s_qn.c(m, 0, TT), P.pb(b, 0, 128, 0, [(1, TT)]), qkn.v(0, 128, 0, [(1, 1)]), full(rstd, TT), ALU.mult, ALU.mult)
                    P.free(b)
            mark("swa_q")
            base = next_slab("swa_wqkv")
            for jh in range(4):
                b = P.bank()
                for half in range(2):
                    for k in range(KC):
                        P.mm(P.pb(b, half * 64, 64, 0, [(1, TT)]), wl(base, k, jh * 64, 64), hT.c(k, 0, TT), k == 0, k == KC - 1)
                sqv = sq.v(0, 128, (jh % 2) * 512, [(1, TT)])
                P.act(sqv, P.pb(b, 0, 128, 0, [(1, TT)]), AF.Square)
                bn = P.bank()
                P.mm(P.pb(bn, 0, 128, 0, [(1, TT)]), full(blkb, 128), sqv, True, True)
                rstd_from(bn, TT, 1.0 / 64, NORM_EPS, full(rstd, TT))
                P.free(bn)
                kf = tmpf.v(0, 128, 512, [(1, TT)])
                P.stt(kf, P.pb(b, 0, 128, 0, [(1, TT)]), qkn.v(0, 128, 1, [(1, 1)]), full(rstd, TT), ALU.mult, ALU.mult)
                P.free(b)
                P.act(kTd.v(0, 128, jh * 640 + KOFF, [(1, TT)]), kf, AF.Copy)
                for li, i in enumerate(last_subs):
                    P.cp(s_kfin.v(0, 64, (li * 4 + jh) * 128, [(1, SL)]), tmpf.v(0, 64, 512 + i * SL, [(1, SL)]))
            mark("swa_k")
            for li, i in enumerate(last_subs):
                bt = P.bank()
                for j2 in range(4):
                    P.tr(P.pb(bt, 0, SL, j2 * 64, [(1, 64)]), s_kfin.v(0, 64, (li * 4 + j2) * 128, [(1, SL)]), cst.v(0, 64, C_ID, [(1, 64)]))
                P.act(s_kout.v(0, SL, 0, [(1, 256)]), P.pb(bt, 0, SL, 0, [(1, 256)]), AF.Copy)
                P.free(bt)
                od = o_kp if kind == "p" else o_ks
                P.dma("sp", od[subs[i]["bidx"]], s_kout.v(0, SL, 0, [(1, 256)]), "out")
            mark("swa_kcache")
            for i, sb_ in enumerate(subs):
                b = P.bank()
                for k in range(KC):
                    P.mm(P.pb(b, 0, SL, 0, [(1, 256)]), hT.c(k, i * SL, SL), wl(base, k, 256, 256), k == 0, k == KC - 1)
                slot = (i + 1) if kind == "p" else i
                P.act(vdup.v(0, SL, slot * 512, [(128, 4), (1, 64)]), P.pb(b, 0, SL, 0, [(64, 4), (1, 64)]), AF.Copy)
                P.cp(vdup.v(0, SL, slot * 512 + 64, [(128, 4), (1, 64)]), P.pb(b, 0, SL, 0, [(64, 4), (1, 64)]))
                if sb_["last"]:
                    P.cp(s_vfin.v(0, SL, 0, [(1, 256)]), P.pb(b, 0, SL, 0, [(1, 256)]))
                    od = o_vp if kind == "p" else o_vs
                    P.dma("sp", od[sb_["bidx"]], s_vfin.v(0, SL, 0, [(1, 256)]), "out")
                P.free(b)
            mark("swa_v")
            P.phase = "swa_att"
            def tiles_for(i):
                if kind == "p":
                    tl = []
                    if not subs[i]["first"]:
                        tl.append(("p", i, C_NDPP, 128))
                    tl.append(("o", i, C_NDOP, 128))
                    return tl
                return [("c", i, C_NDPS, 128), ("o", i, C_NDOS, SL)]

            its = [(i, j) for i in range(NSUB) for j in range(4)]

            def bufs(n):
                par = n % 2
                return (s_lg, s_lg2)[par], (s_pT, s_pT2)[par], (s_den, s_den2)[par], (s_rden, s_rden2)[par]

            def stage_a(n):
                i, j = its[n]
                cols = i * SL
                lg_, pT_, den_, rden_ = bufs(n)
                for ti, (tk, _, ndc, SK) in enumerate(tiles_for(i)):
                    bh = [P.bank(), P.bank()]
                    for half in range(2):
                        if tk == "c":
                            kap = ckT.v(half * 64, 64, (i * 4 + j) * 128, [(1, 128)])
                        elif tk == "p":
                            kap = kTd.v(half * 64, 64, j * 640 + i * SL, [(1, SK)])
                        else:
                            kap = kTd.v(half * 64, 64, j * 640 + KOFF + i * SL, [(1, SK)])
                        P.mm(P.pb(bh[half], 0, SK, 0, [(1, 2 * SL)]), kap,
                             s_qn.v(half * 64, 64, (2 * j) * W + cols, [(W, 2), (1, SL)]), True, True)
                    for g in range(4):
                        cb = (g % 2) * 2 + g // 2
                        P.stt(lg_.v(0, SK, ti * 512 + cb * SL, [(1, SL)]), cst.v(0, SK, ndc, [(1, SL)]), 8.0 * SLOPES[4 * j + g],
                              P.pb(bh[g % 2], 0, SK, (g // 2) * SL, [(1, SL)]), ALU.mult, ALU.add)
                    P.free(bh[0])
                    P.free(bh[1])
                    P.act(pT_.v(0, SK, ti * 512, [(1, 4 * SL)]), lg_.v(0, SK, ti * 512, [(1, 4 * SL)]), AF.Exp, scale=0.125)

            def stage_b(n):
                i, j = its[n]
                cols = i * SL
                lg_, pT_, den_, rden_ = bufs(n)
                tiles = tiles_for(i)
                bo = P.bank()
                bd = P.bank()
                for ti, (tk, _, ndc, SK) in enumerate(tiles):
                    if tk == "c":
                        vap = cvd.v(0, 128, i * 512 + j * 128, [(1, 128)])
                    elif tk == "p":
                        vap = vdup.v(0, SK, i * 512 + j * 128, [(1, 128)])
                    else:
                        vap = vdup.v(0, SK, ((i + 1) if kind == "p" else i) * 512 + j * 128, [(1, 128)])
                    P.mm(P.pb(bo, 0, 128, 0, [(1, 4 * SL)]), vap, pT_.v(0, SK, ti * 512, [(1, 4 * SL)]), ti == 0, ti == len(tiles) - 1)
                for ti, (tk, _, ndc, SK) in enumerate(tiles):
                    P.mm(P.pb(bd, 0, 128, 0, [(1, 4 * SL)]), onesb.v(0, SK, 0, [(1, 128)]), pT_.v(0, SK, ti * 512, [(1, 4 * SL)]),
                         ti == 0, ti == len(tiles) - 1)
                for g in range(4):
                    cb = (g % 2) * 2 + g // 2
                    P.act(den_.v(0, 128, cb * SL, [(1, SL)]), P.pb(bd, 0, 128, cb * SL, [(1, SL)]), AF.Ln,
                          bias=esink.v(0, 128, 4 * j + g, [(1, 1)]))
                P.free(bd)
                P.act(full(rden_, 4 * SL), full(den_, 4 * SL), AF.Exp, scale=-1.0)
                for g in range(4):
                    cb = (g % 2) * 2 + g // 2
                    hp = (g % 2) * 64
                    P.tt(s_og.v(hp, 64, (2 * j + g // 2) * W + cols, [(1, SL)]), P.pb(bo, hp, 64, cb * SL, [(1, SL)]),
                         rden_.v(hp, 64, cb * SL, [(1, SL)]), ALU.mult)
                P.free(bo)

            stage_a(0)
            for n in range(len(its)):
                if n + 1 < len(its):
                    stage_a(n + 1)
                stage_b(n)
            mark("swa_att")
            P.phase = "swa_wo"
            lin_fm("swa_wo", 2, 8, s_og, lambda m, b: resid(l, 0, m, b))
            if kind == "p":
                for j in range(4):
                    P.cp(kTd.v(0, 128, j * 640, [(1, 128)]), kTd.v(0, 128, j * 640 + 512, [(1, 128)]), eng="act" if j % 2 else "dve")
                P.cp(vdup.v(0, 128, 0, [(1, 512)]), vdup.v(0, 128, 4 * 512, [(1, 512)]))

        def mlp(l):
            P.phase = "mlp"

            def ev_h(m, b):
                r = m_rl.v(0, 128, (m % 2) * 512, [(1, TT)])
                P.act(r, P.pb(b, 0, 128, 0, [(1, TT)]), AF.Relu)
                P.tt(m_hid.c(m, 0, TT), r, r, ALU.mult)
            lin_fm("mlp_w1", 8, 8, hT, ev_h, kouter=True)
            lin_fm_bigk("mlp_w2", 4, m_hid, lambda m, b: resid(l, 1, m, b))

        for l in range(L_LAYERS):
            norm_mod(l, 0)
            mark(f"norm{l}")
            (gla, ret, swa)[l % 3](l)
            mark(f"mixer{l}")
            norm_mod(l, 1)
            if l == L_LAYERS - 1 and nxt is not None:
                emit_xload(*nxt)
            mlp(l)
            mark(f"mlp{l}")

        P.phase = "ystore"
        for i, sb_ in enumerate(subs):
            for half in range(2):
                b = P.bank()
                for kk in range(4):
                    k = half * 4 + kk
                    P.tr(P.pb(b, 0, SL, kk * 128, [(1, 128)]), xT.c(k, i * SL, SL), cst.v(0, 128, C_ID, [(1, 128)]))
                o = stage.v(0, SL, (i % 2) * 1024 + half * 512, [(1, 512)])
                if half:
                    P.act(o, P.pb(b, 0, SL, 0, [(1, 512)]), AF.Copy)
                else:
                    P.cp(o, P.pb(b, 0, SL, 0, [(1, 512)]))
                P.free(b)
            P.dma("sp", yout[sb_["xrow"]:sb_["xrow"] + SL, :], stage.v(0, SL, (i % 2) * 1024, [(1, 1024)]), ("st", i % 2))

    NT = SEQ // 512
    tiles_ = []
    for p in range(NP):
        for t in range(NT):
            tiles_.append(("p", [dict(slot=p, xrow=p * SEQ + t * 512 + i * 128, first=(t == 0 and i == 0),
                                      last=(t == NT - 1 and i == 3), bidx=p) for i in range(4)]))
    sample_tile = ("s", [dict(slot=NP + i, xrow=i * TS, first=True, last=True, bidx=i) for i in range(NS)])
    for n_, (kd_, sb__) in enumerate(tiles_):
        nx = tiles_[n_ + 1] if n_ + 1 < len(tiles_) else sample_tile
        process_tile(kd_, sb__, prefetched=(n_ > 0), nxt=nx)
    for i in range(NS):
        P.dma("sp", s_kc.v(0, 128, 0, [(1, 256)]), ck_d[i], "misc")
        P.dma("sp", s_vc.v(0, 128, 0, [(1, 256)]), cv_d[i], "misc")
        P.cp(cvd.v(0, 128, i * 512, [(128, 4), (1, 64)]), s_vc.v(0, 128, 0, [(64, 4), (1, 64)]))
        P.cp(cvd.v(0, 128, i * 512 + 64, [(128, 4), (1, 64)]), s_vc.v(0, 128, 0, [(64, 4), (1, 64)]), eng="act")
        for j in range(4):
            P.cp(s_kdp.v(0, 128, 0, [(1, 64)]), s_kc.v(0, 128, j * 64, [(1, 64)]))
            P.cp(s_kdp.v(0, 128, 64, [(1, 64)]), s_kc.v(0, 128, j * 64, [(1, 64)]), eng="act")
            b = P.bank()
            P.tr(P.pb(b, 0, 128, 0, [(1, 128)]), s_kdp.v(0, 128, 0, [(1, 128)]), cst.v(0, 128, C_ID, [(1, 128)]))
            P.cp(ckT.v(0, 128, (i * 4 + j) * 128, [(1, 128)]), P.pb(b, 0, 128, 0, [(1, 128)]))
            P.free(b)
    process_tile("s", sample_tile[1], prefetched=True, nxt=None)
    assert wstate["next"] == TOTAL_SLABS, (wstate, TOTAL_SLABS)


def host_pack(inp, NP, NS, core, n_cores_total=None):
    f = lambda a: np.ascontiguousarray(np.asarray(a, dtype=np.float32))
    d = {}
    SEQ = inp["x_prompt"].shape[1]
    TS = inp["x_sample"].shape[1]
    ps = slice(core * NP, (core + 1) * NP)
    ss = slice(core * NS, (core + 1) * NS)
    d["xp"] = f(inp["x_prompt"][ps]).reshape(NP * SEQ, D)
    d["xs"] = f(inp["x_sample"][ss]).reshape(NS * TS, D)
    c = np.concatenate([f(inp["c_prompt"][ps]), f(inp["c_sample"][ss])], 0)
    if c.shape[0] < 4:
        c = np.concatenate([c, np.zeros((4 - c.shape[0], D), np.float32)], 0)
    d["cT"] = f(c.reshape(4, 8, 128).transpose(2, 1, 0).reshape(128, 32))
    d["sgla0"] = f(inp["state_gla_l0"][ss]); d["sgla3"] = f(inp["state_gla_l3"][ss])
    d["sret"] = f(inp["state_ret_l1"][ss])
    d["ck"] = f(inp["cache_swa_k_l2"][ss]).reshape(NS, 128, 256)
    d["cv"] = f(inp["cache_swa_v_l2"][ss]).reshape(NS, 128, 256)
    return d


def host_shared(inp):
    f = lambda a: np.ascontiguousarray(np.asarray(a, dtype=np.float32))
    d = {}
    blocks = stream_blocks()
    ws = np.empty((len(blocks), 128, SLAB), np.float32)
    wn = {k: np.asarray(inp[k], dtype=np.float32) for k in set(b[0] for b in blocks)}
    for i, (name, j, r0, c0) in enumerate(blocks):
        ws[i] = slab_np(wn[name][j], r0, c0)
    d["wstream"] = ws
    aw = np.asarray(inp["ada_w"], dtype=np.float32)
    d["adaw"] = np.stack([slab_np(aw[l], 0, n * 512) for l in range(L_LAYERS) for n in range(12)], 0)
    ab = np.asarray(inp["ada_b"], dtype=np.float32)
    d["adab"] = f(ab.reshape(L_LAYERS, 48, 128).transpose(2, 0, 1).reshape(128, L_LAYERS * 48))
    nm = np.stack([np.asarray(inp["norm_mix"], np.float32), np.asarray(inp["norm_mlp"], np.float32)], 1)
    d["nrm"] = f(nm.reshape(L_LAYERS, 2, 8, 128).transpose(3, 0, 1, 2).reshape(128, 64))
    gn = np.asarray(inp["gla_norm"], np.float32)
    d["glan"] = f(gn.reshape(2, 2, 128).transpose(2, 0, 1).reshape(128, 4))
    qn = np.asarray(inp["swa_qnorm"], np.float32)[0]; kn = np.asarray(inp["swa_knorm"], np.float32)[0]
    d["qkn"] = f(np.stack([np.concatenate([qn, qn]), np.concatenate([kn, kn])], 1))
    d["sinks"] = f(np.asarray(inp["swa_sinks"], np.float32).reshape(1, 16))
    w1 = np.asarray(inp["gla_wg1"], np.float32)
    d["wg1"] = f(w1.reshape(2, 8, 128, 16).transpose(2, 0, 1, 3).reshape(128, 256))
    w2 = np.asarray(inp["gla_wg2"], np.float32); bg = np.asarray(inp["gla_bg"], np.float32)
    wa = np.zeros((32, 2, 512), np.float32)
    wa[:16] = w2.transpose(1, 0, 2)
    wa[16] = bg
    d["wg2a"] = f(wa.reshape(32, 1024))
    d["cst"] = const_table()
    return d


def run(inp, n_cores, NP, NS):
    SEQ = inp["x_prompt"].shape[1]
    TS = inp["x_sample"].shape[1]
    nc = build(NP, SEQ, NS, TS)
    shared = host_shared(inp)
    in_maps = []
    for c in range(n_cores):
        m = dict(shared)
        m.update(host_pack(inp, NP, NS, c))
        in_maps.append(m)
    res = run_bass_kernel_spmd(nc, in_maps, core_ids=list(range(n_cores)))
    R = res.results
    cat = lambda k: np.concatenate([np.asarray(r[k]) for r in R], 0)
    B = n_cores * NP
    Bs = n_cores * NS
    outs = (
        cat("yp").reshape(B, SEQ, D), cat("ys").reshape(Bs, TS, D),
        cat("o_g0p"), cat("o_g0s"), cat("o_rp"), cat("o_rs"),
        cat("o_kp").reshape(B, 128, 4, 64), cat("o_vp").reshape(B, 128, 4, 64),
        cat("o_ks").reshape(Bs, TS, 4, 64), cat("o_vs").reshape(Bs, TS, 4, 64),
        cat("o_g3p"), cat("o_g3s"),
    )
    return tuple(np.ascontiguousarray(o.astype(np.float32)) for o in outs)


def kernel(**inputs):
    return run(inputs, 8, 2, 2)
```

```python
import math
from contextlib import ExitStack
import numpy as np
import concourse.bass as bass
import concourse.mybir as mybir
from concourse.bass_utils import run_bass_kernel_spmd

F32 = mybir.dt.float32
BF16 = mybir.dt.bfloat16
AF = mybir.ActivationFunctionType
ALU = mybir.AluOpType

D = 1024
KC = 8
L_LAYERS = 4
SB_BASE = 16896
SB_END = 229376
PG = 256
W = 512
NSLOT = 5
SLAB = 4096
NORM_EPS = 1e-6
GN_EPS = 1e-5
RET_LG = [math.log(1.0 - 2.0 ** (-5.0 - h)) for h in range(4)]
SLOPES = [2.0 ** (-8.0 * (h + 1) / 16) for h in range(16)]


class T:
    def __init__(self, handle, pstep):
        self.h = handle
        self.pstep = pstep

    def v(self, p0, npart, off, dims):
        return bass.AP(self.h, p0 * self.pstep + off, [[self.pstep, npart]] + [list(d) for d in dims])

    def c(self, k, lo, n, p0=0, npart=128, w=W):
        return self.v(p0, npart, k * w + lo, [(1, n)])


class Op:
    __slots__ = ("eng", "fn", "waits", "signal", "sig", "dma", "cseq", "ph")


class Prog:
    def __init__(self, nc):
        self.nc = nc
        self.tinfo = {}
        self.sb_ptr = SB_BASE
        self.ops = {e: [] for e in ("pe", "act", "dve", "pool", "sp")}
        self.cops = {e: [] for e in self.ops}
        self.last_w = {}
        self.readers = {}
        self.seen = {e: {} for e in self.ops}
        self.dma_cnt = {}
        self.pcache = {}
        self.phase = "pro"
        self.banks = []
        for i in range(8):
            h = nc.alloc_psum_tensor(f"psb{i}", [128, 512], F32)
            self.tinfo[h.name] = ("ps", i * 2048)
            self.banks.append(T(h, 512))
        self.bank_live = [False] * 8
        self.bank_rr = 0

    def sb(self, name, free, dtype, at=None, parts=128):
        esz = 2 if dtype == BF16 else 4
        nbytes = (free * esz + 63) // 64 * 64
        if at is None:
            at = (self.sb_ptr + PG - 1) // PG * PG
            self.sb_ptr = at + nbytes
        assert at + nbytes <= SB_END, (name, at, nbytes)
        h = self.nc.alloc_sbuf_tensor_at(name, [parts, free], dtype, offset=at)
        self.tinfo[h.name] = ("sb", at)
        return T(h, free)

    def bank(self):
        for i in range(8):
            b = (self.bank_rr + i) % 8
            if not self.bank_live[b]:
                self.bank_live[b] = True
                self.bank_rr = (b + 1) % 8
                return b
        raise RuntimeError("no free psum bank")

    def free(self, b):
        self.bank_live[b] = False

    def pb(self, b, p0, npart, off, dims):
        return self.banks[b].v(p0, npart, off, dims)

    def pb16(self, b, p0, npart, off, dims):
        a = self.banks[b].h.ap().bitcast(BF16)
        return bass.AP(a.tensor, p0 * 1024 + off, [[1024, npart]] + [list(d) for d in dims])

    def pages(self, ap):
        key = (ap.tensor.name, ap.offset, ap.ap, ap.dtype == BF16)
        r = self.pcache.get(key)
        if r is not None:
            return r
        info = self.tinfo.get(ap.tensor.name)
        if info is None:
            self.pcache[key] = ()
            return ()
        space, base = info
        if space == "ps":
            r = ((1 << 20) + base // 2048,)
            self.pcache[key] = r
            return r
        esz = 2 if ap.dtype == BF16 else 4
        dims = ap.ap
        pstep = dims[0][0]
        inner = ap.offset % pstep if pstep else ap.offset
        free = list(dims[1:])
        starts = [inner]
        for st, cnt in free[:-1]:
            if len(starts) * cnt > 512:
                starts = [min(starts), max(starts) + st * (cnt - 1)]
                continue
            starts = [s + st * i for s in starts for i in range(cnt)]
        lst, lcnt = free[-1]
        length = (lcnt - 1) * abs(lst) + 1
        pg = set()
        sp = 0 if space == "sb" else (1 << 20)
        for s in set(starts):
            lo = base + s * esz
            hi = base + (s + length) * esz
            pg.update(range(sp + lo // PG, sp + (hi - 1) // PG + 1))
        r = tuple(pg)
        self.pcache[key] = r
        return r

    def add(self, eng, fn, reads, writes, stream=None, extra=()):
        op = Op()
        op.eng = eng
        op.ph = self.phase
        op.fn = fn
        op.signal = False
        op.sig = None
        op.dma = stream
        deps = {}

        def need(tok, raw):
            k, s = tok
            if stream is None and k == eng and eng == "pe":
                return
            if deps.get(k, 0) < s:
                deps[k] = s

        rp = set()
        for a in reads:
            rp.update(self.pages(a))
        wp = set()
        for a in writes:
            wp.update(self.pages(a))
        for p in rp:
            t = self.last_w.get(p)
            if t:
                need(t, True)
        for p in wp:
            t = self.last_w.get(p)
            if t:
                need(t, False)
            rd = self.readers.get(p)
            if rd:
                for k, s in rd.items():
                    need((k, s), False)
        for tok_ in extra:
            need(tok_, True)
        if stream is not None:
            n = self.dma_cnt.get(stream, 0) + 1
            self.dma_cnt[stream] = n
            tok = (("dma", stream), n)
            if n > 1:
                need((("dma", stream), n - 1), True)
        else:
            self.cops[eng].append(op)
            op.cseq = len(self.cops[eng])
            tok = (eng, op.cseq)
        seen = self.seen[eng]
        waits = []
        for k, s in deps.items():
            if seen.get(k, 0) >= s:
                continue
            seen[k] = s
            waits.append((k, s))
            if not isinstance(k, tuple):
                self.cops[k][s - 1].signal = True
        op.waits = waits
        self.ops[eng].append(op)
        for p in rp:
            rd = self.readers.get(p)
            if rd is None:
                rd = self.readers[p] = {}
            if rd.get(tok[0], 0) < tok[1]:
                rd[tok[0]] = tok[1]
        for p in wp:
            self.last_w[p] = tok
            self.readers[p] = {}
        return op

    def mm(self, out, lhsT, rhs, start, stop):
        self.add("pe", lambda e: e.matmul(out, lhsT, rhs, start=start, stop=stop), [lhsT, rhs], [out])

    def tr(self, out, in_, ident):
        self.add("pe", lambda e: e.transpose(out, in_, ident), [in_, ident], [out])

    def act(self, out, in_, func, scale=None, bias=None):
        kw = {}
        rd = [in_]
        if scale is not None:
            kw["scale"] = scale
            if not isinstance(scale, float):
                rd.append(scale)
        if bias is not None:
            kw["bias"] = bias
            if not isinstance(bias, float):
                rd.append(bias)
        self.add("act", lambda e: e.activation(out, in_, func, **kw), rd, [out])

    def tt(self, out, a, b, op, eng="dve"):
        self.add(eng, lambda e: e.tensor_tensor(out, a, b, op), [a, b], [out])

    def stt(self, out, in0, scalar, in1, op0, op1, eng="dve"):
        rd = [in0, in1]
        if not isinstance(scalar, float):
            rd.append(scalar)
        self.add(eng, lambda e: e.scalar_tensor_tensor(out, in0, scalar, in1, op0, op1), rd, [out])

    def ts(self, out, in0, s1, s2, op0, op1=None, eng="dve"):
        rd = [in0]
        for s in (s1, s2):
            if s is not None and not isinstance(s, float):
                rd.append(s)
        if op1 is None:
            self.add(eng, lambda e: e.tensor_scalar(out, in0, s1, None, op0), rd, [out])
        else:
            self.add(eng, lambda e: e.tensor_scalar(out, in0, s1, s2, op0, op1), rd, [out])

    def cp(self, out, in_, eng="dve"):
        if eng == "act":
            self.add("act", lambda e: e.activation(out, in_, AF.Copy), [in_], [out])
        else:
            self.add(eng, lambda e: e.tensor_copy(out, in_), [in_], [out])

    def recip(self, out, in_):
        self.add("dve", lambda e: e.reciprocal(out, in_), [in_], [out])

    def memset(self, out, val, eng="dve"):
        self.add(eng, lambda e: e.memset(out, val), [], [out])

    def dma(self, eng, out, in_, stream, extra=()):
        self.add(eng, lambda e: e.dma_start(out=out, in_=in_), [in_], [out], stream=stream, extra=extra)

    def finish(self):
        nc = self.nc
        GEN = 60000
        with ExitStack() as es:
            sems = {}
            for e in self.ops:
                n = 0
                for op in self.cops[e]:
                    if op.signal:
                        n += 1
                        op.sig = n
                ng = max(1, (n + GEN - 1) // GEN)
                sems[e] = [es.enter_context(nc.semaphore(f"s_{e}{g}")) for g in range(ng)]
            dsem = {s: es.enter_context(nc.semaphore(f"d_{i}")) for i, s in enumerate(self.dma_cnt)}
            block = es.enter_context(nc.Block())

            def run(ename, e):
                for op in self.ops[ename]:
                    for k, s in op.waits:
                        if isinstance(k, tuple):
                            e.wait_ge(dsem[k[1]], 16 * s)
                        else:
                            c = self.cops[k][s - 1].sig
                            e.wait_ge(sems[k][(c - 1) // GEN], (c - 1) % GEN + 1)
                    ins = op.fn(e)
                    if op.dma is not None:
                        ins.then_inc(dsem[op.dma], 16)
                    elif op.signal:
                        ins.then_inc(sems[ename][(op.sig - 1) // GEN], 1)
                if ename == "sp":
                    for s, n in self.dma_cnt.items():
                        e.wait_ge(dsem[s], 16 * n)

            block.tensor(lambda e: run("pe", e))
            block.scalar(lambda e: run("act", e))
            block.vector(lambda e: run("dve", e))
            block.gpsimd(lambda e: run("pool", e))
            block.sync(lambda e: run("sp", e))


def stream_blocks():
    out = []
    for l in range(L_LAYERS):
        kind, j = l % 3, l // 3
        if kind == 0:
            out += [("gla_wq", j, 0, 0), ("gla_wk", j, 0, 0)]
            out += [("gla_wv", j, 0, c * 512) for c in range(2)]
            out += [("gla_wr", j, 0, c * 512) for c in range(2)]
            out += [("gla_wo", j, 0, c * 512) for c in range(2)]
        elif kind == 1:
            out += [("ret_wq", j, 0, c * 512) for c in range(2)]
            out += [("ret_wk", j, 0, c * 512) for c in range(2)]
            out += [("ret_wv", j, 0, c * 512) for c in range(4)]
            out += [("ret_wg", j, 0, c * 512) for c in range(4)]
            out += [("ret_wo", j, kb * 1024, cb * 512) for cb in range(2) for kb in range(2)]
        else:
            out += [("swa_wqkv", j, 0, c * 512) for c in range(3)]
            out += [("swa_wo", j, 0, c * 512) for c in range(2)]
        out += [("mlp_w1", l, 0, n * 512) for n in range(8)]
        out += [("mlp_w2", l, kb * 1024, cb * 512) for cb in range(2) for kb in range(4)]
    return out


def slab_np(Wm, r0, c0):
    return np.ascontiguousarray(Wm[r0:r0 + 1024, c0:c0 + 512].reshape(8, 128, 512).transpose(1, 0, 2)).reshape(128, 4096)


def const_table():
    s = np.arange(128)[:, None].astype(np.float64)
    t = np.arange(128)[None, :].astype(np.float64)
    ident = np.eye(128)
    tri16 = -(1.0 / 16.0) * (s <= t)
    triu = 1.0 * (s <= t)
    blk = 1.0 * ((s // 64) == (t // 64))
    BIG = -1.0e6
    ndpp = -(t - s + 128.0)
    ndpp = np.where((t >= 64) & (s < 64), BIG, ndpp)
    ndop = -np.abs(t - s)
    ndop = np.where((t < 64) & (s >= 64), BIG, ndop)
    ndps = -(t - s + 128.0)
    ndos = -np.abs(t - s)
    reb = np.concatenate([np.broadcast_to(np.exp(RET_LG[h] * (t + 1.0)), (128, 128)) for h in range(4)], 1)
    renb = np.concatenate([np.broadcast_to(np.exp(-RET_LG[h] * (t + 1.0)), (128, 128)) for h in range(4)], 1)
    return np.concatenate([ident, tri16, triu, blk, ndpp, ndop, ndps, ndos, reb, renb], 1).astype(np.float32)


C_ID, C_TRI16, C_TRIU, C_BLK, C_NDPP, C_NDOP, C_NDPS, C_NDOS, C_REB, C_RENB = [i * 128 for i in range(8)] + [1024, 1536]
NCST = 2048


class StopBuild(Exception):
    pass


def build(NP, SEQ, NS, TS=16):
    import os
    nc = bass.Bass("TRN2", target_bir_lowering=False)
    P = Prog(nc)
    kstop = int(os.environ.get("KSTOP", "0"))
    stg = {"n": 0}

    def mark(name):
        stg["n"] += 1
        if kstop and stg["n"] == kstop:
            print("STOP at stage", stg["n"], name, flush=True)
            raise StopBuild()
    try:
        _build_body(nc, P, NP, SEQ, NS, TS, mark)
    except StopBuild:
        pass
    P.finish()
    return nc


def _build_body(nc, P, NP, SEQ, NS, TS, mark):
    blocks = stream_blocks()
    NBLK = len(blocks)
    NSEQ = NP + NS

    def din(name, shape):
        return nc.dram_tensor(name, list(shape), F32, kind="ExternalInput").ap()

    def dout(name, shape):
        return nc.dram_tensor(name, list(shape), F32, kind="ExternalOutput").ap()

    xp = din("xp", [NP * SEQ, D]); xs = din("xs", [NS * TS, D])
    wstream = din("wstream", [NBLK, 128, SLAB])
    wbf = nc.dram_tensor("wbf", [NBLK, 128, SLAB], BF16, kind="Internal").ap()
    adaw = din("adaw", [L_LAYERS * 12, 128, SLAB])
    cT_d = din("cT", [128, KC * 4]); adab_d = din("adab", [128, L_LAYERS * 48]); nrm_d = din("nrm", [128, 64])
    glan_d = din("glan", [128, 4]); qkn_d = din("qkn", [128, 2]); sinks_d = din("sinks", [1, 16])
    wg1_d = din("wg1", [128, 256]); wg2_d = din("wg2a", [32, 1024]); cst_d = din("cst", [128, NCST])
    sgla0_d = din("sgla0", [NS, 4, 128, 256]); sgla3_d = din("sgla3", [NS, 4, 128, 256])
    sret_d = din("sret", [NS, 4, 256, 512]); ck_d = din("ck", [NS, 128, 256]); cv_d = din("cv", [NS, 128, 256])
    yp = dout("yp", [NP * SEQ, D]); ys = dout("ys", [NS * TS, D])
    o_g0p = dout("o_g0p", [NP, 4, 128, 256]); o_g0s = dout("o_g0s", [NS, 4, 128, 256])
    o_rp = dout("o_rp", [NP, 4, 256, 512]); o_rs = dout("o_rs", [NS, 4, 256, 512])
    o_kp = dout("o_kp", [NP, 128, 256]); o_vp = dout("o_vp", [NP, 128, 256])
    o_ks = dout("o_ks", [NS, TS, 256]); o_vs = dout("o_vs", [NS, TS, 256])
    o_g3p = dout("o_g3p", [NP, 4, 128, 256]); o_g3s = dout("o_g3s", [NS, 4, 128, 256])

    cst = P.sb("cst", NCST, F32)
    idb = P.sb("idb", 128, BF16); onesb = P.sb("onesb", 128, BF16); blkb = P.sb("blkb", 128, BF16)
    modT = P.sb("modT", L_LAYERS * 6 * 8 * 4, F32); gsT = P.sb("gsT", L_LAYERS * 2 * 8 * 4, F32)
    nrm = P.sb("nrm", 64, F32); glan = P.sb("glan", 4, F32); qkn = P.sb("qkn", 2, F32)
    esink = P.sb("esink", 16, F32); csil = P.sb("csil", 32, F32)
    wg1b = P.sb("wg1b", 256, BF16); wg2b = P.sb("wg2b", 1024, BF16); g1a = P.sb("g1a", 512, BF16)
    xT = P.sb("xT", KC * W, F32); hT = P.sb("hT", KC * W, BF16)
    ring = P.sb("ring", NSLOT * SLAB, BF16)
    Sg = [P.sb("Sg0", 1024, F32), P.sb("Sg3", 1024, F32)]
    Sr = P.sb("Sr", 4096, F32)
    kTd = P.sb("kTd", 4 * 640, BF16); vdup = P.sb("vdup", 5 * 512, BF16)
    ckT = P.sb("ckT", NS * 4 * 128, BF16); cvd = P.sb("cvd", NS * 512, BF16)
    sq = P.sb("sq", 2 * 512, BF16); rstd = P.sb("rstd", 512, F32); tmpf = P.sb("tmpf", 2 * 512, F32)
    stage = P.sb("stage", 2 * 1024, F32)
    AR = (P.sb_ptr + PG - 1) // PG * PG
    P.sb_ptr = AR
    ARENA = 66 * 1024
    P.sb_ptr += ARENA
    assert P.sb_ptr <= SB_END, P.sb_ptr

    def ar(name, off, free, dt):
        return P.sb(name, free, dt, at=AR + off)

    g_eb = ar("g_eb", 0, 4 * 512, F32); g_enb = ar("g_enb", 8192, 4 * 512, F32)
    g_qd = ar("g_qd", 16384, 4 * 512, BF16); g_kd = ar("g_kd", 20480, 4 * 512, BF16)
    g_vt = ar("g_vt", 24576, 4 * 1024, BF16); g_rT = ar("g_rT", 32768, 8 * 512, BF16)
    g_gt = ar("g_gt", 40960, 2 * 512, F32); g_kdt = ar("g_kdt", 45056, 2 * 512, BF16)
    g_att = ar("g_att", 47104, 2 * 512, BF16); g_Sb = ar("g_Sb", 49152, 1024, BF16)
    g_ot = ar("g_ot", 51200, 512, F32)
    g_qlo = ar("g_qlo", 53248, 4 * 128, BF16); g_klo = ar("g_klo", 54272, 4 * 128, BF16)
    g_Sb2 = ar("g_Sb2", 55296, 1024, BF16)
    r_qd = ar("r_qd", 0, 8 * 512, BF16); r_kd = ar("r_kd", 8192, 8 * 512, BF16)
    r_vt = ar("r_vt", 16384, 4 * 2048, BF16); r_rT = ar("r_rT", 32768, 16 * 512, BF16)
    r_kdt = ar("r_kdt", 49152, 1024, BF16); r_att = ar("r_att", 51200, 512, BF16)
    r_Sb = ar("r_Sb", 52224, 1024, BF16); r_ob = ar("r_ob", 54272, 512, BF16); r_sq = ar("r_sq", 55296, 512, BF16)
    r_ot = ar("r_ot", 56320, 512, F32)
    r_on = ar("r_on", 56320, 512, BF16); r_on2 = ar("r_on2", 57344, 512, BF16)
    r_qlo = ar("r_qlo", 58368, 8 * 128, BF16); r_klo = ar("r_klo", 60416, 8 * 128, BF16)
    r_Sb2 = ar("r_Sb2", 62464, 1024, BF16)
    r_kdt2 = ar("r_kdt2", 64512, 1024, BF16); r_att2 = ar("r_att2", 66560, 512, BF16)
    s_qn = ar("s_qn", 0, 8 * 512, BF16); s_og = ar("s_og", 8192, 8 * 512, BF16)
    s_lg = ar("s_lg", 16384, 2 * 512, F32); s_pT = ar("s_pT", 20480, 2 * 512, BF16)
    s_kfin = ar("s_kfin", 22528, 8 * 128, F32); s_vfin = ar("s_vfin", 26624, 256, F32)
    s_kout = ar("s_kout", 27648, 256, F32); s_den = ar("s_den", 28672, 512, F32); s_rden = ar("s_rden", 30720, 512, F32)
    s_lg2 = ar("s_lg2", 36864, 2 * 512, F32); s_pT2 = ar("s_pT2", 40960, 2 * 512, BF16)
    s_den2 = ar("s_den2", 43008, 512, F32); s_rden2 = ar("s_rden2", 45056, 512, F32)
    s_kc = ar("s_kc", 32768, 256, F32); s_vc = ar("s_vc", 33792, 256, F32); s_kdp = ar("s_kdp", 34816, 128, F32)
    m_hid = ar("m_hid", 0, 32 * 512, BF16); m_rl = ar("m_rl", 32768, 2 * 512, F32)
    xstage = ar("xstage", 40960, 4 * 1024, F32)
    p_aw = [ar("p_aw0", 0, SLAB, F32), ar("p_aw1", 16384, SLAB, F32)]
    p_ab = ar("p_ab", 32768, L_LAYERS * 48, F32); p_cT = ar("p_cT", 34816, 32, F32)
    p_wg1 = ar("p_wg1", 36864, 256, F32)

    def full(t, n, p0=0, npart=128, off=0):
        return t.v(p0, npart, off, [(1, n)])

    wstate = {"emitted": 0, "next": 0}
    TOTAL_PASSES = NP * (SEQ // 512) + 1
    TOTAL_SLABS = TOTAL_PASSES * NBLK

    def next_slab(expect):
        i = wstate["next"]
        assert blocks[i % NBLK][0] == expect, (blocks[i % NBLK], expect)
        while wstate["emitted"] < min(TOTAL_SLABS, i + NSLOT):
            jj = wstate["emitted"]
            slot = jj % NSLOT
            if jj < NBLK:
                P.dma("pool", full(ring, SLAB, off=slot * SLAB), wstream[jj], ("w", slot))
                P.dma("sp", wbf[jj], full(ring, SLAB, off=slot * SLAB), "wst")
            else:
                ex = [(("dma", "wst"), jj % NBLK + 1)] if jj < 2 * NBLK else []
                P.dma("pool", full(ring, SLAB, off=slot * SLAB), wbf[jj % NBLK], ("w", slot), extra=ex)
            wstate["emitted"] += 1
        wstate["next"] += 1
        return (i % NSLOT) * SLAB

    def wl(base, k, c0, n):
        return ring.v(0, 128, base + k * 512 + c0, [(1, n)])

    P.dma("sp", full(cst, NCST), cst_d, "misc")
    P.dma("sp", full(nrm, 64), nrm_d, "misc")
    P.dma("sp", full(glan, 4), glan_d, "misc")
    P.dma("sp", full(qkn, 2), qkn_d, "misc")
    P.dma("sp", full(p_ab, L_LAYERS * 48), adab_d, "misc")
    P.dma("sp", full(p_cT, 32), cT_d, "misc")
    P.dma("sp", full(esink, 16), bass.AP(sinks_d.tensor, 0, [[0, 128], [1, 16]]), "misc")
    P.dma("pool", full(wg1b, 256), wg1_d, "misc2")
    P.dma("pool", full(wg2b, 1024, npart=32), wg2_d, "misc2")
    P.cp(full(idb, 128), cst.v(0, 128, C_ID, [(1, 128)]))
    P.cp(full(blkb, 128), cst.v(0, 128, C_BLK, [(1, 128)]))
    P.memset(full(onesb, 128), 1.0)
    P.memset(full(g1a, 512, npart=32), 1.0)
    P.act(full(esink, 16), full(esink, 16), AF.Exp)
    P.act(full(csil, 32), full(p_cT, 32), AF.Silu)
    p_mt = ar("p_mt", 38912, 512, F32)
    for l in range(L_LAYERS):
        bt = P.bank()
        for n12 in range(12):
            sidx = l * 12 + n12
            P.dma("sp", full(p_aw[sidx % 2], SLAB), adaw[sidx], ("aw", sidx % 2))
            aw = p_aw[sidx % 2]
            b = P.bank()
            for k in range(KC):
                P.mm(P.pb(b, 0, 4, 0, [(1, 512)]), csil.v(0, 128, k * 4, [(1, 4)]), aw.v(0, 128, k * 512, [(1, 512)]), k == 0, k == KC - 1)
            P.cp(p_mt.v(0, 4, 0, [(1, 512)]), P.pb(b, 0, 4, 0, [(1, 512)]), eng="act" if n12 % 2 else "dve")
            P.free(b)
            for jj4 in range(4):
                j = n12 * 4 + jj4
                P.tr(P.pb(bt, 0, 128, j * 4, [(1, 4)]), p_mt.v(0, 4, jj4 * 128, [(1, 128)]), cst.v(0, 4, C_ID, [(1, 4)]))
        P.tt(modT.v(0, 128, l * 192, [(4, 48), (1, 4)]), P.pb(bt, 0, 128, 0, [(4, 48), (1, 4)]),
             p_ab.v(0, 128, l * 48, [(1, 48), (0, 4)]), ALU.add)
        P.free(bt)

    def modap(l, c, k, s):
        return modT.v(0, 128, ((l * 6 + c) * 8 + k) * 4 + s, [(1, 1)])

    def gsap(l, which, k, s):
        return gsT.v(0, 128, ((l * 2 + which) * 8 + k) * 4 + s, [(1, 1)])

    for l in range(L_LAYERS):
        for which in range(2):
            for k in range(KC):
                P.ts(gsT.v(0, 128, ((l * 2 + which) * 8 + k) * 4, [(1, 4)]),
                     modT.v(0, 128, ((l * 6 + 1 + 3 * which) * 8 + k) * 4, [(1, 4)]),
                     1.0, nrm.v(0, 128, (l * 2 + which) * 8 + k, [(1, 1)]), ALU.add, ALU.mult)

    mark("prologue")

    def emit_xload(kind_, subs_):
        SL_ = 128 if kind_ == "p" else TS
        xin_ = xp if kind_ == "p" else xs
        for i_, q_ in enumerate(subs_):
            P.dma("sp", xstage.v(0, SL_, i_ * 1024, [(1, 1024)]), xin_[q_["xrow"]:q_["xrow"] + SL_, :], ("xl", i_))

    def process_tile(kind, subs, prefetched=False, nxt=None):
        SL = 128 if kind == "p" else TS
        NSUB = len(subs)
        TT = SL * NSUB
        if kind == "p":
            groups = [(0, TT, subs[0]["slot"])]
        else:
            groups = [(i * SL, SL, sb_["slot"]) for i, sb_ in enumerate(subs)]
        xin = xp if kind == "p" else xs
        yout = yp if kind == "p" else ys
        idf = cst.v(0, SL, C_ID, [(1, SL)])

        P.phase = "xload"
        if not prefetched:
            emit_xload(kind, subs)
        for i, sb_ in enumerate(subs):
            for half in range(2):
                b = P.bank()
                for kk in range(4):
                    k = half * 4 + kk
                    P.tr(P.pb(b, 0, 128, kk * SL, [(1, SL)]), xstage.v(0, SL, i * 1024 + k * 128, [(1, 128)]), idf)
                P.cp(xT.v(0, 128, half * 4 * W + i * SL, [(W, 4), (1, SL)]), P.pb(b, 0, 128, 0, [(SL, 4), (1, SL)]),
                     eng="act" if half else "dve")
                P.free(b)
        mark("xload")

        def norm_mod(l, which):
            P.phase = "norm"
            b = P.bank()
            for k in range(KC):
                sqk = sq.v(0, 128, (k % 2) * 512, [(1, TT)])
                P.act(sqk, xT.c(k, 0, TT), AF.Square)
                P.mm(P.pb(b, 0, 128, 0, [(1, TT)]), full(onesb, 128), sqk, k == 0, k == KC - 1)
            P.act(tmpf.v(0, 128, 0, [(1, TT)]), P.pb(b, 0, 128, 0, [(1, TT)]), AF.Ln, scale=1.0 / D, bias=NORM_EPS)
            P.free(b)
            P.act(full(rstd, TT), tmpf.v(0, 128, 0, [(1, TT)]), AF.Exp, scale=-0.5)
            for k in range(KC):
                t = tmpf.v(0, 128, (k % 2) * 512, [(1, TT)])
                P.tt(t, xT.c(k, 0, TT), full(rstd, TT), ALU.mult)
                for (c0, n, s) in groups:
                    P.act(hT.c(k, c0, n), tmpf.v(0, 128, (k % 2) * 512 + c0, [(1, n)]), AF.Identity,
                          scale=gsap(l, which, k, s), bias=modap(l, 3 * which, k, s))

        def resid(l, which, m, b):
            for (c0, n, s) in groups:
                P.stt(xT.c(m, c0, n), P.pb(b, 0, 128, c0, [(1, n)]), modap(l, 2 + 3 * which, m, s), xT.c(m, c0, n),
                      ALU.mult, ALU.add)

        def lin_fm(wname, nslab, kchunks_per_slab, src, evac, kouter=False):
            for n in range(nslab):
                base = next_slab(wname)
                if kouter and n == 0:
                    bs = [P.bank() for _ in range(4)]
                    for k in range(KC):
                        for m4 in range(4):
                            P.mm(P.pb(bs[m4], 0, 128, 0, [(1, TT)]), wl(base, k, m4 * 128, 128), src.c(k, 0, TT), k == 0, k == KC - 1)
                    for m4 in range(4):
                        evac(m4, bs[m4])
                        P.free(bs[m4])
                    continue
                for m4 in range(4):
                    b = P.bank()
                    for k in range(KC):
                        P.mm(P.pb(b, 0, 128, 0, [(1, TT)]), wl(base, k, m4 * 128, 128), src.c(k, 0, TT), k == 0, k == KC - 1)
                    evac(n * 4 + m4, b)
                    P.free(b)

        def lin_fm_bigk(wname, nkb, src, evac):
            for cb in range(2):
                bs = [P.bank() for _ in range(4)]
                for kb in range(nkb):
                    base = next_slab(wname)
                    for m4 in range(4):
                        for k in range(KC):
                            P.mm(P.pb(bs[m4], 0, 128, 0, [(1, TT)]), wl(base, k, m4 * 128, 128), src.c(kb * 8 + k, 0, TT),
                                 kb == 0 and k == 0, kb == nkb - 1 and k == KC - 1)
                for m4 in range(4):
                    evac(cb * 4 + m4, bs[m4])
                    P.free(bs[m4])

        def lin_tm(wname, nslab, dst, dst_w, evac_eng_alt=True):
            for n in range(nslab):
                base = next_slab(wname)
                for i in range(NSUB):
                    b = P.bank()
                    for k in range(KC):
                        P.mm(P.pb(b, 0, SL, 0, [(1, 512)]), hT.c(k, i * SL, SL), wl(base, k, 0, 512), k == 0, k == KC - 1)
                    if (i + n) % 2 == 0:
                        P.act(dst.v(0, SL, i * dst_w + n * 512, [(1, 512)]), P.pb(b, 0, SL, 0, [(1, 512)]), AF.Copy)
                    else:
                        P.cp(dst.v(0, SL, i * dst_w + n * 512, [(1, 512)]), P.pb(b, 0, SL, 0, [(1, 512)]))
                    P.free(b)

        def rstd_from(b, ncols, scale, eps, out):
            P.act(tmpf.v(0, 128, 0, [(1, ncols)]), P.pb(b, 0, 128, 0, [(1, ncols)]), AF.Ln, scale=scale, bias=eps)
            P.act(out, tmpf.v(0, 128, 0, [(1, ncols)]), AF.Exp, scale=-0.5)

        def gla(l):
            j = l // 3
            S = Sg[j]
            sdram = (sgla0_d, sgla3_d)[j]
            odram_p, odram_s = ((o_g0p, o_g0s), (o_g3p, o_g3s))[j]
            P.phase = "gla_gate"
            b = P.bank()
            for k in range(KC):
                P.mm(P.pb(b, 0, 16, 0, [(1, TT)]), wg1b.v(0, 128, (j * 8 + k) * 16, [(1, 16)]), hT.c(k, 0, TT), k == 0, k == KC - 1)
            P.act(g1a.v(0, 16, 0, [(1, TT)]), P.pb(b, 0, 16, 0, [(1, TT)]), AF.Copy)
            P.free(b)
            def gate_l(i):
                b = P.bank()
                P.mm(P.pb(b, 0, SL, 0, [(1, 512)]), g1a.v(0, 32, i * SL, [(1, SL)]), wg2b.v(0, 32, j * 512, [(1, 512)]), True, True)
                gt = g_gt.v(0, SL, (i % 2) * 512, [(1, 512)])
                P.act(gt, P.pb(b, 0, SL, 0, [(1, 512)]), AF.Exp, scale=-1.0)
                P.free(b)
                P.act(gt, gt, AF.Ln, bias=1.0)

            def gate_c(i):
                b = P.bank()
                for h in range(4):
                    P.mm(P.pb(b, 0, 128, h * SL, [(1, SL)]), g_gt.v(0, SL, (i % 2) * 512 + h * 128, [(1, 128)]),
                         cst.v(0, SL, C_TRI16, [(1, SL)]), True, True)
                P.act(g_eb.v(0, 128, i * SL, [(512, 4), (1, SL)]), P.pb(b, 0, 128, 0, [(SL, 4), (1, SL)]), AF.Exp)
                P.act(g_enb.v(0, 128, i * SL, [(512, 4), (1, SL)]), P.pb(b, 0, 128, 0, [(SL, 4), (1, SL)]), AF.Exp, scale=-1.0)
                P.free(b)
            gate_l(0)
            for i in range(NSUB):
                if i + 1 < NSUB:
                    gate_l(i + 1)
                gate_c(i)
            hilo = subs[0]["first"] and kind == "p"
            P.phase = "gla_proj"

            def ev_gq(m, b):
                P.stt(g_qd.c(m, 0, TT), P.pb(b, 0, 128, 0, [(1, TT)]), 128.0 ** -0.5, g_eb.c(m, 0, TT), ALU.mult, ALU.mult)
                if hilo:
                    xf = tmpf.v(0, 128, (m % 2) * 512, [(1, 128)])
                    P.stt(xf, P.pb(b, 0, 128, 0, [(1, 128)]), 128.0 ** -0.5, g_eb.c(m, 0, 128), ALU.mult, ALU.mult)
                    P.tt(g_qlo.c(m, 0, 128, w=128), xf, g_qd.c(m, 0, 128), ALU.subtract)

            def ev_gk(m, b):
                P.tt(g_kd.c(m, 0, TT), P.pb(b, 0, 128, 0, [(1, TT)]), g_enb.c(m, 0, TT), ALU.mult)
                if hilo:
                    xf = tmpf.v(0, 128, (m % 2) * 512, [(1, 128)])
                    P.tt(xf, P.pb(b, 0, 128, 0, [(1, 128)]), g_enb.c(m, 0, 128), ALU.mult)
                    P.tt(g_klo.c(m, 0, 128, w=128), xf, g_kd.c(m, 0, 128), ALU.subtract)
            lin_fm("gla_wq", 1, 8, hT, ev_gq)
            lin_fm("gla_wk", 1, 8, hT, ev_gk)
            lin_tm("gla_wv", 2, g_vt, 1024)

            def ev_r(m, b):
                P.act(tmpf.v(0, 128, (m % 2) * 512, [(1, TT)]), P.pb(b, 0, 128, 0, [(1, TT)]), AF.Silu)
                P.ts(g_rT.c(m, 0, TT), tmpf.v(0, 128, (m % 2) * 512, [(1, TT)]), glan.v(0, 128, j * 2 + m % 2, [(1, 1)]), None, ALU.mult)
            lin_fm("gla_wr", 2, 8, hT, ev_r)
            P.phase = "gla_rec"
            hs_ = [(not q["first"]) or kind == "s" for q in subs]
            bo_of = {}

            def g_T(i):
                sb_ = subs[i]
                cols = i * SL
                if kind == "s":
                    P.dma("sp", S.v(0, 128, 0, [(256, 4), (1, 256)]), sdram[sb_["bidx"]].rearrange("h d v -> d h v"), "misc")
                b = P.bank()
                for h in range(4):
                    P.tr(P.pb16(b, 0, SL, h * 128, [(1, 128)]), g_kd.c(h, cols, SL), full(idb, 128))
                P.act(g_kdt.v(0, SL, (i % 2) * 512, [(1, 512)]), P.pb16(b, 0, SL, 0, [(1, 512)]), AF.Copy)
                P.free(b)
                if hs_[i]:
                    P.act(full((g_Sb, g_Sb2)[i % 2], 1024), full(S, 1024), AF.Copy)

            def g_A(i):
                cols = i * SL
                b = P.bank()
                for h in range(4):
                    ex = hilo and i == 0
                    P.mm(P.pb(b, 0, SL, h * SL, [(1, SL)]), g_kd.c(h, cols, SL), g_qd.c(h, cols, SL), True, not ex)
                    if ex:
                        P.mm(P.pb(b, 0, SL, h * SL, [(1, SL)]), g_kd.c(h, 0, 128), g_qlo.c(h, 0, 128, w=128), False, False)
                        P.mm(P.pb(b, 0, SL, h * SL, [(1, SL)]), g_klo.c(h, 0, 128, w=128), g_qd.c(h, 0, 128), False, True)
                att = g_att.v(0, SL, (i % 2) * 512, [(SL, 4), (1, SL)])
                P.tt(att, P.pb(b, 0, SL, 0, [(SL, 4), (1, SL)]), cst.v(0, SL, C_TRIU, [(0, 4), (1, SL)]), ALU.mult)
                P.free(b)

            def g_U(i):
                sb_ = subs[i]
                cols = i * SL
                for hp in range(2):
                    b = P.bank()
                    for hh in range(2):
                        h = hp * 2 + hh
                        P.mm(P.pb(b, 0, 128, hh * 256, [(1, 256)]), g_kdt.v(0, SL, (i % 2) * 512 + h * 128, [(1, 128)]),
                             g_vt.v(0, SL, i * 1024 + h * 256, [(1, 256)]), True, True)
                    for hh in range(2):
                        h = hp * 2 + hh
                        ebl = g_eb.v(0, 128, h * 512 + cols + SL - 1, [(1, 1)])
                        Sh = S.v(0, 128, h * 256, [(1, 256)])
                        if hs_[i]:
                            P.ts(Sh, Sh, ebl, None, ALU.mult)
                            P.stt(Sh, P.pb(b, 0, 128, hh * 256, [(1, 256)]), ebl, Sh, ALU.mult, ALU.add)
                        else:
                            P.ts(Sh, P.pb(b, 0, 128, hh * 256, [(1, 256)]), ebl, None, ALU.mult)
                    P.free(b)
                if sb_["last"]:
                    od = odram_p if kind == "p" else odram_s
                    P.dma("sp", od[sb_["bidx"]].rearrange("h d v -> d h v"), S.v(0, 128, 0, [(256, 4), (1, 256)]), "out")

            def g_O(i):
                cols = i * SL
                Sb_ = (g_Sb, g_Sb2)[i % 2]
                bo = [P.bank(), P.bank()]
                for h in range(4):
                    for jv in range(2):
                        o_ap = P.pb(bo[h // 2], 0, 128, ((h % 2) * 2 + jv) * SL, [(1, SL)])
                        P.mm(o_ap, g_vt.v(0, SL, i * 1024 + h * 256 + jv * 128, [(1, 128)]),
                             g_att.v(0, SL, (i % 2) * 512 + h * SL, [(1, SL)]), True, not hs_[i])
                        if hs_[i]:
                            P.mm(o_ap, Sb_.v(0, 128, h * 256 + jv * 128, [(1, 128)]), g_qd.c(h, cols, SL), False, True)
                for hp in range(2):
                    P.act(sq.v(0, 128, hp * 512, [(1, 4 * SL)]), P.pb(bo[hp], 0, 128, 0, [(1, 4 * SL)]), AF.Square)
                bo_of[i] = bo

            def g_N(i):
                cols = i * SL
                bo = bo_of[i]
                bn = P.bank()
                for hp in range(2):
                    for hh in range(2):
                        for jv in range(2):
                            P.mm(P.pb(bn, 0, 128, (hp * 2 + hh) * SL, [(1, SL)]), full(onesb, 128),
                                 sq.v(0, 128, hp * 512 + (hh * 2 + jv) * SL, [(1, SL)]), jv == 0, jv == 1)
                rstd_from(bn, 4 * SL, 1.0 / 256, NORM_EPS, full(rstd, 4 * SL))
                P.free(bn)
                for hp in range(2):
                    ot = g_ot.v(0, 128, 0, [(2 * SL, 2), (SL, 2), (1, SL)])
                    P.tt(ot, P.pb(bo[hp], 0, 128, 0, [(2 * SL, 2), (SL, 2), (1, SL)]),
                         rstd.v(0, 128, hp * 2 * SL, [(SL, 2), (0, 2), (1, SL)]), ALU.mult)
                    rt = g_rT.v(0, 128, hp * 4 * W + cols, [(W, 4), (1, SL)])
                    P.tt(rt, g_ot.v(0, 128, 0, [(SL, 4), (1, SL)]), rt, ALU.mult)
                    P.free(bo[hp])

            g_T(0); g_A(0); g_U(0)
            for i in range(NSUB):
                if i + 1 < NSUB:
                    g_T(i + 1); g_A(i + 1)
                g_O(i)
                if i + 1 < NSUB:
                    g_U(i + 1)
                g_N(i)
            P.phase = "gla_wo"
            lin_fm("gla_wo", 2, 8, g_rT, lambda m, b: resid(l, 0, m, b))

        def ret(l):
            S = Sr
            sc_k = 256.0 ** -0.5

            hilo = subs[0]["first"] and kind == "p"

            def ev_q(m, b):
                h = m // 2
                P.tt(r_qd.v(0, 128, m * W, [(SL, NSUB), (1, SL)]), P.pb(b, 0, 128, 0, [(SL, NSUB), (1, SL)]),
                     cst.v(0, 128, C_REB + h * 128, [(0, NSUB), (1, SL)]), ALU.mult)
                if hilo:
                    xf = tmpf.v(0, 128, (m % 2) * 512, [(1, 128)])
                    P.tt(xf, P.pb(b, 0, 128, 0, [(1, 128)]), cst.v(0, 128, C_REB + h * 128, [(1, 128)]), ALU.mult)
                    P.tt(r_qlo.c(m, 0, 128, w=128), xf, r_qd.c(m, 0, 128), ALU.subtract)

            def ev_k(m, b):
                h = m // 2
                P.stt(r_kd.v(0, 128, m * W, [(SL, NSUB), (1, SL)]), P.pb(b, 0, 128, 0, [(SL, NSUB), (1, SL)]), sc_k,
                      cst.v(0, 128, C_RENB + h * 128, [(0, NSUB), (1, SL)]), ALU.mult, ALU.mult)
                if hilo:
                    xf = tmpf.v(0, 128, (m % 2) * 512, [(1, 128)])
                    P.stt(xf, P.pb(b, 0, 128, 0, [(1, 128)]), sc_k, cst.v(0, 128, C_RENB + h * 128, [(1, 128)]), ALU.mult, ALU.mult)
                    P.tt(r_klo.c(m, 0, 128, w=128), xf, r_kd.c(m, 0, 128), ALU.subtract)
            P.phase = "ret_proj"
            lin_fm("ret_wq", 2, 8, hT, ev_q, kouter=True)
            lin_fm("ret_wk", 2, 8, hT, ev_k)
            lin_tm("ret_wv", 4, r_vt, 2048)
            lin_fm("ret_wg", 4, 8, hT, lambda m, b: P.act(r_rT.c(m, 0, TT), P.pb(b, 0, 128, 0, [(1, TT)]), AF.Silu))
            P.phase = "ret_rec"

            def r_pre(i):
                cols = i * SL
                kdt_ = (r_kdt, r_kdt2)[i % 2]
                att_ = (r_att, r_att2)[i % 2]
                b = P.bank()
                for c in range(8):
                    P.tr(P.pb16(b, 0, SL, c * 128, [(1, 128)]), r_kd.c(c, cols, SL), full(idb, 128))
                for h in range(4):
                    P.act(kdt_.v(0, SL, h * 256, [(1, 256)]), P.pb16(b, 0, SL, h * 256, [(1, 256)]), AF.Copy,
                          scale=math.exp(RET_LG[h] * SL))
                P.free(b)
                b = P.bank()
                for h in range(4):
                    ex = hilo and i == 0
                    for c in range(2):
                        m = h * 2 + c
                        P.mm(P.pb(b, 0, SL, h * SL, [(1, SL)]), r_kd.c(m, cols, SL), r_qd.c(m, cols, SL), c == 0, c == 1 and not ex)
                        if ex:
                            P.mm(P.pb(b, 0, SL, h * SL, [(1, SL)]), r_kd.c(m, 0, 128), r_qlo.c(m, 0, 128, w=128), False, False)
                            P.mm(P.pb(b, 0, SL, h * SL, [(1, SL)]), r_klo.c(m, 0, 128, w=128), r_qd.c(m, 0, 128), False, c == 1)
                P.tt(att_.v(0, SL, 0, [(SL, 4), (1, SL)]), P.pb(b, 0, SL, 0, [(SL, 4), (1, SL)]),
                     cst.v(0, SL, C_TRIU, [(0, 4), (1, SL)]), ALU.mult)
                P.free(b)

            r_pre(0)
            for i, sb_ in enumerate(subs):
                cols = i * SL
                has_state = not sb_["first"]
                if kind == "s":
                    P.dma("sp", S.v(0, 128, 0, [(1024, 4), (512, 2), (1, 512)]),
                          sret_d[sb_["bidx"]].rearrange("h (c d) v -> d h c v", d=128), "misc")
                    has_state = True
                if i + 1 < NSUB:
                    r_pre(i + 1)
                kdt_ = (r_kdt, r_kdt2)[i % 2]
                att_ = (r_att, r_att2)[i % 2]
                bo_of = {}

                def ret_o(h):
                    rSb = (r_Sb, r_Sb2)[h % 2]
                    if has_state:
                        P.cp(full(rSb, 1024), S.v(0, 128, h * 1024, [(1, 1024)]), eng="act")
                    bo = P.bank()
                    o_ap = P.pb(bo, 0, SL, 0, [(1, 512)])
                    P.mm(o_ap, att_.v(0, SL, h * SL, [(1, SL)]), r_vt.v(0, SL, i * 2048 + h * 512, [(1, 512)]), True, not has_state)
                    if has_state:
                        for c in range(2):
                            P.mm(o_ap, r_qd.c(h * 2 + c, cols, SL), rSb.v(0, 128, c * 512, [(1, 512)]), False, c == 1)
                    bo_of[h] = bo

                def ret_norm(h):
                    bo = bo_of[h]
                    o_ap = P.pb(bo, 0, SL, 0, [(1, 512)])

                    def sc(idx):
                        return tmpf.v(0, SL, 512 + 8 * (h % 2) + idx, [(1, 1)])
                    s1, s2, mean, msq, var, rs = [sc(q) for q in range(6)]
                    ob_ = r_ob.v(0, SL, 0, [(1, 512)])
                    sq_ = r_sq.v(0, SL, 0, [(1, 512)])
                    P.add("act", lambda e: e.activation(ob_, o_ap, AF.Copy, accum_out=s1), [o_ap], [ob_, s1])
                    P.add("act", lambda e: e.activation(sq_, o_ap, AF.Square, accum_out=s2), [o_ap], [sq_, s2])
                    P.ts(mean, s1, 1.0 / 512, None, ALU.mult)
                    P.tt(msq, mean, mean, ALU.mult)
                    P.stt(var, s2, 1.0 / 512, msq, ALU.mult, ALU.subtract)
                    P.act(var, var, AF.Ln, bias=GN_EPS)
                    P.act(rs, var, AF.Exp, scale=-0.5)
                    on = (r_on, r_on2)[h % 2].v(0, SL, 0, [(1, 512)])
                    P.ts(on, o_ap, mean, rs, ALU.subtract, ALU.mult)
                    P.free(bo)

                def ret_rest(h):
                    ron = (r_on, r_on2)[h % 2]
                    bt = P.bank()
                    for jv in range(4):
                        P.tr(P.pb16(bt, 0, 128, jv * SL, [(1, SL)]), ron.v(0, SL, jv * 128, [(1, 128)]), idb.v(0, SL, 0, [(1, SL)]))
                    rt = r_rT.v(0, 128, h * 4 * W + cols, [(W, 4), (1, SL)])
                    P.tt(rt, P.pb16(bt, 0, 128, 0, [(SL, 4), (1, SL)]), rt, ALU.mult)
                    P.free(bt)
                    gL = math.exp(RET_LG[h] * SL)
                    for c in range(2):
                        b = P.bank()
                        P.mm(P.pb(b, 0, 128, 0, [(1, 512)]), kdt_.v(0, SL, (h * 2 + c) * 128, [(1, 128)]),
                             r_vt.v(0, SL, i * 2048 + h * 512, [(1, 512)]), True, True)
                        Sh = S.v(0, 128, h * 1024 + c * 512, [(1, 512)])
                        if has_state:
                            P.stt(Sh, Sh, gL, P.pb(b, 0, 128, 0, [(1, 512)]), ALU.mult, ALU.add)
                        else:
                            P.cp(Sh, P.pb(b, 0, 128, 0, [(1, 512)]))
                        P.free(b)


                ret_o(0)
                ret_o(1)
                ret_norm(0)
                for h in range(4):
                    if h + 2 < 4:
                        ret_o(h + 2)
                    if h + 1 < 4:
                        ret_norm(h + 1)
                    ret_rest(h)
                if sb_["last"]:
                    od = o_rp if kind == "p" else o_rs
                    P.dma("sp", od[sb_["bidx"]].rearrange("h (c d) v -> d h c v", d=128),
                          S.v(0, 128, 0, [(1024, 4), (512, 2), (1, 512)]), "out")
            P.phase = "ret_wo"
            lin_fm_bigk("ret_wo", 2, r_rT, lambda m, b: resid(l, 0, m, b))

        def swa(l):
            KOFF = 128 if kind == "p" else 0
            last_subs = [i for i, q in enumerate(subs) if q["last"]]
            P.phase = "swa_proj"
            for n in range(2):
                base = next_slab("swa_wqkv")
                for m4 in range(4):
                    m = n * 4 + m4
                    b = P.bank()
                    for k in range(KC):
                        P.mm(P.pb(b, 0, 128, 0, [(1, TT)]), wl(base, k, m4 * 128, 128), hT.c(k, 0, TT), k == 0, k == KC - 1)
                    sqv = sq.v(0, 128, (m % 2) * 512, [(1, TT)])
                    P.act(sqv, P.pb(b, 0, 128, 0, [(1, TT)]), AF.Square)
                    bn = P.bank()
                    P.mm(P.pb(bn, 0, 128, 0, [(1, TT)]), full(blkb, 128), sqv, True, True)
                    rstd_from(bn, TT, 1.0 / 64, NORM_EPS, full(rstd, TT))
                    P.free(bn)
                    P.stt(s_qn.c(m, 0, TT), P.pb(b, 0, 128, 0, [(1, TT)]), qkn.v(0, 128, 0, [(1, 1)]), full(rstd, TT), ALU.mult, ALU.mult)
                    P.free(b)
            mark("swa_q")
            base = next_slab("swa_wqkv")
            for jh in range(4):
                b = P.bank()
                for half in range(2):
                    for k in range(KC):
                        P.mm(P.pb(b, half * 64, 64, 0, [(1, TT)]), wl(base, k, jh * 64, 64), hT.c(k, 0, TT), k == 0, k == KC - 1)
                sqv = sq.v(0, 128, (jh % 2) * 512, [(1, TT)])
                P.act(sqv, P.pb(b, 0, 128, 0, [(1, TT)]), AF.Square)
                bn = P.bank()
                P.mm(P.pb(bn, 0, 128, 0, [(1, TT)]), full(blkb, 128), sqv, True, True)
                rstd_from(bn, TT, 1.0 / 64, NORM_EPS, full(rstd, TT))
                P.free(bn)
                kf = tmpf.v(0, 128, 512, [(1, TT)])
                P.stt(kf, P.pb(b, 0, 128, 0, [(1, TT)]), qkn.v(0, 128, 1, [(1, 1)]), full(rstd, TT), ALU.mult, ALU.mult)
                P.free(b)
                P.act(kTd.v(0, 128, jh * 640 + KOFF, [(1, TT)]), kf, AF.Copy)
                for li, i in enumerate(last_subs):
                    P.cp(s_kfin.v(0, 64, (li * 4 + jh) * 128, [(1, SL)]), tmpf.v(0, 64, 512 + i * SL, [(1, SL)]))
            mark("swa_k")
            for li, i in enumerate(last_subs):
                bt = P.bank()
                for j2 in range(4):
                    P.tr(P.pb(bt, 0, SL, j2 * 64, [(1, 64)]), s_kfin.v(0, 64, (li * 4 + j2) * 128, [(1, SL)]), cst.v(0, 64, C_ID, [(1, 64)]))
                P.act(s_kout.v(0, SL, 0, [(1, 256)]), P.pb(bt, 0, SL, 0, [(1, 256)]), AF.Copy)
                P.free(bt)
                od = o_kp if kind == "p" else o_ks
                P.dma("sp", od[subs[i]["bidx"]], s_kout.v(0, SL, 0, [(1, 256)]), "out")
            mark("swa_kcache")
            for i, sb_ in enumerate(subs):
                b = P.bank()
                for k in range(KC):
                    P.mm(P.pb(b, 0, SL, 0, [(1, 256)]), hT.c(k, i * SL, SL), wl(base, k, 256, 256), k == 0, k == KC - 1)
                slot = (i + 1) if kind == "p" else i
                P.act(vdup.v(0, SL, slot * 512, [(128, 4), (1, 64)]), P.pb(b, 0, SL, 0, [(64, 4), (1, 64)]), AF.Copy)
                P.cp(vdup.v(0, SL, slot * 512 + 64, [(128, 4), (1, 64)]), P.pb(b, 0, SL, 0, [(64, 4), (1, 64)]))
                if sb_["last"]:
                    P.cp(s_vfin.v(0, SL, 0, [(1, 256)]), P.pb(b, 0, SL, 0, [(1, 256)]))
                    od = o_vp if kind == "p" else o_vs
                    P.dma("sp", od[sb_["bidx"]], s_vfin.v(0, SL, 0, [(1, 256)]), "out")
                P.free(b)
            mark("swa_v")
            P.phase = "swa_att"
            def tiles_for(i):
                if kind == "p":
                    tl = []
                    if not subs[i]["first"]:
                        tl.append(("p", i, C_NDPP, 128))
                    tl.append(("o", i, C_NDOP, 128))
                    return tl
                return [("c", i, C_NDPS, 128), ("o", i, C_NDOS, SL)]

            its = [(i, j) for i in range(NSUB) for j in range(4)]

            def bufs(n):
                par = n % 2
                return (s_lg, s_lg2)[par], (s_pT, s_pT2)[par], (s_den, s_den2)[par], (s_rden, s_rden2)[par]

            def stage_a(n):
                i, j = its[n]
                cols = i * SL
                lg_, pT_, den_, rden_ = bufs(n)
                for ti, (tk, _, ndc, SK) in enumerate(tiles_for(i)):
                    bh = [P.bank(), P.bank()]
                    for half in range(2):
                        if tk == "c":
                            kap = ckT.v(half * 64, 64, (i * 4 + j) * 128, [(1, 128)])
                        elif tk == "p":
                            kap = kTd.v(half * 64, 64, j * 640 + i * SL, [(1, SK)])
                        else:
                            kap = kTd.v(half * 64, 64, j * 640 + KOFF + i * SL, [(1, SK)])
                        P.mm(P.pb(bh[half], 0, SK, 0, [(1, 2 * SL)]), kap,
                             s_qn.v(half * 64, 64, (2 * j) * W + cols, [(W, 2), (1, SL)]), True, True)
                    for g in range(4):
                        cb = (g % 2) * 2 + g // 2
                        P.stt(lg_.v(0, SK, ti * 512 + cb * SL, [(1, SL)]), cst.v(0, SK, ndc, [(1, SL)]), 8.0 * SLOPES[4 * j + g],
                              P.pb(bh[g % 2], 0, SK, (g // 2) * SL, [(1, SL)]), ALU.mult, ALU.add)
                    P.free(bh[0])
                    P.free(bh[1])
                    P.act(pT_.v(0, SK, ti * 512, [(1, 4 * SL)]), lg_.v(0, SK, ti * 512, [(1, 4 * SL)]), AF.Exp, scale=0.125)

            def stage_b(n):
                i, j = its[n]
                cols = i * SL
                lg_, pT_, den_, rden_ = bufs(n)
                tiles = tiles_for(i)
                bo = P.bank()
                bd = P.bank()
                for ti, (tk, _, ndc, SK) in enumerate(tiles):
                    if tk == "c":
                        vap = cvd.v(0, 128, i * 512 + j * 128, [(1, 128)])
                    elif tk == "p":
                        vap = vdup.v(0, SK, i * 512 + j * 128, [(1, 128)])
                    else:
                        vap = vdup.v(0, SK, ((i + 1) if kind == "p" else i) * 512 + j * 128, [(1, 128)])
                    P.mm(P.pb(bo, 0, 128, 0, [(1, 4 * SL)]), vap, pT_.v(0, SK, ti * 512, [(1, 4 * SL)]), ti == 0, ti == len(tiles) - 1)
                for ti, (tk, _, ndc, SK) in enumerate(tiles):
                    P.mm(P.pb(bd, 0, 128, 0, [(1, 4 * SL)]), onesb.v(0, SK, 0, [(1, 128)]), pT_.v(0, SK, ti * 512, [(1, 4 * SL)]),
                         ti == 0, ti == len(tiles) - 1)
                for g in range(4):
                    cb = (g % 2) * 2 + g // 2
                    P.act(den_.v(0, 128, cb * SL, [(1, SL)]), P.pb(bd, 0, 128, cb * SL, [(1, SL)]), AF.Ln,
                          bias=esink.v(0, 128, 4 * j + g, [(1, 1)]))
                P.free(bd)
                P.act(full(rden_, 4 * SL), full(den_, 4 * SL), AF.Exp, scale=-1.0)
                for g in range(4):
                    cb = (g % 2) * 2 + g // 2
                    hp = (g % 2) * 64
                    P.tt(s_og.v(hp, 64, (2 * j + g // 2) * W + cols, [(1, SL)]), P.pb(bo, hp, 64, cb * SL, [(1, SL)]),
                         rden_.v(hp, 64, cb * SL, [(1, SL)]), ALU.mult)
                P.free(bo)

            stage_a(0)
            for n in range(len(its)):
                if n + 1 < len(its):
                    stage_a(n + 1)
                stage_b(n)
            mark("swa_att")
            P.phase = "swa_wo"
            lin_fm("swa_wo", 2, 8, s_og, lambda m, b: resid(l, 0, m, b))
            if kind == "p":
                for j in range(4):
                    P.cp(kTd.v(0, 128, j * 640, [(1, 128)]), kTd.v(0, 128, j * 640 + 512, [(1, 128)]), eng="act" if j % 2 else "dve")
                P.cp(vdup.v(0, 128, 0, [(1, 512)]), vdup.v(0, 128, 4 * 512, [(1, 512)]))

        def mlp(l):
            P.phase = "mlp"

            def ev_h(m, b):
                r = m_rl.v(0, 128, (m % 2) * 512, [(1, TT)])
                P.act(r, P.pb(b, 0, 128, 0, [(1, TT)]), AF.Relu)
                P.tt(m_hid.c(m, 0, TT), r, r, ALU.mult)
            lin_fm("mlp_w1", 8, 8, hT, ev_h, kouter=True)
            lin_fm_bigk("mlp_w2", 4, m_hid, lambda m, b: resid(l, 1, m, b))

        for l in range(L_LAYERS):
            norm_mod(l, 0)
            mark(f"norm{l}")
            (gla, ret, swa)[l % 3](l)
            mark(f"mixer{l}")
            norm_mod(l, 1)
            if l == L_LAYERS - 1 and nxt is not None:
                emit_xload(*nxt)
            mlp(l)
            mark(f"mlp{l}")

        P.phase = "ystore"
        for i, sb_ in enumerate(subs):
            for half in range(2):
                b = P.bank()
                for kk in range(4):
                    k = half * 4 + kk
                    P.tr(P.pb(b, 0, SL, kk * 128, [(1, 128)]), xT.c(k, i * SL, SL), cst.v(0, 128, C_ID, [(1, 128)]))
                o = stage.v(0, SL, (i % 2) * 1024 + half * 512, [(1, 512)])
                if half:
                    P.act(o, P.pb(b, 0, SL, 0, [(1, 512)]), AF.Copy)
                else:
                    P.cp(o, P.pb(b, 0, SL, 0, [(1, 512)]))
                P.free(b)
            P.dma("sp", yout[sb_["xrow"]:sb_["xrow"] + SL, :], stage.v(0, SL, (i % 2) * 1024, [(1, 1024)]), ("st", i % 2))

    NT = SEQ // 512
    tiles_ = []
    for p in range(NP):
        for t in range(NT):
            tiles_.append(("p", [dict(slot=p, xrow=p * SEQ + t * 512 + i * 128, first=(t == 0 and i == 0),
                                      last=(t == NT - 1 and i == 3), bidx=p) for i in range(4)]))
    sample_tile = ("s", [dict(slot=NP + i, xrow=i * TS, first=True, last=True, bidx=i) for i in range(NS)])
    for n_, (kd_, sb__) in enumerate(tiles_):
        nx = tiles_[n_ + 1] if n_ + 1 < len(tiles_) else sample_tile
        process_tile(kd_, sb__, prefetched=(n_ > 0), nxt=nx)
    for i in range(NS):
        P.dma("sp", s_kc.v(0, 128, 0, [(1, 256)]), ck_d[i], "misc")
        P.dma("sp", s_vc.v(0, 128, 0, [(1, 256)]), cv_d[i], "misc")
        P.cp(cvd.v(0, 128, i * 512, [(128, 4), (1, 64)]), s_vc.v(0, 128, 0, [(64, 4), (1, 64)]))
        P.cp(cvd.v(0, 128, i * 512 + 64, [(128, 4), (1, 64)]), s_vc.v(0, 128, 0, [(64, 4), (1, 64)]), eng="act")
        for j in range(4):
            P.cp(s_kdp.v(0, 128, 0, [(1, 64)]), s_kc.v(0, 128, j * 64, [(1, 64)]))
            P.cp(s_kdp.v(0, 128, 64, [(1, 64)]), s_kc.v(0, 128, j * 64, [(1, 64)]), eng="act")
            b = P.bank()
            P.tr(P.pb(b, 0, 128, 0, [(1, 128)]), s_kdp.v(0, 128, 0, [(1, 128)]), cst.v(0, 128, C_ID, [(1, 128)]))
            P.cp(ckT.v(0, 128, (i * 4 + j) * 128, [(1, 128)]), P.pb(b, 0, 128, 0, [(1, 128)]))
            P.free(b)
    process_tile("s", sample_tile[1], prefetched=True, nxt=None)
    assert wstate["next"] == TOTAL_SLABS, (wstate, TOTAL_SLABS)


def host_pack(inp, NP, NS, core, n_cores_total=None):
    f = lambda a: np.ascontiguousarray(np.asarray(a, dtype=np.float32))
    d = {}
    SEQ = inp["x_prompt"].shape[1]
    TS = inp["x_sample"].shape[1]
    ps = slice(core * NP, (core + 1) * NP)
    ss = slice(core * NS, (core + 1) * NS)
    d["xp"] = f(inp["x_prompt"][ps]).reshape(NP * SEQ, D)
    d["xs"] = f(inp["x_sample"][ss]).reshape(NS * TS, D)
    c = np.concatenate([f(inp["c_prompt"][ps]), f(inp["c_sample"][ss])], 0)
    if c.shape[0] < 4:
        c = np.concatenate([c, np.zeros((4 - c.shape[0], D), np.float32)], 0)
    d["cT"] = f(c.reshape(4, 8, 128).transpose(2, 1, 0).reshape(128, 32))
    d["sgla0"] = f(inp["state_gla_l0"][ss]); d["sgla3"] = f(inp["state_gla_l3"][ss])
    d["sret"] = f(inp["state_ret_l1"][ss])
    d["ck"] = f(inp["cache_swa_k_l2"][ss]).reshape(NS, 128, 256)
    d["cv"] = f(inp["cache_swa_v_l2"][ss]).reshape(NS, 128, 256)
    return d


def host_shared(inp):
    f = lambda a: np.ascontiguousarray(np.asarray(a, dtype=np.float32))
    d = {}
    blocks = stream_blocks()
    ws = np.empty((len(blocks), 128, SLAB), np.float32)
    wn = {k: np.asarray(inp[k], dtype=np.float32) for k in set(b[0] for b in blocks)}
    for i, (name, j, r0, c0) in enumerate(blocks):
        ws[i] = slab_np(wn[name][j], r0, c0)
    d["wstream"] = ws
    aw = np.asarray(inp["ada_w"], dtype=np.float32)
    d["adaw"] = np.stack([slab_np(aw[l], 0, n * 512) for l in range(L_LAYERS) for n in range(12)], 0)
    ab = np.asarray(inp["ada_b"], dtype=np.float32)
    d["adab"] = f(ab.reshape(L_LAYERS, 48, 128).transpose(2, 0, 1).reshape(128, L_LAYERS * 48))
    nm = np.stack([np.asarray(inp["norm_mix"], np.float32), np.asarray(inp["norm_mlp"], np.float32)], 1)
    d["nrm"] = f(nm.reshape(L_LAYERS, 2, 8, 128).transpose(3, 0, 1, 2).reshape(128, 64))
    gn = np.asarray(inp["gla_norm"], np.float32)
    d["glan"] = f(gn.reshape(2, 2, 128).transpose(2, 0, 1).reshape(128, 4))
    qn = np.asarray(inp["swa_qnorm"], np.float32)[0]; kn = np.asarray(inp["swa_knorm"], np.float32)[0]
    d["qkn"] = f(np.stack([np.concatenate([qn, qn]), np.concatenate([kn, kn])], 1))
    d["sinks"] = f(np.asarray(inp["swa_sinks"], np.float32).reshape(1, 16))
    w1 = np.asarray(inp["gla_wg1"], np.float32)
    d["wg1"] = f(w1.reshape(2, 8, 128, 16).transpose(2, 0, 1, 3).reshape(128, 256))
    w2 = np.asarray(inp["gla_wg2"], np.float32); bg = np.asarray(inp["gla_bg"], np.float32)
    wa = np.zeros((32, 2, 512), np.float32)
    wa[:16] = w2.transpose(1, 0, 2)
    wa[16] = bg
    d["wg2a"] = f(wa.reshape(32, 1024))
    d["cst"] = const_table()
    return d


def run(inp, n_cores, NP, NS):
    SEQ = inp["x_prompt"].shape[1]
    TS = inp["x_sample"].shape[1]
    nc = build(NP, SEQ, NS, TS)
    shared = host_shared(inp)
    in_maps = []
    for c in range(n_cores):
        m = dict(shared)
        m.update(host_pack(inp, NP, NS, c))
        in_maps.append(m)
    res = run_bass_kernel_spmd(nc, in_maps, core_ids=list(range(n_cores)))
    R = res.results
    cat = lambda k: np.concatenate([np.asarray(r[k]) for r in R], 0)
    B = n_cores * NP
    Bs = n_cores * NS
    outs = (
        cat("yp").reshape(B, SEQ, D), cat("ys").reshape(Bs, TS, D),
        cat("o_g0p"), cat("o_g0s"), cat("o_rp"), cat("o_rs"),
        cat("o_kp").reshape(B, 128, 4, 64), cat("o_vp").reshape(B, 128, 4, 64),
        cat("o_ks").reshape(Bs, TS, 4, 64), cat("o_vs").reshape(Bs, TS, 4, 64),
        cat("o_g3p"), cat("o_g3s"),
    )
    return tuple(np.ascontiguousarray(o.astype(np.float32)) for o in outs)


def kernel(**inputs):
    return run(inputs, 8, 2, 2)
```

```python
import math
from contextlib import ExitStack
import numpy as np
import concourse.bass as bass
import concourse.mybir as mybir
from concourse.bass_utils import run_bass_kernel_spmd

F32 = mybir.dt.float32
BF16 = mybir.dt.bfloat16
AF = mybir.ActivationFunctionType
ALU = mybir.AluOpType

D = 1024
KC = 8
L_LAYERS = 4
SB_BASE = 16896
SB_END = 229376
PG = 256
W = 512
NSLOT = 6
SLAB = 4096
NORM_EPS = 1e-6
GN_EPS = 1e-5
RET_LG = [math.log(1.0 - 2.0 ** (-5.0 - h)) for h in range(4)]
SLOPES = [2.0 ** (-8.0 * (h + 1) / 16) for h in range(16)]


class T:
    def __init__(self, handle, pstep):
        self.h = handle
        self.pstep = pstep

    def v(self, p0, npart, off, dims):
        return bass.AP(self.h, p0 * self.pstep + off, [[self.pstep, npart]] + [list(d) for d in dims])

    def c(self, k, lo, n, p0=0, npart=128, w=W):
        return self.v(p0, npart, k * w + lo, [(1, n)])


class Op:
    __slots__ = ("eng", "fn", "waits", "signal", "sig", "dma", "cseq", "ph")


class Prog:
    def __init__(self, nc):
        self.nc = nc
        self.tinfo = {}
        self.sb_ptr = SB_BASE
        self.ops = {e: [] for e in ("pe", "act", "dve", "pool", "sp")}
        self.cops = {e: [] for e in self.ops}
        self.last_w = {}
        self.readers = {}
        self.seen = {e: {} for e in self.ops}
        self.dma_cnt = {}
        self.pcache = {}
        self.phase = "pro"
        self.banks = []
        for i in range(8):
            h = nc.alloc_psum_tensor(f"psb{i}", [128, 512], F32)
            self.tinfo[h.name] = ("ps", i * 2048)
            self.banks.append(T(h, 512))
        self.bank_live = [False] * 8
        self.bank_rr = 0

    def sb(self, name, free, dtype, at=None, parts=128):
        esz = 2 if dtype == BF16 else 4
        nbytes = (free * esz + 63) // 64 * 64
        if at is None:
            at = (self.sb_ptr + PG - 1) // PG * PG
            self.sb_ptr = at + nbytes
        assert at + nbytes <= SB_END, (name, at, nbytes)
        h = self.nc.alloc_sbuf_tensor_at(name, [parts, free], dtype, offset=at)
        self.tinfo[h.name] = ("sb", at)
        return T(h, free)

    def bank(self):
        for i in range(8):
            b = (self.bank_rr + i) % 8
            if not self.bank_live[b]:
                self.bank_live[b] = True
                self.bank_rr = (b + 1) % 8
                return b
        raise RuntimeError("no free psum bank")

    def free(self, b):
        self.bank_live[b] = False

    def pb(self, b, p0, npart, off, dims):
        return self.banks[b].v(p0, npart, off, dims)

    def pb16(self, b, p0, npart, off, dims):
        a = self.banks[b].h.ap().bitcast(BF16)
        return bass.AP(a.tensor, p0 * 1024 + off, [[1024, npart]] + [list(d) for d in dims])

    def pages(self, ap):
        key = (ap.tensor.name, ap.offset, ap.ap, ap.dtype == BF16)
        r = self.pcache.get(key)
        if r is not None:
            return r
        info = self.tinfo.get(ap.tensor.name)
        if info is None:
            self.pcache[key] = ()
            return ()
        space, base = info
        if space == "ps":
            r = ((1 << 20) + base // 2048,)
            self.pcache[key] = r
            return r
        esz = 2 if ap.dtype == BF16 else 4
        dims = ap.ap
        pstep = dims[0][0]
        inner = ap.offset % pstep if pstep else ap.offset
        free = list(dims[1:])
        starts = [inner]
        for st, cnt in free[:-1]:
            if len(starts) * cnt > 512:
                starts = [min(starts), max(starts) + st * (cnt - 1)]
                continue
            starts = [s + st * i for s in starts for i in range(cnt)]
        lst, lcnt = free[-1]
        length = (lcnt - 1) * abs(lst) + 1
        pg = set()
        sp = 0 if space == "sb" else (1 << 20)
        for s in set(starts):
            lo = base + s * esz
            hi = base + (s + length) * esz
            pg.update(range(sp + lo // PG, sp + (hi - 1) // PG + 1))
        r = tuple(pg)
        self.pcache[key] = r
        return r

    def add(self, eng, fn, reads, writes, stream=None, extra=()):
        op = Op()
        op.eng = eng
        op.ph = self.phase
        op.fn = fn
        op.signal = False
        op.sig = None
        op.dma = stream
        deps = {}

        def need(tok, raw):
            k, s = tok
            if stream is None and k == eng and eng == "pe":
                return
            if deps.get(k, 0) < s:
                deps[k] = s

        rp = set()
        for a in reads:
            rp.update(self.pages(a))
        wp = set()
        for a in writes:
            wp.update(self.pages(a))
        for p in rp:
            t = self.last_w.get(p)
            if t:
                need(t, True)
        for p in wp:
            t = self.last_w.get(p)
            if t:
                need(t, False)
            rd = self.readers.get(p)
            if rd:
                for k, s in rd.items():
                    need((k, s), False)
        for tok_ in extra:
            need(tok_, True)
        if stream is not None:
            n = self.dma_cnt.get(stream, 0) + 1
            self.dma_cnt[stream] = n
            tok = (("dma", stream), n)
            if n > 1:
                need((("dma", stream), n - 1), True)
        else:
            self.cops[eng].append(op)
            op.cseq = len(self.cops[eng])
            tok = (eng, op.cseq)
        seen = self.seen[eng]
        waits = []
        for k, s in deps.items():
            if seen.get(k, 0) >= s:
                continue
            seen[k] = s
            waits.append((k, s))
            if not isinstance(k, tuple):
                self.cops[k][s - 1].signal = True
        op.waits = waits
        self.ops[eng].append(op)
        for p in rp:
            rd = self.readers.get(p)
            if rd is None:
                rd = self.readers[p] = {}
            if rd.get(tok[0], 0) < tok[1]:
                rd[tok[0]] = tok[1]
        for p in wp:
            self.last_w[p] = tok
            self.readers[p] = {}
        return op

    def mm(self, out, lhsT, rhs, start, stop):
        self.add("pe", lambda e: e.matmul(out, lhsT, rhs, start=start, stop=stop), [lhsT, rhs], [out])

    def tr(self, out, in_, ident):
        self.add("pe", lambda e: e.transpose(out, in_, ident), [in_, ident], [out])

    def act(self, out, in_, func, scale=None, bias=None):
        kw = {}
        rd = [in_]
        if scale is not None:
            kw["scale"] = scale
            if not isinstance(scale, float):
                rd.append(scale)
        if bias is not None:
            kw["bias"] = bias
            if not isinstance(bias, float):
                rd.append(bias)
        self.add("act", lambda e: e.activation(out, in_, func, **kw), rd, [out])

    def tt(self, out, a, b, op, eng="dve"):
        self.add(eng, lambda e: e.tensor_tensor(out, a, b, op), [a, b], [out])

    def stt(self, out, in0, scalar, in1, op0, op1, eng="dve"):
        rd = [in0, in1]
        if not isinstance(scalar, float):
            rd.append(scalar)
        self.add(eng, lambda e: e.scalar_tensor_tensor(out, in0, scalar, in1, op0, op1), rd, [out])

    def ts(self, out, in0, s1, s2, op0, op1=None, eng="dve"):
        rd = [in0]
        for s in (s1, s2):
            if s is not None and not isinstance(s, float):
                rd.append(s)
        if op1 is None:
            self.add(eng, lambda e: e.tensor_scalar(out, in0, s1, None, op0), rd, [out])
        else:
            self.add(eng, lambda e: e.tensor_scalar(out, in0, s1, s2, op0, op1), rd, [out])

    def cp(self, out, in_, eng="dve"):
        if eng == "act":
            self.add("act", lambda e: e.activation(out, in_, AF.Copy), [in_], [out])
        else:
            self.add(eng, lambda e: e.tensor_copy(out, in_), [in_], [out])

    def recip(self, out, in_):
        self.add("dve", lambda e: e.reciprocal(out, in_), [in_], [out])

    def memset(self, out, val, eng="dve"):
        self.add(eng, lambda e: e.memset(out, val), [], [out])

    def dma(self, eng, out, in_, stream, extra=()):
        self.add(eng, lambda e: e.dma_start(out=out, in_=in_), [in_], [out], stream=stream, extra=extra)

    def finish(self):
        nc = self.nc
        GEN = 60000
        with ExitStack() as es:
            sems = {}
            for e in self.ops:
                n = 0
                for op in self.cops[e]:
                    if op.signal:
                        n += 1
                        op.sig = n
                ng = max(1, (n + GEN - 1) // GEN)
                sems[e] = [es.enter_context(nc.semaphore(f"s_{e}{g}")) for g in range(ng)]
            dsem = {s: es.enter_context(nc.semaphore(f"d_{i}")) for i, s in enumerate(self.dma_cnt)}
            block = es.enter_context(nc.Block())

            def run(ename, e):
                for op in self.ops[ename]:
                    for k, s in op.waits:
                        if isinstance(k, tuple):
                            e.wait_ge(dsem[k[1]], 16 * s)
                        else:
                            c = self.cops[k][s - 1].sig
                            e.wait_ge(sems[k][(c - 1) // GEN], (c - 1) % GEN + 1)
                    ins = op.fn(e)
                    if op.dma is not None:
                        ins.then_inc(dsem[op.dma], 16)
                    elif op.signal:
                        ins.then_inc(sems[ename][(op.sig - 1) // GEN], 1)
                if ename == "sp":
                    for s, n in self.dma_cnt.items():
                        e.wait_ge(dsem[s], 16 * n)

            block.tensor(lambda e: run("pe", e))
            block.scalar(lambda e: run("act", e))
            block.vector(lambda e: run("dve", e))
            block.gpsimd(lambda e: run("pool", e))
            block.sync(lambda e: run("sp", e))


def stream_blocks():
    out = []
    for l in range(L_LAYERS):
        kind, j = l % 3, l // 3
        if kind == 0:
            out += [("gla_wq", j, 0, 0), ("gla_wk", j, 0, 0)]
            out += [("gla_wv", j, 0, c * 512) for c in range(2)]
            out += [("gla_wr", j, 0, c * 512) for c in range(2)]
            out += [("gla_wo", j, 0, c * 512) for c in range(2)]
        elif kind == 1:
            out += [("ret_wq", j, 0, c * 512) for c in range(2)]
            out += [("ret_wk", j, 0, c * 512) for c in range(2)]
            out += [("ret_wv", j, 0, c * 512) for c in range(4)]
            out += [("ret_wg", j, 0, c * 512) for c in range(4)]
            out += [("ret_wo", j, kb * 1024, cb * 512) for cb in range(2) for kb in range(2)]
        else:
            out += [("swa_wqkv", j, 0, c * 512) for c in range(3)]
            out += [("swa_wo", j, 0, c * 512) for c in range(2)]
        out += [("mlp_w1", l, 0, n * 512) for n in range(8)]
        out += [("mlp_w2", l, kb * 1024, cb * 512) for cb in range(2) for kb in range(4)]
    return out


def slab_np(Wm, r0, c0):
    return np.ascontiguousarray(Wm[r0:r0 + 1024, c0:c0 + 512].reshape(8, 128, 512).transpose(1, 0, 2)).reshape(128, 4096)


def const_table():
    s = np.arange(128)[:, None].astype(np.float64)
    t = np.arange(128)[None, :].astype(np.float64)
    ident = np.eye(128)
    tri16 = -(1.0 / 16.0) * (s <= t)
    triu = 1.0 * (s <= t)
    blk = 1.0 * ((s // 64) == (t // 64))
    BIG = -1.0e6
    ndpp = -(t - s + 128.0)
    ndpp = np.where((t >= 64) & (s < 64), BIG, ndpp)
    ndop = -np.abs(t - s)
    ndop = np.where((t < 64) & (s >= 64), BIG, ndop)
    ndps = -(t - s + 128.0)
    ndos = -np.abs(t - s)
    reb = np.concatenate([np.broadcast_to(np.exp(RET_LG[h] * (t + 1.0)), (128, 128)) for h in range(4)], 1)
    renb = np.concatenate([np.broadcast_to(np.exp(-RET_LG[h] * (t + 1.0)), (128, 128)) for h in range(4)], 1)
    return np.concatenate([ident, tri16, triu, blk, ndpp, ndop, ndps, ndos, reb, renb], 1).astype(np.float32)


C_ID, C_TRI16, C_TRIU, C_BLK, C_NDPP, C_NDOP, C_NDPS, C_NDOS, C_REB, C_RENB = [i * 128 for i in range(8)] + [1024, 1536]
NCST = 2048


class StopBuild(Exception):
    pass


def build(NP, SEQ, NS, TS=16):
    import os
    nc = bass.Bass("TRN2", target_bir_lowering=False)
    P = Prog(nc)
    kstop = int(os.environ.get("KSTOP", "0"))
    stg = {"n": 0}

    def mark(name):
        stg["n"] += 1
        if kstop and stg["n"] == kstop:
            print("STOP at stage", stg["n"], name, flush=True)
            raise StopBuild()
    try:
        _build_body(nc, P, NP, SEQ, NS, TS, mark)
    except StopBuild:
        pass
    P.finish()
    return nc


def _build_body(nc, P, NP, SEQ, NS, TS, mark):
    blocks = stream_blocks()
    NBLK = len(blocks)
    NSEQ = NP + NS

    def din(name, shape):
        return nc.dram_tensor(name, list(shape), F32, kind="ExternalInput").ap()

    def dout(name, shape):
        return nc.dram_tensor(name, list(shape), F32, kind="ExternalOutput").ap()

    xp = din("xp", [NP * SEQ, D]); xs = din("xs", [NS * TS, D])
    wstream = din("wstream", [NBLK, 128, SLAB])
    wbf = nc.dram_tensor("wbf", [NBLK, 128, SLAB], BF16, kind="Internal").ap()
    adaw = din("adaw", [L_LAYERS * 12, 128, SLAB])
    cT_d = din("cT", [128, KC * 4]); adab_d = din("adab", [128, L_LAYERS * 48]); nrm_d = din("nrm", [128, 64])
    glan_d = din("glan", [128, 4]); qkn_d = din("qkn", [128, 2]); sinks_d = din("sinks", [1, 16])
    wg1_d = din("wg1", [128, 256]); wg2_d = din("wg2a", [32, 1024]); cst_d = din("cst", [128, NCST])
    sgla0_d = din("sgla0", [NS, 4, 128, 256]); sgla3_d = din("sgla3", [NS, 4, 128, 256])
    sret_d = din("sret", [NS, 4, 256, 512]); ck_d = din("ck", [NS, 128, 256]); cv_d = din("cv", [NS, 128, 256])
    yp = dout("yp", [NP * SEQ, D]); ys = dout("ys", [NS * TS, D])
    o_g0p = dout("o_g0p", [NP, 4, 128, 256]); o_g0s = dout("o_g0s", [NS, 4, 128, 256])
    o_rp = dout("o_rp", [NP, 4, 256, 512]); o_rs = dout("o_rs", [NS, 4, 256, 512])
    o_kp = dout("o_kp", [NP, 128, 256]); o_vp = dout("o_vp", [NP, 128, 256])
    o_ks = dout("o_ks", [NS, TS, 256]); o_vs = dout("o_vs", [NS, TS, 256])
    o_g3p = dout("o_g3p", [NP, 4, 128, 256]); o_g3s = dout("o_g3s", [NS, 4, 128, 256])

    cst = P.sb("cst", NCST, F32)
    idb = P.sb("idb", 128, BF16); onesb = P.sb("onesb", 128, BF16); blkb = P.sb("blkb", 128, BF16)
    modT = P.sb("modT", L_LAYERS * 6 * 8 * 4, F32); gsT = P.sb("gsT", L_LAYERS * 2 * 8 * 4, F32)
    nrm = P.sb("nrm", 64, F32); glan = P.sb("glan", 4, F32); qkn = P.sb("qkn", 2, F32)
    esink = P.sb("esink", 16, F32); csil = P.sb("csil", 32, F32)
    wg1b = P.sb("wg1b", 256, BF16); wg2b = P.sb("wg2b", 1024, BF16); g1a = P.sb("g1a", 512, BF16)
    xT = P.sb("xT", KC * W, F32); hT = P.sb("hT", KC * W, BF16)
    ring = P.sb("ring", NSLOT * SLAB, BF16)
    Sg = [P.sb("Sg0", 1024, F32), P.sb("Sg3", 1024, F32)]
    Sr = P.sb("Sr", 4096, F32)
    kTd = P.sb("kTd", 4 * 640, BF16); vdup = P.sb("vdup", 5 * 512, BF16)
    ckT = P.sb("ckT", NS * 4 * 128, BF16); cvd = P.sb("cvd", NS * 512, BF16)
    sq = P.sb("sq", 2 * 512, BF16); rstd = P.sb("rstd", 512, F32); tmpf = P.sb("tmpf", 2 * 512, F32)
    AR = (P.sb_ptr + PG - 1) // PG * PG
    P.sb_ptr = AR
    ARENA = 66 * 1024
    P.sb_ptr += ARENA
    assert P.sb_ptr <= SB_END, P.sb_ptr

    def ar(name, off, free, dt):
        return P.sb(name, free, dt, at=AR + off)

    g_eb = ar("g_eb", 0, 4 * 512, F32); g_enb = ar("g_enb", 8192, 4 * 512, F32)
    g_qd = ar("g_qd", 16384, 4 * 512, BF16); g_kd = ar("g_kd", 20480, 4 * 512, BF16)
    g_vt = ar("g_vt", 24576, 4 * 1024, BF16); g_rT = ar("g_rT", 32768, 8 * 512, BF16)
    g_gt = ar("g_gt", 40960, 2 * 512, F32); g_kdt = ar("g_kdt", 45056, 2 * 512, BF16)
    g_att = ar("g_att", 47104, 2 * 512, BF16); g_Sb = ar("g_Sb", 49152, 1024, BF16)
    g_ot = ar("g_ot", 51200, 512, F32)
    g_qlo = ar("g_qlo", 53248, 4 * 128, BF16); g_klo = ar("g_klo", 54272, 4 * 128, BF16)
    g_Sb2 = ar("g_Sb2", 55296, 1024, BF16)
    r_qd = ar("r_qd", 0, 8 * 512, BF16); r_kd = ar("r_kd", 8192, 8 * 512, BF16)
    r_vt = ar("r_vt", 16384, 4 * 2048, BF16); r_rT = ar("r_rT", 32768, 16 * 512, BF16)
    r_kdt = ar("r_kdt", 49152, 1024, BF16); r_att = ar("r_att", 51200, 512, BF16)
    r_Sb = ar("r_Sb", 52224, 1024, BF16); r_ob = ar("r_ob", 54272, 512, BF16); r_sq = ar("r_sq", 55296, 512, BF16)
    r_ot = ar("r_ot", 56320, 512, F32)
    r_on = ar("r_on", 56320, 512, BF16); r_on2 = ar("r_on2", 57344, 512, BF16)
    r_qlo = ar("r_qlo", 58368, 8 * 128, BF16); r_klo = ar("r_klo", 60416, 8 * 128, BF16)
    r_Sb2 = ar("r_Sb2", 62464, 1024, BF16)
    r_kdt2 = ar("r_kdt2", 64512, 1024, BF16); r_att2 = ar("r_att2", 66560, 512, BF16)
    s_qn = ar("s_qn", 0, 8 * 512, BF16); s_og = ar("s_og", 8192, 8 * 512, BF16)
    s_lg = ar("s_lg", 16384, 2 * 512, F32); s_pT = ar("s_pT", 20480, 2 * 512, BF16)
    s_kfin = ar("s_kfin", 22528, 8 * 128, F32); s_vfin = ar("s_vfin", 26624, 256, F32)
    s_kout = ar("s_kout", 27648, 256, F32); s_den = ar("s_den", 28672, 512, F32); s_rden = ar("s_rden", 30720, 512, F32)
    s_lg2 = ar("s_lg2", 36864, 2 * 512, F32); s_pT2 = ar("s_pT2", 40960, 2 * 512, BF16)
    s_den2 = ar("s_den2", 43008, 512, F32); s_rden2 = ar("s_rden2", 45056, 512, F32)
    s_kc = ar("s_kc", 32768, 256, F32); s_vc = ar("s_vc", 33792, 256, F32); s_kdp = ar("s_kdp", 34816, 128, F32)
    stage = ar("stage", 0, 2 * 1024, F32)
    m_hid = ar("m_hid", 0, 32 * 512, BF16); m_rl = ar("m_rl", 32768, 2 * 512, F32)
    xstage = ar("xstage", 40960, 4 * 1024, F32)
    p_aw = [ar("p_aw0", 0, SLAB, F32), ar("p_aw1", 16384, SLAB, F32)]
    p_ab = ar("p_ab", 32768, L_LAYERS * 48, F32); p_cT = ar("p_cT", 34816, 32, F32)
    p_wg1 = ar("p_wg1", 36864, 256, F32)

    def full(t, n, p0=0, npart=128, off=0):
        return t.v(p0, npart, off, [(1, n)])

    wstate = {"emitted": 0, "next": 0}
    TOTAL_PASSES = NP * (SEQ // 512) + 1
    TOTAL_SLABS = TOTAL_PASSES * NBLK

    def next_slab(expect):
        i = wstate["next"]
        assert blocks[i % NBLK][0] == expect, (blocks[i % NBLK], expect)
        while wstate["emitted"] < min(TOTAL_SLABS, i + NSLOT):
            jj = wstate["emitted"]
            slot = jj % NSLOT
            if jj < NBLK:
                P.dma("pool", full(ring, SLAB, off=slot * SLAB), wstream[jj], ("w", slot))
                P.dma("sp", wbf[jj], full(ring, SLAB, off=slot * SLAB), "wst")
            else:
                ex = [(("dma", "wst"), jj % NBLK + 1)] if jj < 2 * NBLK else []
                P.dma("pool", full(ring, SLAB, off=slot * SLAB), wbf[jj % NBLK], ("w", slot), extra=ex)
            wstate["emitted"] += 1
        wstate["next"] += 1
        return (i % NSLOT) * SLAB

    def wl(base, k, c0, n):
        return ring.v(0, 128, base + k * 512 + c0, [(1, n)])

    P.dma("sp", full(cst, NCST), cst_d, "misc")
    P.dma("sp", full(nrm, 64), nrm_d, "misc")
    P.dma("sp", full(glan, 4), glan_d, "misc")
    P.dma("sp", full(qkn, 2), qkn_d, "misc")
    P.dma("sp", full(p_ab, L_LAYERS * 48), adab_d, "misc")
    P.dma("sp", full(p_cT, 32), cT_d, "misc")
    P.dma("sp", full(esink, 16), bass.AP(sinks_d.tensor, 0, [[0, 128], [1, 16]]), "misc")
    P.dma("pool", full(wg1b, 256), wg1_d, "misc2")
    P.dma("pool", full(wg2b, 1024, npart=32), wg2_d, "misc2")
    P.cp(full(idb, 128), cst.v(0, 128, C_ID, [(1, 128)]))
    P.cp(full(blkb, 128), cst.v(0, 128, C_BLK, [(1, 128)]))
    P.memset(full(onesb, 128), 1.0)
    P.memset(full(g1a, 512, npart=32), 1.0)
    P.act(full(esink, 16), full(esink, 16), AF.Exp)
    P.act(full(csil, 32), full(p_cT, 32), AF.Silu)
    p_mt = ar("p_mt", 38912, 512, F32)
    for l in range(L_LAYERS):
        bt = P.bank()
        for n12 in range(12):
            sidx = l * 12 + n12
            P.dma("sp", full(p_aw[sidx % 2], SLAB), adaw[sidx], ("aw", sidx % 2))
            aw = p_aw[sidx % 2]
            b = P.bank()
            for k in range(KC):
                P.mm(P.pb(b, 0, 4, 0, [(1, 512)]), csil.v(0, 128, k * 4, [(1, 4)]), aw.v(0, 128, k * 512, [(1, 512)]), k == 0, k == KC - 1)
            P.cp(p_mt.v(0, 4, 0, [(1, 512)]), P.pb(b, 0, 4, 0, [(1, 512)]), eng="act" if n12 % 2 else "dve")
            P.free(b)
            for jj4 in range(4):
                j = n12 * 4 + jj4
                P.tr(P.pb(bt, 0, 128, j * 4, [(1, 4)]), p_mt.v(0, 4, jj4 * 128, [(1, 128)]), cst.v(0, 4, C_ID, [(1, 4)]))
        P.tt(modT.v(0, 128, l * 192, [(4, 48), (1, 4)]), P.pb(bt, 0, 128, 0, [(4, 48), (1, 4)]),
             p_ab.v(0, 128, l * 48, [(1, 48), (0, 4)]), ALU.add)
        P.free(bt)

    def modap(l, c, k, s):
        return modT.v(0, 128, ((l * 6 + c) * 8 + k) * 4 + s, [(1, 1)])

    def gsap(l, which, k, s):
        return gsT.v(0, 128, ((l * 2 + which) * 8 + k) * 4 + s, [(1, 1)])

    for l in range(L_LAYERS):
        for which in range(2):
            for k in range(KC):
                P.ts(gsT.v(0, 128, ((l * 2 + which) * 8 + k) * 4, [(1, 4)]),
                     modT.v(0, 128, ((l * 6 + 1 + 3 * which) * 8 + k) * 4, [(1, 4)]),
                     1.0, nrm.v(0, 128, (l * 2 + which) * 8 + k, [(1, 1)]), ALU.add, ALU.mult)

    mark("prologue")

    def emit_xload(kind_, subs_):
        SL_ = 128 if kind_ == "p" else TS
        xin_ = xp if kind_ == "p" else xs
        for i_, q_ in enumerate(subs_):
            P.dma("sp", xstage.v(0, SL_, i_ * 1024, [(1, 1024)]), xin_[q_["xrow"]:q_["xrow"] + SL_, :], ("xl", i_))

    def process_tile(kind, subs, prefetched=False, nxt=None):
        SL = 128 if kind == "p" else TS
        NSUB = len(subs)
        TT = SL * NSUB
        if kind == "p":
            groups = [(0, TT, subs[0]["slot"])]
        else:
            groups = [(i * SL, SL, sb_["slot"]) for i, sb_ in enumerate(subs)]
        xin = xp if kind == "p" else xs
        yout = yp if kind == "p" else ys
        idf = cst.v(0, SL, C_ID, [(1, SL)])

        P.phase = "xload"
        if not prefetched:
            emit_xload(kind, subs)
        for i, sb_ in enumerate(subs):
            for half in range(2):
                b = P.bank()
                for kk in range(4):
                    k = half * 4 + kk
                    P.tr(P.pb(b, 0, 128, kk * SL, [(1, SL)]), xstage.v(0, SL, i * 1024 + k * 128, [(1, 128)]), idf)
                P.cp(xT.v(0, 128, half * 4 * W + i * SL, [(W, 4), (1, SL)]), P.pb(b, 0, 128, 0, [(SL, 4), (1, SL)]),
                     eng="act" if half else "dve")
                P.free(b)
        mark("xload")

        def norm_mod(l, which):
            P.phase = "norm"
            b = P.bank()
            for k in range(KC):
                sqk = sq.v(0, 128, (k % 2) * 512, [(1, TT)])
                P.act(sqk, xT.c(k, 0, TT), AF.Square)
                P.mm(P.pb(b, 0, 128, 0, [(1, TT)]), full(onesb, 128), sqk, k == 0, k == KC - 1)
            P.act(tmpf.v(0, 128, 0, [(1, TT)]), P.pb(b, 0, 128, 0, [(1, TT)]), AF.Ln, scale=1.0 / D, bias=NORM_EPS)
            P.free(b)
            P.act(full(rstd, TT), tmpf.v(0, 128, 0, [(1, TT)]), AF.Exp, scale=-0.5)
            for k in range(KC):
                t = tmpf.v(0, 128, (k % 2) * 512, [(1, TT)])
                P.tt(t, xT.c(k, 0, TT), full(rstd, TT), ALU.mult)
                for (c0, n, s) in groups:
                    P.act(hT.c(k, c0, n), tmpf.v(0, 128, (k % 2) * 512 + c0, [(1, n)]), AF.Identity,
                          scale=gsap(l, which, k, s), bias=modap(l, 3 * which, k, s))

        def resid(l, which, m, b):
            for (c0, n, s) in groups:
                P.stt(xT.c(m, c0, n), P.pb(b, 0, 128, c0, [(1, n)]), modap(l, 2 + 3 * which, m, s), xT.c(m, c0, n),
                      ALU.mult, ALU.add)

        def lin_fm(wname, nslab, kchunks_per_slab, src, evac, kouter=False):
            for n in range(nslab):
                base = next_slab(wname)
                if kouter and n == 0:
                    bs = [P.bank() for _ in range(4)]
                    for k in range(KC):
                        for m4 in range(4):
                            P.mm(P.pb(bs[m4], 0, 128, 0, [(1, TT)]), wl(base, k, m4 * 128, 128), src.c(k, 0, TT), k == 0, k == KC - 1)
                    for m4 in range(4):
                        evac(m4, bs[m4])
                        P.free(bs[m4])
                    continue
                for m4 in range(4):
                    b = P.bank()
                    for k in range(KC):
                        P.mm(P.pb(b, 0, 128, 0, [(1, TT)]), wl(base, k, m4 * 128, 128), src.c(k, 0, TT), k == 0, k == KC - 1)
                    evac(n * 4 + m4, b)
                    P.free(b)

        def lin_fm_bigk(wname, nkb, src, evac):
            for cb in range(2):
                bs = [P.bank() for _ in range(4)]
                for kb in range(nkb):
                    base = next_slab(wname)
                    for m4 in range(4):
                        for k in range(KC):
                            P.mm(P.pb(bs[m4], 0, 128, 0, [(1, TT)]), wl(base, k, m4 * 128, 128), src.c(kb * 8 + k, 0, TT),
                                 kb == 0 and k == 0, kb == nkb - 1 and k == KC - 1)
                for m4 in range(4):
                    evac(cb * 4 + m4, bs[m4])
                    P.free(bs[m4])

        def lin_tm(wname, nslab, dst, dst_w, evac_eng_alt=True):
            for n in range(nslab):
                base = next_slab(wname)
                for i in range(NSUB):
                    b = P.bank()
                    for k in range(KC):
                        P.mm(P.pb(b, 0, SL, 0, [(1, 512)]), hT.c(k, i * SL, SL), wl(base, k, 0, 512), k == 0, k == KC - 1)
                    if (i + n) % 2 == 0:
                        P.act(dst.v(0, SL, i * dst_w + n * 512, [(1, 512)]), P.pb(b, 0, SL, 0, [(1, 512)]), AF.Copy)
                    else:
                        P.cp(dst.v(0, SL, i * dst_w + n * 512, [(1, 512)]), P.pb(b, 0, SL, 0, [(1, 512)]))
                    P.free(b)

        def rstd_from(b, ncols, scale, eps, out):
            P.act(tmpf.v(0, 128, 0, [(1, ncols)]), P.pb(b, 0, 128, 0, [(1, ncols)]), AF.Ln, scale=scale, bias=eps)
            P.act(out, tmpf.v(0, 128, 0, [(1, ncols)]), AF.Exp, scale=-0.5)

        def gla(l):
            j = l // 3
            S = Sg[j]
            sdram = (sgla0_d, sgla3_d)[j]
            odram_p, odram_s = ((o_g0p, o_g0s), (o_g3p, o_g3s))[j]
            P.phase = "gla_gate"
            b = P.bank()
            for k in range(KC):
                P.mm(P.pb(b, 0, 16, 0, [(1, TT)]), wg1b.v(0, 128, (j * 8 + k) * 16, [(1, 16)]), hT.c(k, 0, TT), k == 0, k == KC - 1)
            P.act(g1a.v(0, 16, 0, [(1, TT)]), P.pb(b, 0, 16, 0, [(1, TT)]), AF.Copy)
            P.free(b)
            def gate_l(i):
                b = P.bank()
                P.mm(P.pb(b, 0, SL, 0, [(1, 512)]), g1a.v(0, 32, i * SL, [(1, SL)]), wg2b.v(0, 32, j * 512, [(1, 512)]), True, True)
                gt = g_gt.v(0, SL, (i % 2) * 512, [(1, 512)])
                P.act(gt, P.pb(b, 0, SL, 0, [(1, 512)]), AF.Exp, scale=-1.0)
                P.free(b)
                P.act(gt, gt, AF.Ln, bias=1.0)

            def gate_c(i):
                b = P.bank()
                for h in range(4):
                    P.mm(P.pb(b, 0, 128, h * SL, [(1, SL)]), g_gt.v(0, SL, (i % 2) * 512 + h * 128, [(1, 128)]),
                         cst.v(0, SL, C_TRI16, [(1, SL)]), True, True)
                P.act(g_eb.v(0, 128, i * SL, [(512, 4), (1, SL)]), P.pb(b, 0, 128, 0, [(SL, 4), (1, SL)]), AF.Exp)
                P.act(g_enb.v(0, 128, i * SL, [(512, 4), (1, SL)]), P.pb(b, 0, 128, 0, [(SL, 4), (1, SL)]), AF.Exp, scale=-1.0)
                P.free(b)
            gate_l(0)
            for i in range(NSUB):
                if i + 1 < NSUB:
                    gate_l(i + 1)
                gate_c(i)
            hilo = subs[0]["first"] and kind == "p"
            P.phase = "gla_proj"

            def ev_gq(m, b):
                P.stt(g_qd.c(m, 0, TT), P.pb(b, 0, 128, 0, [(1, TT)]), 128.0 ** -0.5, g_eb.c(m, 0, TT), ALU.mult, ALU.mult)
                if hilo:
                    xf = tmpf.v(0, 128, (m % 2) * 512, [(1, 128)])
                    P.stt(xf, P.pb(b, 0, 128, 0, [(1, 128)]), 128.0 ** -0.5, g_eb.c(m, 0, 128), ALU.mult, ALU.mult)
                    P.tt(g_qlo.c(m, 0, 128, w=128), xf, g_qd.c(m, 0, 128), ALU.subtract)

            def ev_gk(m, b):
                P.tt(g_kd.c(m, 0, TT), P.pb(b, 0, 128, 0, [(1, TT)]), g_enb.c(m, 0, TT), ALU.mult)
                if hilo:
                    xf = tmpf.v(0, 128, (m % 2) * 512, [(1, 128)])
                    P.tt(xf, P.pb(b, 0, 128, 0, [(1, 128)]), g_enb.c(m, 0, 128), ALU.mult)
                    P.tt(g_klo.c(m, 0, 128, w=128), xf, g_kd.c(m, 0, 128), ALU.subtract)
            lin_fm("gla_wq", 1, 8, hT, ev_gq)
            lin_fm("gla_wk", 1, 8, hT, ev_gk)
            lin_tm("gla_wv", 2, g_vt, 1024)

            def ev_r(m, b):
                P.act(tmpf.v(0, 128, (m % 2) * 512, [(1, TT)]), P.pb(b, 0, 128, 0, [(1, TT)]), AF.Silu)
                P.ts(g_rT.c(m, 0, TT), tmpf.v(0, 128, (m % 2) * 512, [(1, TT)]), glan.v(0, 128, j * 2 + m % 2, [(1, 1)]), None, ALU.mult)
            lin_fm("gla_wr", 2, 8, hT, ev_r)
            P.phase = "gla_rec"
            hs_ = [(not q["first"]) or kind == "s" for q in subs]
            bo_of = {}

            def g_T(i):
                sb_ = subs[i]
                cols = i * SL
                if kind == "s":
                    P.dma("sp", S.v(0, 128, 0, [(256, 4), (1, 256)]), sdram[sb_["bidx"]].rearrange("h d v -> d h v"), "misc")
                b = P.bank()
                for h in range(4):
                    P.tr(P.pb16(b, 0, SL, h * 128, [(1, 128)]), g_kd.c(h, cols, SL), full(idb, 128))
                P.act(g_kdt.v(0, SL, (i % 2) * 512, [(1, 512)]), P.pb16(b, 0, SL, 0, [(1, 512)]), AF.Copy)
                P.free(b)
                if hs_[i]:
                    P.act(full((g_Sb, g_Sb2)[i % 2], 1024), full(S, 1024), AF.Copy)

            def g_A(i):
                cols = i * SL
                b = P.bank()
                for h in range(4):
                    ex = hilo and i == 0
                    P.mm(P.pb(b, 0, SL, h * SL, [(1, SL)]), g_kd.c(h, cols, SL), g_qd.c(h, cols, SL), True, not ex)
                    if ex:
                        P.mm(P.pb(b, 0, SL, h * SL, [(1, SL)]), g_kd.c(h, 0, 128), g_qlo.c(h, 0, 128, w=128), False, False)
                        P.mm(P.pb(b, 0, SL, h * SL, [(1, SL)]), g_klo.c(h, 0, 128, w=128), g_qd.c(h, 0, 128), False, True)
                att = g_att.v(0, SL, (i % 2) * 512, [(SL, 4), (1, SL)])
                P.tt(att, P.pb(b, 0, SL, 0, [(SL, 4), (1, SL)]), cst.v(0, SL, C_TRIU, [(0, 4), (1, SL)]), ALU.mult)
                P.free(b)

            def g_U(i):
                sb_ = subs[i]
                cols = i * SL
                for hp in range(2):
                    b = P.bank()
                    for hh in range(2):
                        h = hp * 2 + hh
                        P.mm(P.pb(b, 0, 128, hh * 256, [(1, 256)]), g_kdt.v(0, SL, (i % 2) * 512 + h * 128, [(1, 128)]),
                             g_vt.v(0, SL, i * 1024 + h * 256, [(1, 256)]), True, True)
                    for hh in range(2):
                        h = hp * 2 + hh
                        ebl = g_eb.v(0, 128, h * 512 + cols + SL - 1, [(1, 1)])
                        Sh = S.v(0, 128, h * 256, [(1, 256)])
                        if hs_[i]:
                            P.ts(Sh, Sh, ebl, None, ALU.mult)
                            P.stt(Sh, P.pb(b, 0, 128, hh * 256, [(1, 256)]), ebl, Sh, ALU.mult, ALU.add)
                        else:
                            P.ts(Sh, P.pb(b, 0, 128, hh * 256, [(1, 256)]), ebl, None, ALU.mult)
                    P.free(b)
                if sb_["last"]:
                    od = odram_p if kind == "p" else odram_s
                    P.dma("sp", od[sb_["bidx"]].rearrange("h d v -> d h v"), S.v(0, 128, 0, [(256, 4), (1, 256)]), "out")

            def g_O(i):
                cols = i * SL
                Sb_ = (g_Sb, g_Sb2)[i % 2]
                bo = [P.bank(), P.bank()]
                for h in range(4):
                    for jv in range(2):
                        o_ap = P.pb(bo[h // 2], 0, 128, ((h % 2) * 2 + jv) * SL, [(1, SL)])
                        P.mm(o_ap, g_vt.v(0, SL, i * 1024 + h * 256 + jv * 128, [(1, 128)]),
                             g_att.v(0, SL, (i % 2) * 512 + h * SL, [(1, SL)]), True, not hs_[i])
                        if hs_[i]:
                            P.mm(o_ap, Sb_.v(0, 128, h * 256 + jv * 128, [(1, 128)]), g_qd.c(h, cols, SL), False, True)
                for hp in range(2):
                    P.act(sq.v(0, 128, hp * 512, [(1, 4 * SL)]), P.pb(bo[hp], 0, 128, 0, [(1, 4 * SL)]), AF.Square)
                bo_of[i] = bo

            def g_N(i):
                cols = i * SL
                bo = bo_of[i]
                bn = P.bank()
                for hp in range(2):
                    for hh in range(2):
                        for jv in range(2):
                            P.mm(P.pb(bn, 0, 128, (hp * 2 + hh) * SL, [(1, SL)]), full(onesb, 128),
                                 sq.v(0, 128, hp * 512 + (hh * 2 + jv) * SL, [(1, SL)]), jv == 0, jv == 1)
                rstd_from(bn, 4 * SL, 1.0 / 256, NORM_EPS, full(rstd, 4 * SL))
                P.free(bn)
                for hp in range(2):
                    ot = g_ot.v(0, 128, 0, [(2 * SL, 2), (SL, 2), (1, SL)])
                    P.tt(ot, P.pb(bo[hp], 0, 128, 0, [(2 * SL, 2), (SL, 2), (1, SL)]),
                         rstd.v(0, 128, hp * 2 * SL, [(SL, 2), (0, 2), (1, SL)]), ALU.mult)
                    rt = g_rT.v(0, 128, hp * 4 * W + cols, [(W, 4), (1, SL)])
                    P.tt(rt, g_ot.v(0, 128, 0, [(SL, 4), (1, SL)]), rt, ALU.mult)
                    P.free(bo[hp])

            g_T(0); g_A(0); g_U(0)
            for i in range(NSUB):
                if i + 1 < NSUB:
                    g_T(i + 1); g_A(i + 1)
                g_O(i)
                if i + 1 < NSUB:
                    g_U(i + 1)
                g_N(i)
            P.phase = "gla_wo"
            lin_fm("gla_wo", 2, 8, g_rT, lambda m, b: resid(l, 0, m, b))

        def ret(l):
            S = Sr
            sc_k = 256.0 ** -0.5

            hilo = subs[0]["first"] and kind == "p"

            def ev_q(m, b):
                h = m // 2
                P.tt(r_qd.v(0, 128, m * W, [(SL, NSUB), (1, SL)]), P.pb(b, 0, 128, 0, [(SL, NSUB), (1, SL)]),
                     cst.v(0, 128, C_REB + h * 128, [(0, NSUB), (1, SL)]), ALU.mult)
                if hilo:
                    xf = tmpf.v(0, 128, (m % 2) * 512, [(1, 128)])
                    P.tt(xf, P.pb(b, 0, 128, 0, [(1, 128)]), cst.v(0, 128, C_REB + h * 128, [(1, 128)]), ALU.mult)
                    P.tt(r_qlo.c(m, 0, 128, w=128), xf, r_qd.c(m, 0, 128), ALU.subtract)

            def ev_k(m, b):
                h = m // 2
                P.stt(r_kd.v(0, 128, m * W, [(SL, NSUB), (1, SL)]), P.pb(b, 0, 128, 0, [(SL, NSUB), (1, SL)]), sc_k,
                      cst.v(0, 128, C_RENB + h * 128, [(0, NSUB), (1, SL)]), ALU.mult, ALU.mult)
                if hilo:
                    xf = tmpf.v(0, 128, (m % 2) * 512, [(1, 128)])
                    P.stt(xf, P.pb(b, 0, 128, 0, [(1, 128)]), sc_k, cst.v(0, 128, C_RENB + h * 128, [(1, 128)]), ALU.mult, ALU.mult)
                    P.tt(r_klo.c(m, 0, 128, w=128), xf, r_kd.c(m, 0, 128), ALU.subtract)
            P.phase = "ret_proj"
            lin_fm("ret_wq", 2, 8, hT, ev_q, kouter=True)
            lin_fm("ret_wk", 2, 8, hT, ev_k)
            lin_tm("ret_wv", 4, r_vt, 2048)
            lin_fm("ret_wg", 4, 8, hT, lambda m, b: P.act(r_rT.c(m, 0, TT), P.pb(b, 0, 128, 0, [(1, TT)]), AF.Silu))
            P.phase = "ret_rec"

            def r_pre(i):
                cols = i * SL
                kdt_ = (r_kdt, r_kdt2)[i % 2]
                att_ = (r_att, r_att2)[i % 2]
                b = P.bank()
                for c in range(8):
                    P.tr(P.pb16(b, 0, SL, c * 128, [(1, 128)]), r_kd.c(c, cols, SL), full(idb, 128))
                for h in range(4):
                    P.act(kdt_.v(0, SL, h * 256, [(1, 256)]), P.pb16(b, 0, SL, h * 256, [(1, 256)]), AF.Copy,
                          scale=math.exp(RET_LG[h] * SL))
                P.free(b)
                b = P.bank()
                for h in range(4):
                    ex = hilo and i == 0
                    for c in range(2):
                        m = h * 2 + c
                        P.mm(P.pb(b, 0, SL, h * SL, [(1, SL)]), r_kd.c(m, cols, SL), r_qd.c(m, cols, SL), c == 0, c == 1 and not ex)
                        if ex:
                            P.mm(P.pb(b, 0, SL, h * SL, [(1, SL)]), r_kd.c(m, 0, 128), r_qlo.c(m, 0, 128, w=128), False, False)
                            P.mm(P.pb(b, 0, SL, h * SL, [(1, SL)]), r_klo.c(m, 0, 128, w=128), r_qd.c(m, 0, 128), False, c == 1)
                P.tt(att_.v(0, SL, 0, [(SL, 4), (1, SL)]), P.pb(b, 0, SL, 0, [(SL, 4), (1, SL)]),
                     cst.v(0, SL, C_TRIU, [(0, 4), (1, SL)]), ALU.mult)
                P.free(b)

            r_pre(0)
            for i, sb_ in enumerate(subs):
                cols = i * SL
                has_state = not sb_["first"]
                if kind == "s":
                    P.dma("sp", S.v(0, 128, 0, [(1024, 4), (512, 2), (1, 512)]),
                          sret_d[sb_["bidx"]].rearrange("h (c d) v -> d h c v", d=128), "misc")
                    has_state = True
                if i + 1 < NSUB:
                    r_pre(i + 1)
                kdt_ = (r_kdt, r_kdt2)[i % 2]
                att_ = (r_att, r_att2)[i % 2]
                bo_of = {}

                def ret_o(h):
                    rSb = (r_Sb, r_Sb2)[h % 2]
                    if has_state:
                        P.cp(full(rSb, 1024), S.v(0, 128, h * 1024, [(1, 1024)]), eng="act")
                    bo = P.bank()
                    o_ap = P.pb(bo, 0, SL, 0, [(1, 512)])
                    P.mm(o_ap, att_.v(0, SL, h * SL, [(1, SL)]), r_vt.v(0, SL, i * 2048 + h * 512, [(1, 512)]), True, not has_state)
                    if has_state:
                        for c in range(2):
                            P.mm(o_ap, r_qd.c(h * 2 + c, cols, SL), rSb.v(0, 128, c * 512, [(1, 512)]), False, c == 1)
                    bo_of[h] = bo

                def ret_norm(h):
                    bo = bo_of[h]
                    o_ap = P.pb(bo, 0, SL, 0, [(1, 512)])

                    def sc(idx):
                        return tmpf.v(0, SL, 512 + 8 * (h % 2) + idx, [(1, 1)])
                    s1, s2, mean, msq, var, rs = [sc(q) for q in range(6)]
                    ob_ = r_ob.v(0, SL, 0, [(1, 512)])
                    sq_ = r_sq.v(0, SL, 0, [(1, 512)])
                    P.add("act", lambda e: e.activation(ob_, o_ap, AF.Copy, accum_out=s1), [o_ap], [ob_, s1])
                    P.add("act", lambda e: e.activation(sq_, o_ap, AF.Square, accum_out=s2), [o_ap], [sq_, s2])
                    P.ts(mean, s1, 1.0 / 512, None, ALU.mult)
                    P.tt(msq, mean, mean, ALU.mult)
                    P.stt(var, s2, 1.0 / 512, msq, ALU.mult, ALU.subtract)
                    P.act(var, var, AF.Ln, bias=GN_EPS)
                    P.act(rs, var, AF.Exp, scale=-0.5)
                    on = (r_on, r_on2)[h % 2].v(0, SL, 0, [(1, 512)])
                    P.ts(on, o_ap, mean, rs, ALU.subtract, ALU.mult)
                    P.free(bo)

                def ret_rest(h):
                    ron = (r_on, r_on2)[h % 2]
                    bt = P.bank()
                    for jv in range(4):
                        P.tr(P.pb16(bt, 0, 128, jv * SL, [(1, SL)]), ron.v(0, SL, jv * 128, [(1, 128)]), idb.v(0, SL, 0, [(1, SL)]))
                    rt = r_rT.v(0, 128, h * 4 * W + cols, [(W, 4), (1, SL)])
                    P.tt(rt, P.pb16(bt, 0, 128, 0, [(SL, 4), (1, SL)]), rt, ALU.mult)
                    P.free(bt)
                    gL = math.exp(RET_LG[h] * SL)
                    for c in range(2):
                        b = P.bank()
                        P.mm(P.pb(b, 0, 128, 0, [(1, 512)]), kdt_.v(0, SL, (h * 2 + c) * 128, [(1, 128)]),
                             r_vt.v(0, SL, i * 2048 + h * 512, [(1, 512)]), True, True)
                        Sh = S.v(0, 128, h * 1024 + c * 512, [(1, 512)])
                        if has_state:
                            P.stt(Sh, Sh, gL, P.pb(b, 0, 128, 0, [(1, 512)]), ALU.mult, ALU.add)
                        else:
                            P.cp(Sh, P.pb(b, 0, 128, 0, [(1, 512)]))
                        P.free(b)


                ret_o(0)
                ret_o(1)
                ret_norm(0)
                for h in range(4):
                    if h + 2 < 4:
                        ret_o(h + 2)
                    if h + 1 < 4:
                        ret_norm(h + 1)
                    ret_rest(h)
                if sb_["last"]:
                    od = o_rp if kind == "p" else o_rs
                    P.dma("sp", od[sb_["bidx"]].rearrange("h (c d) v -> d h c v", d=128),
                          S.v(0, 128, 0, [(1024, 4), (512, 2), (1, 512)]), "out")
            P.phase = "ret_wo"
            lin_fm_bigk("ret_wo", 2, r_rT, lambda m, b: resid(l, 0, m, b))

        def swa(l):
            KOFF = 128 if kind == "p" else 0
            last_subs = [i for i, q in enumerate(subs) if q["last"]]
            P.phase = "swa_proj"
            for n in range(2):
                base = next_slab("swa_wqkv")
                for m4 in range(4):
                    m = n * 4 + m4
                    b = P.bank()
                    for k in range(KC):
                        P.mm(P.pb(b, 0, 128, 0, [(1, TT)]), wl(base, k, m4 * 128, 128), hT.c(k, 0, TT), k == 0, k == KC - 1)
                    sqv = sq.v(0, 128, (m % 2) * 512, [(1, TT)])
                    P.act(sqv, P.pb(b, 0, 128, 0, [(1, TT)]), AF.Square)
                    bn = P.bank()
                    P.mm(P.pb(bn, 0, 128, 0, [(1, TT)]), full(blkb, 128), sqv, True, True)
                    rstd_from(bn, TT, 1.0 / 64, NORM_EPS, full(rstd, TT))
                    P.free(bn)
                    P.stt(s_qn.c(m, 0, TT), P.pb(b, 0, 128, 0, [(1, TT)]), qkn.v(0, 128, 0, [(1, 1)]), full(rstd, TT), ALU.mult, ALU.mult)
                    P.free(b)
            mark("swa_q")
            base = next_slab("swa_wqkv")
            for jh in range(4):
                b = P.bank()
                for half in range(2):
                    for k in range(KC):
                        P.mm(P.pb(b, half * 64, 64, 0, [(1, TT)]), wl(base, k, jh * 64, 64), hT.c(k, 0, TT), k == 0, k == KC - 1)
                sqv = sq.v(0, 128, (jh % 2) * 512, [(1, TT)])
                P.act(sqv, P.pb(b, 0, 128, 0, [(1, TT)]), AF.Square)
                bn = P.bank()
                P.mm(P.pb(bn, 0, 128, 0, [(1, TT)]), full(blkb, 128), sqv, True, True)
                rstd_from(bn, TT, 1.0 / 64, NORM_EPS, full(rstd, TT))
                P.free(bn)
                kf = tmpf.v(0, 128, 512, [(1, TT)])
                P.stt(kf, P.pb(b, 0, 128, 0, [(1, TT)]), qkn.v(0, 128, 1, [(1, 1)]), full(rstd, TT), ALU.mult, ALU.mult)
                P.free(b)
                P.act(kTd.v(0, 128, jh * 640 + KOFF, [(1, TT)]), kf, AF.Copy)
                for li, i in enumerate(last_subs):
                    P.cp(s_kfin.v(0, 64, (li * 4 + jh) * 128, [(1, SL)]), tmpf.v(0, 64, 512 + i * SL, [(1, SL)]))
            mark("swa_k")
            for li, i in enumerate(last_subs):
                bt = P.bank()
                for j2 in range(4):
                    P.tr(P.pb(bt, 0, SL, j2 * 64, [(1, 64)]), s_kfin.v(0, 64, (li * 4 + j2) * 128, [(1, SL)]), cst.v(0, 64, C_ID, [(1, 64)]))
                P.act(s_kout.v(0, SL, 0, [(1, 256)]), P.pb(bt, 0, SL, 0, [(1, 256)]), AF.Copy)
                P.free(bt)
                od = o_kp if kind == "p" else o_ks
                P.dma("sp", od[subs[i]["bidx"]], s_kout.v(0, SL, 0, [(1, 256)]), "out")
            mark("swa_kcache")
            for i, sb_ in enumerate(subs):
                b = P.bank()
                for k in range(KC):
                    P.mm(P.pb(b, 0, SL, 0, [(1, 256)]), hT.c(k, i * SL, SL), wl(base, k, 256, 256), k == 0, k == KC - 1)
                slot = (i + 1) if kind == "p" else i
                P.act(vdup.v(0, SL, slot * 512, [(128, 4), (1, 64)]), P.pb(b, 0, SL, 0, [(64, 4), (1, 64)]), AF.Copy)
                P.cp(vdup.v(0, SL, slot * 512 + 64, [(128, 4), (1, 64)]), P.pb(b, 0, SL, 0, [(64, 4), (1, 64)]))
                if sb_["last"]:
                    P.cp(s_vfin.v(0, SL, 0, [(1, 256)]), P.pb(b, 0, SL, 0, [(1, 256)]))
                    od = o_vp if kind == "p" else o_vs
                    P.dma("sp", od[sb_["bidx"]], s_vfin.v(0, SL, 0, [(1, 256)]), "out")
                P.free(b)
            mark("swa_v")
            P.phase = "swa_att"
            def tiles_for(i):
                if kind == "p":
                    tl = []
                    if not subs[i]["first"]:
                        tl.append(("p", i, C_NDPP, 128))
                    tl.append(("o", i, C_NDOP, 128))
                    return tl
                return [("c", i, C_NDPS, 128), ("o", i, C_NDOS, SL)]

            its = [(i, j) for i in range(NSUB) for j in range(4)]

            def bufs(n):
                par = n % 2
                return (s_lg, s_lg2)[par], (s_pT, s_pT2)[par], (s_den, s_den2)[par], (s_rden, s_rden2)[par]

            def stage_a(n):
                i, j = its[n]
                cols = i * SL
                lg_, pT_, den_, rden_ = bufs(n)
                for ti, (tk, _, ndc, SK) in enumerate(tiles_for(i)):
                    bh = [P.bank(), P.bank()]
                    for half in range(2):
                        if tk == "c":
                            kap = ckT.v(half * 64, 64, (i * 4 + j) * 128, [(1, 128)])
                        elif tk == "p":
                            kap = kTd.v(half * 64, 64, j * 640 + i * SL, [(1, SK)])
                        else:
                            kap = kTd.v(half * 64, 64, j * 640 + KOFF + i * SL, [(1, SK)])
                        P.mm(P.pb(bh[half], 0, SK, 0, [(1, 2 * SL)]), kap,
                             s_qn.v(half * 64, 64, (2 * j) * W + cols, [(W, 2), (1, SL)]), True, True)
                    for g in range(4):
                        cb = (g % 2) * 2 + g // 2
                        P.stt(lg_.v(0, SK, ti * 512 + cb * SL, [(1, SL)]), cst.v(0, SK, ndc, [(1, SL)]), 8.0 * SLOPES[4 * j + g],
                              P.pb(bh[g % 2], 0, SK, (g // 2) * SL, [(1, SL)]), ALU.mult, ALU.add)
                    P.free(bh[0])
                    P.free(bh[1])
                    P.act(pT_.v(0, SK, ti * 512, [(1, 4 * SL)]), lg_.v(0, SK, ti * 512, [(1, 4 * SL)]), AF.Exp, scale=0.125)

            def stage_b(n):
                i, j = its[n]
                cols = i * SL
                lg_, pT_, den_, rden_ = bufs(n)
                tiles = tiles_for(i)
                bo = P.bank()
                bd = P.bank()
                for ti, (tk, _, ndc, SK) in enumerate(tiles):
                    if tk == "c":
                        vap = cvd.v(0, 128, i * 512 + j * 128, [(1, 128)])
                    elif tk == "p":
                        vap = vdup.v(0, SK, i * 512 + j * 128, [(1, 128)])
                    else:
                        vap = vdup.v(0, SK, ((i + 1) if kind == "p" else i) * 512 + j * 128, [(1, 128)])
                    P.mm(P.pb(bo, 0, 128, 0, [(1, 4 * SL)]), vap, pT_.v(0, SK, ti * 512, [(1, 4 * SL)]), ti == 0, ti == len(tiles) - 1)
                for ti, (tk, _, ndc, SK) in enumerate(tiles):
                    P.mm(P.pb(bd, 0, 128, 0, [(1, 4 * SL)]), onesb.v(0, SK, 0, [(1, 128)]), pT_.v(0, SK, ti * 512, [(1, 4 * SL)]),
                         ti == 0, ti == len(tiles) - 1)
                for g in range(4):
                    cb = (g % 2) * 2 + g // 2
                    P.act(den_.v(0, 128, cb * SL, [(1, SL)]), P.pb(bd, 0, 128, cb * SL, [(1, SL)]), AF.Ln,
                          bias=esink.v(0, 128, 4 * j + g, [(1, 1)]))
                P.free(bd)
                P.act(full(rden_, 4 * SL), full(den_, 4 * SL), AF.Exp, scale=-1.0)
                for g in range(4):
                    cb = (g % 2) * 2 + g // 2
                    hp = (g % 2) * 64
                    P.tt(s_og.v(hp, 64, (2 * j + g // 2) * W + cols, [(1, SL)]), P.pb(bo, hp, 64, cb * SL, [(1, SL)]),
                         rden_.v(hp, 64, cb * SL, [(1, SL)]), ALU.mult)
                P.free(bo)

            stage_a(0)
            for n in range(len(its)):
                if n + 1 < len(its):
                    stage_a(n + 1)
                stage_b(n)
            mark("swa_att")
            P.phase = "swa_wo"
            lin_fm("swa_wo", 2, 8, s_og, lambda m, b: resid(l, 0, m, b))
            if kind == "p":
                for j in range(4):
                    P.cp(kTd.v(0, 128, j * 640, [(1, 128)]), kTd.v(0, 128, j * 640 + 512, [(1, 128)]), eng="act" if j % 2 else "dve")
                P.cp(vdup.v(0, 128, 0, [(1, 512)]), vdup.v(0, 128, 4 * 512, [(1, 512)]))

        def mlp(l):
            P.phase = "mlp"

            def ev_h(m, b):
                r = m_rl.v(0, 128, (m % 2) * 512, [(1, TT)])
                P.act(r, P.pb(b, 0, 128, 0, [(1, TT)]), AF.Relu)
                P.tt(m_hid.c(m, 0, TT), r, r, ALU.mult)
            lin_fm("mlp_w1", 8, 8, hT, ev_h, kouter=True)
            lin_fm_bigk("mlp_w2", 4, m_hid, lambda m, b: resid(l, 1, m, b))

        for l in range(L_LAYERS):
            norm_mod(l, 0)
            mark(f"norm{l}")
            (gla, ret, swa)[l % 3](l)
            mark(f"mixer{l}")
            norm_mod(l, 1)
            if l == L_LAYERS - 1 and nxt is not None:
                emit_xload(*nxt)
            mlp(l)
            mark(f"mlp{l}")

        P.phase = "ystore"
        for i, sb_ in enumerate(subs):
            for half in range(2):
                b = P.bank()
                for kk in range(4):
                    k = half * 4 + kk
                    P.tr(P.pb(b, 0, SL, kk * 128, [(1, 128)]), xT.c(k, i * SL, SL), cst.v(0, 128, C_ID, [(1, 128)]))
                o = stage.v(0, SL, (i % 2) * 1024 + half * 512, [(1, 512)])
                if half:
                    P.act(o, P.pb(b, 0, SL, 0, [(1, 512)]), AF.Copy)
                else:
                    P.cp(o, P.pb(b, 0, SL, 0, [(1, 512)]))
                P.free(b)
            P.dma("sp", yout[sb_["xrow"]:sb_["xrow"] + SL, :], stage.v(0, SL, (i % 2) * 1024, [(1, 1024)]), ("st", i % 2))

    NT = SEQ // 512
    tiles_ = []
    for p in range(NP):
        for t in range(NT):
            tiles_.append(("p", [dict(slot=p, xrow=p * SEQ + t * 512 + i * 128, first=(t == 0 and i == 0),
                                      last=(t == NT - 1 and i == 3), bidx=p) for i in range(4)]))
    sample_tile = ("s", [dict(slot=NP + i, xrow=i * TS, first=True, last=True, bidx=i) for i in range(NS)])
    for n_, (kd_, sb__) in enumerate(tiles_):
        nx = tiles_[n_ + 1] if n_ + 1 < len(tiles_) else sample_tile
        process_tile(kd_, sb__, prefetched=(n_ > 0), nxt=nx)
    for i in range(NS):
        P.dma("sp", s_kc.v(0, 128, 0, [(1, 256)]), ck_d[i], "misc")
        P.dma("sp", s_vc.v(0, 128, 0, [(1, 256)]), cv_d[i], "misc")
        P.cp(cvd.v(0, 128, i * 512, [(128, 4), (1, 64)]), s_vc.v(0, 128, 0, [(64, 4), (1, 64)]))
        P.cp(cvd.v(0, 128, i * 512 + 64, [(128, 4), (1, 64)]), s_vc.v(0, 128, 0, [(64, 4), (1, 64)]), eng="act")
        for j in range(4):
            P.cp(s_kdp.v(0, 128, 0, [(1, 64)]), s_kc.v(0, 128, j * 64, [(1, 64)]))
            P.cp(s_kdp.v(0, 128, 64, [(1, 64)]), s_kc.v(0, 128, j * 64, [(1, 64)]), eng="act")
            b = P.bank()
            P.tr(P.pb(b, 0, 128, 0, [(1, 128)]), s_kdp.v(0, 128, 0, [(1, 128)]), cst.v(0, 128, C_ID, [(1, 128)]))
            P.cp(ckT.v(0, 128, (i * 4 + j) * 128, [(1, 128)]), P.pb(b, 0, 128, 0, [(1, 128)]))
            P.free(b)
    process_tile("s", sample_tile[1], prefetched=True, nxt=None)
    assert wstate["next"] == TOTAL_SLABS, (wstate, TOTAL_SLABS)


def host_pack(inp, NP, NS, core, n_cores_total=None):
    f = lambda a: np.ascontiguousarray(np.asarray(a, dtype=np.float32))
    d = {}
    SEQ = inp["x_prompt"].shape[1]
    TS = inp["x_sample"].shape[1]
    ps = slice(core * NP, (core + 1) * NP)
    ss = slice(core * NS, (core + 1) * NS)
    d["xp"] = f(inp["x_prompt"][ps]).reshape(NP * SEQ, D)
    d["xs"] = f(inp["x_sample"][ss]).reshape(NS * TS, D)
    c = np.concatenate([f(inp["c_prompt"][ps]), f(inp["c_sample"][ss])], 0)
    if c.shape[0] < 4:
        c = np.concatenate([c, np.zeros((4 - c.shape[0], D), np.float32)], 0)
    d["cT"] = f(c.reshape(4, 8, 128).transpose(2, 1, 0).reshape(128, 32))
    d["sgla0"] = f(inp["state_gla_l0"][ss]); d["sgla3"] = f(inp["state_gla_l3"][ss])
    d["sret"] = f(inp["state_ret_l1"][ss])
    d["ck"] = f(inp["cache_swa_k_l2"][ss]).reshape(NS, 128, 256)
    d["cv"] = f(inp["cache_swa_v_l2"][ss]).reshape(NS, 128, 256)
    return d


def host_shared(inp):
    f = lambda a: np.ascontiguousarray(np.asarray(a, dtype=np.float32))
    d = {}
    blocks = stream_blocks()
    ws = np.empty((len(blocks), 128, SLAB), np.float32)
    wn = {k: np.asarray(inp[k], dtype=np.float32) for k in set(b[0] for b in blocks)}
    for i, (name, j, r0, c0) in enumerate(blocks):
        ws[i] = slab_np(wn[name][j], r0, c0)
    d["wstream"] = ws
    aw = np.asarray(inp["ada_w"], dtype=np.float32)
    d["adaw"] = np.stack([slab_np(aw[l], 0, n * 512) for l in range(L_LAYERS) for n in range(12)], 0)
    ab = np.asarray(inp["ada_b"], dtype=np.float32)
    d["adab"] = f(ab.reshape(L_LAYERS, 48, 128).transpose(2, 0, 1).reshape(128, L_LAYERS * 48))
    nm = np.stack([np.asarray(inp["norm_mix"], np.float32), np.asarray(inp["norm_mlp"], np.float32)], 1)
    d["nrm"] = f(nm.reshape(L_LAYERS, 2, 8, 128).transpose(3, 0, 1, 2).reshape(128, 64))
    gn = np.asarray(inp["gla_norm"], np.float32)
    d["glan"] = f(gn.reshape(2, 2, 128).transpose(2, 0, 1).reshape(128, 4))
    qn = np.asarray(inp["swa_qnorm"], np.float32)[0]; kn = np.asarray(inp["swa_knorm"], np.float32)[0]
    d["qkn"] = f(np.stack([np.concatenate([qn, qn]), np.concatenate([kn, kn])], 1))
    d["sinks"] = f(np.asarray(inp["swa_sinks"], np.float32).reshape(1, 16))
    w1 = np.asarray(inp["gla_wg1"], np.float32)
    d["wg1"] = f(w1.reshape(2, 8, 128, 16).transpose(2, 0, 1, 3).reshape(128, 256))
    w2 = np.asarray(inp["gla_wg2"], np.float32); bg = np.asarray(inp["gla_bg"], np.float32)
    wa = np.zeros((32, 2, 512), np.float32)
    wa[:16] = w2.transpose(1, 0, 2)
    wa[16] = bg
    d["wg2a"] = f(wa.reshape(32, 1024))
    d["cst"] = const_table()
    return d


def run(inp, n_cores, NP, NS):
    SEQ = inp["x_prompt"].shape[1]
    TS = inp["x_sample"].shape[1]
    nc = build(NP, SEQ, NS, TS)
    shared = host_shared(inp)
    in_maps = []
    for c in range(n_cores):
        m = dict(shared)
        m.update(host_pack(inp, NP, NS, c))
        in_maps.append(m)
    res = run_bass_kernel_spmd(nc, in_maps, core_ids=list(range(n_cores)))
    R = res.results
    cat = lambda k: np.concatenate([np.asarray(r[k]) for r in R], 0)
    B = n_cores * NP
    Bs = n_cores * NS
    outs = (
        cat("yp").reshape(B, SEQ, D), cat("ys").reshape(Bs, TS, D),
        cat("o_g0p"), cat("o_g0s"), cat("o_rp"), cat("o_rs"),
        cat("o_kp").reshape(B, 128, 4, 64), cat("o_vp").reshape(B, 128, 4, 64),
        cat("o_ks").reshape(Bs, TS, 4, 64), cat("o_vs").reshape(Bs, TS, 4, 64),
        cat("o_g3p"), cat("o_g3s"),
    )
    return tuple(np.ascontiguousarray(o.astype(np.float32)) for o in outs)


def kernel(**inputs):
    return run(inputs, 8, 2, 2)
```
